# Optimizing a Trainium2 kernel written in Bass

```python
import jax
import jax.numpy as jnp
from jax import lax
import numpy as np

D_MODEL = 1024
BATCH = 16
SEQ = 256
DEPTH = 1
DEC_BATCH = 2
DEC_SEQ = 2048
PAST_LEN = 256

GRID_W = 64
RWKV_WIDTH = 512
HEAD_DIM = 64
N_HEADS = RWKV_WIDTH // HEAD_DIM
CONV_WIDTH = 512
CONV_K = 31
D_FF = 4 * D_MODEL
DECAY_LORA = 64
ICLR_LORA = 64
GATE_LORA = 128
N_DIR = 2
IN_COLS = 3 * RWKV_WIDTH + 2 * CONV_WIDTH + 2 * D_MODEL
RMS_EPS = 1e-6
LN_EPS = 1e-5
GN_EPS = 64e-5

kernel_name = 'hybrid_rwkv7_conformer_dit_step'


def _rmsnorm(x, g):
    xf = x.astype(jnp.float32)
    y = xf * lax.rsqrt(jnp.mean(xf * xf, axis=-1, keepdims=True) + RMS_EPS) * g.astype(jnp.float32)
    return y.astype(x.dtype)


def _heads(t):
    return t.reshape(t.shape[0], t.shape[1], N_HEADS, HEAD_DIM)


def _token_shift(u, mu_prev, mu_next):
    prev = jnp.pad(u, ((0, 0), (1, 0), (0, 0)))[:, :-1]
    nxt = jnp.pad(u, ((0, 0), (0, 1), (0, 0)))[:, 1:]
    return u + (prev - u) * mu_prev + (nxt - u) * mu_next


def _wkv(s0, r, w, k, v, a, b, reverse):
    xs = tuple(jnp.moveaxis(t.astype(jnp.float32), 1, 0) for t in (r, w, k, v, a, b))

    def step(S, inp):
        rt, wt, kt, vt, at, bt = inp
        sa = jnp.einsum('bhvk,bhk->bhv', S, at)
        S = S * wt[:, :, None, :] + sa[..., None] * bt[:, :, None, :] + vt[..., None] * kt[:, :, None, :]
        return S, jnp.einsum('bhvk,bhk->bhv', S, rt)

    s_fin, ys = lax.scan(step, s0.astype(jnp.float32), xs, reverse=reverse)
    return jnp.moveaxis(ys, 0, 1), s_fin


def _rwkv_branch(xn, rkv, s0_f, s0_b, lp):
    B, T, _ = xn.shape
    rkv = _token_shift(rkv, lp['mu_prev'], lp['mu_next'])
    r, k, v = jnp.split(rkv, 3, axis=-1)
    kk = _heads(k * lp['k_k']).astype(jnp.float32)
    kk = kk * lax.rsqrt(jnp.maximum(jnp.sum(kk * kk, axis=-1, keepdims=True), 1e-24))
    g = jax.nn.sigmoid(xn @ lp['gate_g1']) @ lp['gate_g2']
    rh, vh = _heads(r), _heads(v)

    def direction(d, s0, reverse):
        w_log = -jax.nn.softplus(-(lp['decay_w0'][d] + jnp.tanh(xn @ lp['decay_w1'][d]) @ lp['decay_w2'][d])) - 0.5
        decay = jnp.exp(-jnp.exp(w_log.astype(jnp.float32)))
        a = jax.nn.sigmoid(lp['iclr_a0'][d] + (xn @ lp['iclr_a1'][d]) @ lp['iclr_a2'][d])
        kd = _heads(k * (1 + (a - 1) * lp['k_a']))
        ah = _heads(a).astype(jnp.float32)
        y, s_fin = _wkv(s0, rh, _heads(decay), kd, vh, -kk, kk * ah, reverse)
        bonus = jnp.sum((rh * kd).astype(jnp.float32) * lp['r_k'], axis=-1, keepdims=True) * vh.astype(jnp.float32)
        return y, bonus, s_fin

    y_f, bonus_f, s_f = direction(0, s0_f, False)
    y_b, bonus_b, s_b = direction(1, s0_b, True)
    y = y_f + y_b
    mu = jnp.mean(y, axis=-1, keepdims=True)
    var = jnp.mean(jnp.square(y - mu), axis=-1, keepdims=True)
    yn = ((y - mu) * lax.rsqrt(var + GN_EPS)).reshape(B, T, RWKV_WIDTH) * lp['lnx_g'] + lp['lnx_b']
    out = (yn + (bonus_f + bonus_b).reshape(B, T, RWKV_WIDTH)) * g
    return out.astype(xn.dtype) @ lp['w_out_rwkv'], s_f, s_b


def _conv_branch(glu_in, rows, lp):
    a, gt = jnp.split(glu_in, 2, axis=-1)
    u = a * jax.nn.sigmoid(gt)
    B, T, C = u.shape
    if rows is not None:
        u = u.reshape(B * rows, T // rows, C)
    u = lax.conv_general_dilated(u, lp['conv_w'][:, None, :], (1,), [(CONV_K // 2, CONV_K // 2)],
                                 dimension_numbers=('NWC', 'WIO', 'NWC'), feature_group_count=C) + lp['conv_b']
    u = u.reshape(B, T, C)
    uf = u.astype(jnp.float32)
    mu = jnp.mean(uf, axis=-1, keepdims=True)
    var = jnp.mean(jnp.square(uf - mu), axis=-1, keepdims=True)
    un = ((uf - mu) * lax.rsqrt(var + LN_EPS) * lp['conv_ln_g'] + lp['conv_ln_b']).astype(u.dtype)
    return jax.nn.silu(un) @ lp['w_out_conv']


def _layer(x, cond, s0_f, s0_b, rows, lp):
    mod = jax.nn.silu(cond) @ lp['ada_w'] + lp['ada_b']
    sh1, sc1, g1, sh2, sc2, g2 = jnp.split(mod[:, None, :], 6, axis=-1)
    xn = _rmsnorm(x, lp['norm1_g']) * (1 + sc1) + sh1
    proj = xn @ lp['w_in']
    rkv, glu_in, gates = jnp.split(proj, [3 * RWKV_WIDTH, 3 * RWKV_WIDTH + 2 * CONV_WIDTH], axis=-1)
    gate_r, gate_c = jnp.split(gates, 2, axis=-1)
    y_r, s_f, s_b = _rwkv_branch(xn, rkv, s0_f, s0_b, lp)
    y_c = _conv_branch(glu_in, rows, lp)
    merged = jax.nn.sigmoid(gate_r) * y_r + jax.nn.sigmoid(gate_c) * y_c
    x = x + g1 * (merged @ lp['w_o'])
    xn2 = _rmsnorm(x, lp['norm2_g']) * (1 + sc2) + sh2
    h = jnp.square(jax.nn.relu(xn2 @ lp['mlp_w1']))
    x = x + g2 * (h @ lp['mlp_w2'])
    return x, s_f, s_b


def setup_inputs(seed: int = 0) -> dict:
    key = jax.random.key(seed)
    ks = jax.random.split(key, 40)
    f32 = jnp.float32

    def nrm(i, shape, scale):
        return jax.random.normal(ks[i], shape, f32) * scale

    def uni(i, shape, lo, hi):
        return jax.random.uniform(ks[i], shape, f32, lo, hi)

    st_shape = (DEC_BATCH, DEPTH, N_HEADS, HEAD_DIM, HEAD_DIM)
    return {
        'x_prompt': nrm(0, (BATCH, SEQ, D_MODEL), 1.0),
        'x_sample': nrm(1, (DEC_BATCH, DEC_SEQ, D_MODEL), 1.0),
        'state_fwd': nrm(2, st_shape, 0.1),
        'state_bwd': nrm(3, st_shape, 0.1),
        'c': nrm(4, (DEC_BATCH, D_MODEL), 1.0),
        'c_ctx': nrm(5, (D_MODEL,), 1.0),
        'ada_w': nrm(6, (DEPTH, D_MODEL, 6 * D_MODEL), D_MODEL ** -0.5),
        'ada_b': nrm(7, (DEPTH, 6 * D_MODEL), 0.02),
        'norm1_g': 1.0 + nrm(8, (DEPTH, D_MODEL), 0.02),
        'norm2_g': 1.0 + nrm(9, (DEPTH, D_MODEL), 0.02),
        'w_in': nrm(10, (DEPTH, D_MODEL, IN_COLS), D_MODEL ** -0.5),
        'mu_prev': uni(11, (DEPTH, 3 * RWKV_WIDTH), 0.0, 0.5),
        'mu_next': uni(12, (DEPTH, 3 * RWKV_WIDTH), 0.0, 0.5),
        'decay_w0': uni(13, (DEPTH, N_DIR, RWKV_WIDTH), -6.0, -1.0),
        'decay_w1': nrm(14, (DEPTH, N_DIR, D_MODEL, DECAY_LORA), D_MODEL ** -0.5),
        'decay_w2': nrm(15, (DEPTH, N_DIR, DECAY_LORA, RWKV_WIDTH), 0.5 * DECAY_LORA ** -0.5),
        'iclr_a0': nrm(16, (DEPTH, N_DIR, RWKV_WIDTH), 0.1),
        'iclr_a1': nrm(17, (DEPTH, N_DIR, D_MODEL, ICLR_LORA), D_MODEL ** -0.5),
        'iclr_a2': nrm(18, (DEPTH, N_DIR, ICLR_LORA, RWKV_WIDTH), ICLR_LORA ** -0.5),
        'gate_g1': nrm(19, (DEPTH, D_MODEL, GATE_LORA), D_MODEL ** -0.5),
        'gate_g2': nrm(20, (DEPTH, GATE_LORA, RWKV_WIDTH), GATE_LORA ** -0.5),
        'k_k': 0.85 + nrm(21, (DEPTH, RWKV_WIDTH), 0.05),
        'k_a': 1.0 + nrm(22, (DEPTH, RWKV_WIDTH), 0.05),
        'r_k': nrm(23, (DEPTH, N_HEADS, HEAD_DIM), 0.1),
        'lnx_g': 1.0 + nrm(24, (DEPTH, RWKV_WIDTH), 0.02),
        'lnx_b': nrm(25, (DEPTH, RWKV_WIDTH), 0.02),
        'w_out_rwkv': nrm(26, (DEPTH, RWKV_WIDTH, D_MODEL), RWKV_WIDTH ** -0.5),
        'conv_w': nrm(27, (DEPTH, CONV_K, CONV_WIDTH), CONV_K ** -0.5),
        'conv_b': nrm(28, (DEPTH, CONV_WIDTH), 0.02),
        'conv_ln_g': 1.0 + nrm(29, (DEPTH, CONV_WIDTH), 0.02),
        'conv_ln_b': nrm(30, (DEPTH, CONV_WIDTH), 0.02),
        'w_out_conv': nrm(31, (DEPTH, CONV_WIDTH, D_MODEL), CONV_WIDTH ** -0.5),
        'w_o': nrm(32, (DEPTH, D_MODEL, D_MODEL), D_MODEL ** -0.5),
        'mlp_w1': nrm(33, (DEPTH, D_MODEL, D_FF), D_MODEL ** -0.5),
        'mlp_w2': nrm(34, (DEPTH, D_FF, D_MODEL), D_FF ** -0.5),
        'final_g': 1.0 + nrm(35, (D_MODEL,), 0.02),
    }


def reference(x_prompt, x_sample, state_fwd, state_bwd, c, c_ctx, ada_w, ada_b, norm1_g, norm2_g, w_in,
              mu_prev, mu_next, decay_w0, decay_w1, decay_w2, iclr_a0, iclr_a1, iclr_a2, gate_g1, gate_g2,
              k_k, k_a, r_k, lnx_g, lnx_b, w_out_rwkv, conv_w, conv_b, conv_ln_g, conv_ln_b, w_out_conv,
              w_o, mlp_w1, mlp_w2, final_g):
    rows = x_sample.shape[1] // GRID_W
    n_ctx_batch = x_prompt.shape[0]
    xp = x_prompt
    xs = x_sample
    new_f = []
    new_b = []
    for l in range(DEPTH):
        lp = dict(ada_w=ada_w[l], ada_b=ada_b[l], norm1_g=norm1_g[l], norm2_g=norm2_g[l], w_in=w_in[l],
                  mu_prev=mu_prev[l], mu_next=mu_next[l], decay_w0=decay_w0[l], decay_w1=decay_w1[l],
                  decay_w2=decay_w2[l], iclr_a0=iclr_a0[l], iclr_a1=iclr_a1[l], iclr_a2=iclr_a2[l],
                  gate_g1=gate_g1[l], gate_g2=gate_g2[l], k_k=k_k[l], k_a=k_a[l], r_k=r_k[l],
                  lnx_g=lnx_g[l], lnx_b=lnx_b[l], w_out_rwkv=w_out_rwkv[l], conv_w=conv_w[l], conv_b=conv_b[l],
                  conv_ln_g=conv_ln_g[l], conv_ln_b=conv_ln_b[l], w_out_conv=w_out_conv[l], w_o=w_o[l],
                  mlp_w1=mlp_w1[l], mlp_w2=mlp_w2[l])
        zeros = jnp.zeros((n_ctx_batch, N_HEADS, HEAD_DIM, HEAD_DIM), jnp.float32)
        xp, s_f, s_b = _layer(xp, c_ctx[None, :], zeros, zeros, None, lp)
        new_f.append(s_f.astype(x_prompt.dtype))
        new_b.append(s_b.astype(x_prompt.dtype))
        xs, _, _ = _layer(xs, c, state_fwd[:, l], state_bwd[:, l], rows, lp)
    y_prompt = _rmsnorm(xp, final_g)
    y_sample = _rmsnorm(xs, final_g)
    new_state_fwd = jnp.stack(new_f, axis=1)
    new_state_bwd = jnp.stack(new_b, axis=1)
    return (y_prompt, y_sample, new_state_fwd, new_state_bwd)
```

```python
import contextlib
import os
import numpy as np
import concourse.bass as bass
import concourse.mybir as mybir
from concourse.bass_utils import run_bass_kernel_spmd

F32 = mybir.dt.float32
BF16 = mybir.dt.bfloat16
AF = mybir.ActivationFunctionType
ALU = mybir.AluOpType
AX = mybir.AxisListType

SAME_ENGINE_SYNC = True
N_DMA_SEMS = 6
NCORES = 8
EM05 = float(np.exp(-0.5))


class Buf:
    __slots__ = ("name", "w", "r", "parts")

    def __init__(self, name=""):
        self.name = name
        self.w = None
        self.r = []
        self.parts = None


def _flat(bufs):
    out = []
    for b in bufs:
        if b.parts:
            out.extend(b.parts)
        else:
            out.append(b)
    return out


class Sched:
    ENGS = ("pe", "act", "dve", "pool", "sp")

    def __init__(self, nc):
        self.nc = nc
        self.ops = {e: [] for e in self.ENGS}
        self.dma_rr = {e: 0 for e in self.ENGS}
        self.dma_hist = {e: [[] for _ in range(N_DMA_SEMS)] for e in self.ENGS}
        self.out_dmas = []

    def _deps(self, reads, writes):
        reads, writes = _flat(reads), _flat(writes)
        deps = []
        for b in reads:
            if b.w is not None:
                deps.append(b.w)
        for b in writes:
            if b.w is not None:
                deps.append(b.w)
            deps.extend(b.r)
        return deps

    def _commit(self, ref, reads, writes):
        reads, writes = _flat(reads), _flat(writes)
        for b in reads:
            b.r.append(ref)
        for b in writes:
            b.w = ref
            b.r = []

    def op(self, eng, fn, reads=(), writes=()):
        deps = self._deps(reads, writes)
        idx = len(self.ops[eng])
        self.ops[eng].append(dict(kind="op", fn=fn, deps=deps, sig=False, cnt=None))
        ref = ("op", eng, idx)
        self._commit(ref, reads, writes)
        return ref

    def dma(self, eng, out, in_, reads=(), writes=(), is_output=False):
        deps = self._deps(reads, writes)
        k = self.dma_rr[eng]
        self.dma_rr[eng] = (k + 1) % N_DMA_SEMS
        hist = self.dma_hist[eng][k]
        if hist:
            deps.append(hist[-1])
        val = 16 * (len(hist) + 1)
        ref = ("dma", eng, k, val)
        hist.append(ref)
        self.ops[eng].append(dict(kind="dma", out=out, in_=in_, deps=deps, semk=k, val=val))
        self._commit(ref, reads, writes)
        if is_output:
            self.out_dmas.append(ref)
        return ref

    def coll(self, fn, reads=(), writes=()):
        deps = self._deps(reads, writes)
        self.n_coll = getattr(self, "n_coll", 0) + 1
        ref = ("dma", "pool", N_DMA_SEMS, self.n_coll)
        self.ops["pool"].append(dict(kind="coll", fn=fn, deps=deps))
        self._commit(ref, reads, writes)
        return ref

    def emit(self):
        nc = self.nc

        def skip_same(d, e):
            return d[1] == e and (not SAME_ENGINE_SYNC or e == "pe")

        for e in self.ENGS:
            for o in self.ops[e]:
                last = {}
                for d in o["deps"]:
                    if d[0] == "op" and not skip_same(d, e):
                        if d[2] > last.get(d[1], -1):
                            last[d[1]] = d[2]
                o["last"] = last
                for pe_, idx_ in last.items():
                    self.ops[pe_][idx_]["sig"] = True
        for e in self.ENGS:
            c = 0
            for o in self.ops[e]:
                if o["kind"] == "op" and o["sig"]:
                    c += 1
                    o["cnt"] = c
        with contextlib.ExitStack() as st:
            esem = {e: st.enter_context(nc.semaphore("s_" + e)) for e in self.ENGS}
            dsem = {e: [st.enter_context(nc.semaphore("d_%s%d" % (e, k))) for k in range(N_DMA_SEMS + 1)]
                    for e in ("sp", "act", "pool")}
            block = st.enter_context(nc.Block())
            sched = self

            def run(e, eng):
                waited = {}
                for o in sched.ops[e]:
                    need = {}
                    for pe_, idx_ in o["last"].items():
                        need[("op", pe_)] = sched.ops[pe_][idx_]["cnt"]
                    for d in o["deps"]:
                        if d[0] != "op":
                            key = ("dma", d[1], d[2])
                            if d[3] > need.get(key, 0):
                                need[key] = d[3]
                    for key, v in need.items():
                        if waited.get(key, 0) >= v:
                            continue
                        waited[key] = v
                        s = esem[key[1]] if key[0] == "op" else dsem[key[1]][key[2]]
                        eng.wait_ge(s, v)
                    if o["kind"] == "op":
                        ins = o["fn"](eng)
                        if o["sig"]:
                            ins.then_inc(esem[e], 1)
                    elif o["kind"] == "coll":
                        o["fn"](eng).then_inc(dsem["pool"][N_DMA_SEMS])
                    else:
                        eng.dma_start(out=o["out"], in_=o["in_"]).then_inc(dsem[e][o["semk"]], 16)
                if e == "sp":
                    for ref in sched.out_dmas:
                        eng.wait_ge(dsem[ref[1]][ref[2]], ref[3])

            block.tensor(lambda eng: run("pe", eng))
            block.scalar(lambda eng: run("act", eng))
            block.vector(lambda eng: run("dve", eng))
            block.gpsimd(lambda eng: run("pool", eng))
            block.sync(lambda eng: run("sp", eng))


PV_FIELDS = [("ada_b", 48), ("n1g", 8), ("n2g", 8), ("mu_p", 12), ("mu_n", 12), ("a0f", 4), ("a0b", 4),
             ("k_k", 4), ("k_a", 4), ("r_k", 4), ("lnx_g", 4), ("lnx_b", 4), ("conv_b", 4), ("cln_g", 4),
             ("cln_b", 4), ("conv_w", 124)]
PV_OFF = {}
_o = 0
for _n, _c in PV_FIELDS:
    PV_OFF[_n] = _o
    _o += _c
NPV = _o


def _fm(v):
    v = np.asarray(v, np.float32).reshape(-1)
    return np.ascontiguousarray(v.reshape(-1, 128).T)


def _consts():
    idx = np.arange(128)
    s, t = idx[:, None], idx[None, :]
    cst = np.zeros((128, 8, 128), np.float32)
    cst[:, 0] = np.eye(128)
    cst[:, 1] = -EM05 * (s <= t)
    cst[:, 2] = -EM05 * (s < t)
    cst[:, 3] = -EM05 * (s >= t)
    cst[:, 4] = -EM05 * (s > t)
    cst[:, 5] = ((s // 64) == (t // 64))
    cst[:, 6] = 1.0 / 512
    cst[:, 7] = 1.0
    msk = np.zeros((128, 2, 2, 384), np.float32)
    for d in range(2):
        strict = (s < t) if d == 0 else (s > t)
        incl = (s <= t) if d == 0 else (s >= t)
        msk[:, d, :, 0:128] = strict[:, None, :]
        msk[:, d, :, 128:256] = incl[:, None, :]
        msk[:, d, :, 256:384] = strict.T[:, None, :]
    id4 = np.zeros((128, 2, 128), np.float32)
    id4[:] = np.eye(128)[:, None, :]
    mk = np.zeros((128, 4, 2, 128), np.float32)
    mk[:, 0] = (s // 16 == t // 16)[:, None, :]
    for li, b in enumerate((16, 32, 64)):
        mk[:, 1 + li] = ((s // (2 * b) == t // (2 * b)) & (s // b != t // b))[:, None, :]
    return cst.reshape(128, 1024), msk.reshape(128, 1536), id4.reshape(128, 256), mk.reshape(128, 1024)


class Arena:
    def __init__(self, t, words):
        self.t, self.words, self.ptr = t, words, 0

    def alloc(self, shape, dt=F32):
        free = int(np.prod(shape[1:]))
        words = free if dt == F32 else (free + 1) // 2
        assert self.ptr + words <= self.words, ("arena overflow", self.ptr, words, self.words)
        ap = self.t[0:shape[0], self.ptr:self.ptr + words]
        self.ptr += words
        if dt == BF16:
            ap = ap.bitcast(BF16)
        if len(shape) == 3:
            ap = ap.rearrange("p (a b) -> p a b", b=shape[2])
        elif len(shape) == 4:
            ap = ap.rearrange("p (a b c) -> p a b c", b=shape[2], c=shape[3])
        elif len(shape) == 5:
            ap = ap.rearrange("p (a b c d) -> p a b c d", b=shape[2], c=shape[3], d=shape[4])
        return ap


def build(dbg=(), stop_after=None):
    nc = bass.Bass("TRN2", target_bir_lowering=False)
    S = Sched(nc)
    st = contextlib.ExitStack()

    def din(name, shape, dt=F32):
        return nc.dram_tensor(name, list(shape), dt, kind="ExternalInput").ap()

    def dout(name, shape):
        return nc.dram_tensor(name, list(shape), F32, kind="ExternalOutput").ap()

    def sb(name, shape, dt=F32):
        t = st.enter_context(nc.sbuf_tensor(name, list(shape), dt))
        return t[:]

    xm = din("xm", [1024, 1024])
    xh = din("xh", [2, 1024])
    hmask = din("hmask", [128, 16])
    condT = din("condT", [128, 24])
    pvec_d = din("pvec", [128, NPV])
    w0row_d = din("w0row", [1, 1024])
    fgrep_d = din("fgrep", [128, 1024])
    cst_d = din("cst", [128, 1024])
    msk_d = din("msk", [128, 1536])
    id4_d = din("id4", [128, 256])
    mk_d = din("mk", [128, 1024])
    s0T_d = din("s0T", [2, 128, 4, 64])
    sel_d = din("sel", [128, 8])
    idh_d = din("idh", [128, 256])
    bounce_d = nc.dram_tensor("bounce", [128, 1024], F32).ap()
    gath_d = nc.dram_tensor("gath", [512, 1024], F32).ap()
    w1cat_d = din("w1cat", [1024, 384])
    w2dec_d = din("w2dec", [128, 512])
    a2cat_d = din("a2cat", [128, 512])
    g2_d = din("g2", [128, 512])
    ada_w_d = din("ada_w", [1024, 1536])
    adab_d = din("adab", [128, 12])
    selb_d = din("selb", [128, 2])
    abounce_d = nc.dram_tensor("abounce", [128, 36], F32).ap()
    agath_d = nc.dram_tensor("agath", [512, 36], F32).ap()
    w_in_d = din("w_in", [1024, 4608])
    wor_d = din("w_out_rwkv", [512, 1024])
    woc_d = din("w_out_conv", [512, 1024])
    wo_d = din("w_o", [1024, 1024])
    w1_d = din("mlp_w1", [1024, 4096])
    w2_d = din("mlp_w2", [4096, 1024])
    y_d = dout("y", [1024, 1024])
    st_d = dout("st", [2, 2, 128, 4, 64])
    dbg_d = {n: dout("dbg_" + n, shp) for n, shp in dbg}

    xnT = sb("xnT", [128, 8, 1024], BF16)
    xnTh = sb("xnTh", [128, 8, 2], BF16)
    pvec = sb("pvec_sb", [128, NPV])
    hm = sb("hm", [128, 16])
    cT = sb("cT", [128, 24]); scT = sb("scT", [128, 24])
    adab = sb("adab_sb", [128, 12]); selb = sb("selb_sb", [128, 2])
    mod = sb("mod", [128, 48, 2])
    A1 = sb("A1", [128, 8, 2]); A2 = sb("A2", [128, 8, 2])
    cst = sb("cst_sb", [128, 8, 128])
    ident32 = cst[:, 0, :]
    blk1 = cst[:, 5, :]
    ident16 = sb("ident16", [128, 128], BF16)
    id4 = sb("id4_sb", [128, 2, 128], BF16)
    msk = sb("msk_sb", [128, 2, 2, 384], BF16)
    mk = sb("mk_sb", [128, 4, 2, 128], BF16)
    w0row = sb("w0row_sb", [1, 1024])
    ones32 = sb("ones32", [1, 128])
    ss = sb("ss", [128, 16]); rstd = sb("rstd", [128, 16])
    epsr = sb("epsr", [128, 4])
    c0v = sb("c0v", [128, 12])
    a0h = sb("a0h", [128, 8]); kah = sb("kah", [128, 4])
    wring = [sb("wring%d" % i, [128, 4096], BF16) for i in range(4)]
    wringb = [Buf("wring%d" % i) for i in range(4)]
    w2dec = sb("w2dec_sb", [128, 512], BF16); a2cat = sb("a2cat_sb", [128, 512], BF16); g2 = sb("g2w", [128, 512], BF16)
    hTd = sb("hTd", [128, 1024], BF16); hTi = sb("hTi", [128, 1024], BF16); hTg = sb("hTg", [128, 1024], BF16)
    outT16 = sb("outT16", [128, 4, 1024], BF16)
    selv = sb("selv", [128, 8])
    hsel = sb("hsel", [128, 2])
    idh = sb("idh_sb", [128, 4, 64])
    Hin = sb("Hin", [128, 2, 4, 64])
    ARENA_WORDS = 31400
    arena_t = sb("arena", [128, ARENA_WORDS])
    AR = Arena(arena_t, ARENA_WORDS)

    psF = [st.enter_context(nc.psum_tensor("psF%d" % i, [128, 512], F32))[:] for i in range(6)]
    psB = [st.enter_context(nc.psum_tensor("psB%d" % i, [128, 512], BF16))[:] for i in range(2)]
    psFb = [Buf("psF%d" % i) for i in range(6)]
    psBb = [Buf("psB%d" % i) for i in range(2)]
    rr = {"F": 0, "B": 0, "W": 0}

    psHb = [Buf("psH%d" % i) for i in range(12)]
    for i in range(6):
        psFb[i].parts = [psHb[i], psHb[i + 6]]
    rr["H"] = 0

    def getF():
        i = rr["F"]; rr["F"] = (i + 1) % 6
        return psF[i], psFb[i]

    def getH():
        k = rr["H"]; rr["H"] = (k + 1) % 12
        return psF[k % 6][:, (k // 6) * 256:(k // 6) * 256 + 256], psHb[k]

    def getB():
        i = rr["B"]; rr["B"] = (i + 1) % 2
        return psB[i], psBb[i]

    def pv(name, i=0, n=1):
        o = PV_OFF[name] + i
        return pvec[:, o:o + n]

    def mm(out, lhsT, rhs, start, stop, r, w, tp=None):
        kw = {} if tp is None else dict(tile_position=tp)
        return S.op("pe", lambda e: e.matmul(out, lhsT, rhs, start=start, stop=stop, **kw), reads=r, writes=w)

    def tr(out, in_, idn, r, w):
        return S.op("pe", lambda e: e.transpose(out, in_, idn), reads=r, writes=w)

    def act(out, in_, func, r, w, bias=None, scale=None, accum=None):
        kw = {}
        if bias is not None: kw["bias"] = bias
        if scale is not None: kw["scale"] = scale
        if accum is not None: kw["accum_out"] = accum
        return S.op("act", lambda e: e.activation(out, in_, func, **kw), reads=r, writes=w)

    def tt(eng, out, a, b, op, r, w):
        return S.op(eng, lambda e: e.tensor_tensor(out, a, b, op), reads=r, writes=w)

    def ts(eng, out, a, s1, s2, op0, op1, r, w):
        if op1 is None:
            return S.op(eng, lambda e: e.tensor_scalar(out, a, s1, None, op0), reads=r, writes=w)
        return S.op(eng, lambda e: e.tensor_scalar(out, a, s1, s2, op0, op1), reads=r, writes=w)

    def stt(eng, out, a, s, b, op0, op1, r, w):
        return S.op(eng, lambda e: e.scalar_tensor_tensor(out, a, s, b, op0, op1), reads=r, writes=w)

    def cp(eng, out, in_, r, w):
        if eng == "act":
            return S.op("act", lambda e: e.copy(out, in_), reads=r, writes=w)
        return S.op(eng, lambda e: e.tensor_copy(out, in_), reads=r, writes=w)

    def recip(out, in_, r, w):
        return S.op("dve", lambda e: e.reciprocal(out, in_), reads=r, writes=w)

    def memset(eng, ap, val, w):
        return S.op(eng, lambda e: e.memset(ap, val), writes=w)

    B = {}
    carry = {"refs": []}

    def bf(name):
        if name not in B:
            b = Buf(name)
            b.r = list(carry["refs"])
            B[name] = b
        return B[name]

    def new_phase():
        refs = set()
        for b in list(B.values()) + psHb + psBb + wringb:
            if b.w is not None:
                refs.add(b.w)
            refs.update(b.r)
        best = {}
        for rf in refs:
            key = rf[:2] if rf[0] == "op" else rf[:3]
            val = rf[2] if rf[0] == "op" else rf[3]
            if key not in best or val > (best[key][2] if rf[0] == "op" else best[key][3]):
                best[key] = rf
        carry["refs"] = list(best.values())
        for wb_ in wringb:
            wb_.r = list(wb_.r) + list(carry["refs"])
        AR.ptr = 0
        for k in [k for k in B if k.startswith("a_")]:
            del B[k]

    def dump(name, ap, r):
        if name in dbg_d:
            S.dma("pool", dbg_d[name], ap, reads=r, is_output=True)

    def loadw(src_ap, view_fn, slot=None):
        if slot is None:
            i = rr["W"]; rr["W"] = (i + 1) % 4
        else:
            i = slot
        v = view_fn(wring[i])
        S.dma("pool", v, src_ap, writes=[wringb[i]])
        return v, wringb[i]

    S.dma("sp", pvec, pvec_d, writes=[bf("pvec")])
    S.dma("sp", cT, condT, writes=[bf("cT")])
    S.dma("sp", hm, hmask, writes=[bf("hm")])
    S.dma("sp", cst.rearrange("p a b -> p (a b)"), cst_d, writes=[bf("cst")])
    S.dma("sp", w0row, w0row_d, writes=[bf("w0row")])
    S.dma("sp", Hin[:, 0], s0T_d[0], writes=[bf("Hin0")])
    S.dma("sp", Hin[:, 1], s0T_d[1], writes=[bf("Hin1")])
    S.dma("sp", selv, sel_d, writes=[bf("selv")])
    S.dma("sp", idh.rearrange("p a b -> p (a b)"), idh_d, writes=[bf("idh")])
    S.dma("pool", ident16, cst_d[:, 0:128], writes=[bf("ident16")])
    S.dma("pool", id4.rearrange("p a b -> p (a b)"), id4_d, writes=[bf("id4")])
    S.dma("pool", msk.rearrange("p a b c -> p (a b c)"), msk_d, writes=[bf("msk")])
    S.dma("pool", mk.rearrange("p a b c -> p (a b c)"), mk_d, writes=[bf("mk")])
    S.dma("pool", w2dec, w2dec_d, writes=[bf("w2dec")])
    S.dma("pool", a2cat, a2cat_d, writes=[bf("a2cat")])
    S.dma("pool", g2, g2_d, writes=[bf("g2")])
    x_sb = AR.alloc([128, 8, 1024])
    xs16 = AR.alloc([128, 8, 1024], BF16)
    junk = AR.alloc([128, 1024])
    xh_sb = AR.alloc([2, 1024]); xh16 = AR.alloc([2, 1024], BF16); xht = AR.alloc([128, 8, 2])
    S.dma("sp", xh_sb, xh, writes=[bf("a_xh")])
    for t in range(8):
        S.dma("sp", x_sb[:, t, :], xm[t * 128:(t + 1) * 128, :], writes=[bf("a_x%d" % t)])
    memset("dve", epsr[:, 0:1], 1e-6, [bf("epsr")])
    memset("dve", epsr[:, 1:2], 64e-5, [bf("epsr")])
    memset("dve", epsr[:, 2:3], 1e-5, [bf("epsr")])
    memset("dve", ss, 0.0, [bf("ss")])
    memset("dve", ones32, 1.0, [bf("ones32")])
    memset("dve", hsel, 0.0, [bf("hsel")])
    memset("dve", hsel[0:64, 0:1], 1.0, [bf("hsel")])
    memset("dve", hsel[64:128, 1:2], 1.0, [bf("hsel")])
    ts("dve", c0v, pv("mu_p", 0, 12), -1.0, 1.0, ALU.mult, ALU.add, [bf("pvec")], [bf("c0v")])
    tt("dve", c0v, c0v, pv("mu_n", 0, 12), ALU.subtract, [bf("c0v"), bf("pvec")], [bf("c0v")])
    ts("dve", a0h, pv("a0f", 0, 8), 0.5, None, ALU.mult, None, [bf("pvec")], [bf("a0h")])
    ts("dve", kah, pv("k_a", 0, 4), 0.5, None, ALU.mult, None, [bf("pvec")], [bf("kah")])

    def rmsnorm_T(x_sb, xs16, junk, Aw, Awname, shc, xn="a_x", part=None):
        if part in (None, 1):
            memset("dve", ss[:, 0:8], 0.0, [bf("ss")])
            for t in range(8):
                act(junk, x_sb[:, t, :], AF.Square, [bf(xn + "%d" % t)], [bf("a_junk"), bf("ss")], accum=ss[:, t:t + 1])
            act(rstd[:, 0:8], ss[:, 0:8], AF.Sqrt, [bf("ss"), bf("epsr")], [bf("rstd")], bias=epsr[:, 0:1], scale=1.0 / 1024)
            recip(rstd[:, 0:8], rstd[:, 0:8], [bf("rstd")], [bf("rstd")])
            for t in range(8):
                if t % 2 == 0:
                    ts("dve", xs16[:, t, :], x_sb[:, t, :], rstd[:, t:t + 1], None, ALU.mult, None,
                       [bf(xn + "%d" % t), bf("rstd")], [bf("a_xs16_%d" % t)])
                else:
                    act(xs16[:, t, :], x_sb[:, t, :], AF.Copy, [bf(xn + "%d" % t), bf("rstd")], [bf("a_xs16_%d" % t)], scale=rstd[:, t:t + 1])
        if part in (None, 2):
            for kc in range(8):
                for half in range(2):
                    p, pb = getB()
                    for q in range(4):
                        t = half * 4 + q
                        tr(p[:, q * 128:(q + 1) * 128], xs16[:, t, kc * 128:(kc + 1) * 128], ident16,
                           [bf("a_xs16_%d" % t), bf("ident16")], [pb])
                    act(xnT[:, kc, half * 512:(half + 1) * 512], p[:, 0:512], AF.Identity, [pb, bf(Awname), bf("mod")],
                        [bf("xnT%d" % half)], bias=mod[:, shc + kc, half:half + 1], scale=Aw[:, kc, half:half + 1])

    wl1v, wl1b = loadw(w1cat_d.rearrange("(kc p) n -> p kc n", p=128),
                     lambda w: w[:, 0:3072].rearrange("p (kc n) -> p kc n", kc=8))
    w_in_v = w_in_d.rearrange("(kc p) n -> p kc n", p=128)
    wrkv = []
    for i in range(3):
        v, b_ = loadw(w_in_v[:, :, i * 512:(i + 1) * 512], lambda w: w.rearrange("p (kc n) -> p kc n", kc=8))
        wrkv.append((v, b_))

    S.dma("sp", adab, adab_d, writes=[bf("adab")])
    S.dma("sp", selb, selb_d, writes=[bf("selb")])
    act(scT, cT, AF.Silu, [bf("cT")], [bf("scT")])
    ada_v = ada_w_d.rearrange("(kc p) n -> p kc n", p=128)
    adaw = AR.alloc([128, 8, 1536])
    S.dma("sp", adaw[:, 0:4, :], ada_v[:, 0:4, :], writes=[bf("a_adaw0")])
    S.dma("act", adaw[:, 4:8, :], ada_v[:, 4:8, :], writes=[bf("a_adaw1")])
    modrow = AR.alloc([3, 1536])
    for nchunk in range(3):
        pr_, prb_ = getF()
        for kc in range(8):
            mm(pr_[0:3, :], scT[:, kc * 3:kc * 3 + 3], adaw[:, kc, nchunk * 512:(nchunk + 1) * 512], kc == 0, kc == 7,
               [bf("a_adaw%d" % (kc // 4)), bf("scT")], [prb_])
        cp("act", modrow[:, nchunk * 512:(nchunk + 1) * 512], pr_[0:3, :], [prb_], [bf("a_modrow")])
    modp, modpb = getF()
    for f in range(12):
        mm(modp[:, f * 3:f * 3 + 3], modrow[:, f * 128:(f + 1) * 128], cst[0:3, 0, 0:3], True, True, [bf("a_modrow"), bf("cst")], [modpb])
    modpart = AR.alloc([128, 12, 3])
    for j in range(3):
        tt("dve", modpart[:, :, j], modp[:, 0:36].rearrange("p (f j) -> p f j", j=3)[:, :, j], adab, ALU.add, [modpb, bf("adab")], [bf("a_modpart")])
    S.dma("pool", abounce_d, modpart.rearrange("p f j -> p (f j)"), reads=[bf("a_modpart")], writes=[bf("abounce")])
    S.coll(lambda en: en.collective_compute("AllGather", ALU.bypass, replica_groups=[[0, 1, 2, 3], [4, 5, 6, 7]],
                                            ins=[abounce_d.opt()], outs=[agath_d.opt()]),
           reads=[bf("abounce")], writes=[bf("agath")])
    modall = AR.alloc([128, 48, 3])
    S.dma("pool", modall.rearrange("p (r f) j -> p r (f j)", r=4), agath_d.rearrange("(r p) n -> p r n", p=128), reads=[bf("agath")], writes=[bf("a_modall")])
    rmsnorm_T(x_sb, xs16, junk, A1, "A1", 0, part=1)
    cp("dve", mod[:, :, 0], modall[:, :, 0], [bf("a_modall")], [bf("mod")])
    ts("dve", mod[:, :, 1], modall[:, :, 1], selb[:, 0:1], None, ALU.mult, None, [bf("a_modall"), bf("selb")], [bf("mod")])
    stt("dve", mod[:, :, 1], modall[:, :, 2], selb[:, 1:2], mod[:, :, 1], ALU.mult, ALU.add, [bf("a_modall"), bf("selb"), bf("mod")], [bf("mod")])
    for j in range(2):
        stt("dve", A1[:, :, j], mod[:, 8:16, j], 1.0, pv("n1g", 0, 8), ALU.add, ALU.mult, [bf("mod"), bf("pvec")], [bf("A1")])
        stt("dve", A2[:, :, j], mod[:, 32:40, j], 1.0, pv("n2g", 0, 8), ALU.add, ALU.mult, [bf("mod"), bf("pvec")], [bf("A2")])
    dump("mod", mod.rearrange("p f j -> p (f j)"), [bf("mod")])

    rmsnorm_T(x_sb, xs16, junk, A1, "A1", 0, part=2)
    memset("dve", ss[0:2, 8:9], 0.0, [bf("ssh")])
    act(junk[0:2, :], xh_sb, AF.Square, [bf("a_xh")], [bf("a_junk"), bf("ssh")], accum=ss[0:2, 8:9])
    act(rstd[0:2, 8:9], ss[0:2, 8:9], AF.Sqrt, [bf("ssh"), bf("epsr")], [bf("rstdh")], bias=epsr[0:2, 0:1], scale=1.0 / 1024)
    recip(rstd[0:2, 8:9], rstd[0:2, 8:9], [bf("rstdh")], [bf("rstdh")])
    ts("dve", xh16, xh_sb, rstd[0:2, 8:9], None, ALU.mult, None, [bf("a_xh"), bf("rstdh")], [bf("a_xh16")])
    p, pb = getB()
    for kc in range(8):
        tr(p[:, kc * 2:kc * 2 + 2], xh16[:, kc * 128:(kc + 1) * 128], ident16[0:2, 0:2], [bf("a_xh16"), bf("ident16")], [pb])
    for kc in range(8):
        act(xht[:, kc, :], p[:, kc * 2:kc * 2 + 2], AF.Identity, [pb, bf("A1"), bf("mod")], [bf("a_xht")],
            bias=mod[:, kc, 1:2], scale=A1[:, kc, 1:2])
    tt("dve", xnTh, xht, hm.rearrange("p (k j) -> p k j", j=2), ALU.mult, [bf("a_xht"), bf("hm")], [bf("xnTh")])
    XN = [bf("xnT0"), bf("xnT1")]
    dump("xnT", xnT.rearrange("p k n -> p (k n)"), XN)

    class _Stop(Exception):
        pass

    def chk(tag):
        if stop_after == tag:
            raise _Stop()

    def _rest():
        for mt, (dst, fn, nm) in enumerate(((hTd, AF.Tanh, "hTd"), (hTi, AF.Identity, "hTi"), (hTg, AF.Sigmoid, "hTg"))):
            for half in range(2):
                p, pb = getF()
                for kc in range(8):
                    mm(p, wl1v[:, kc, mt * 128:(mt + 1) * 128], xnT[:, kc, half * 512:(half + 1) * 512], kc == 0, kc == 7,
                       [wl1b, XN[half]], [pb])
                act(dst[:, half * 512:(half + 1) * 512], p, fn, [pb], [bf(nm)])
        dump("hTd", hTd, [bf("hTd")])
        chk("p3")

        new_phase()
        STO = dict(
            QT=AR.alloc([128, 4, 4, 2 * 128], BF16),
            PT=AR.alloc([128, 4, 4, 2 * 64], BF16),
            G=AR.alloc([128, 4, 4, 2 * 64]),
            Y0=AR.alloc([128, 4, 512]),
            Gam=AR.alloc([128, 4, 4, 2]),
            H0=AR.alloc([128, 4, 4, 2 * 64], BF16),
            bonus=AR.alloc([128, 4, 512], BF16),
            )
        rawA = [AR.alloc([128, 516])] * 2
        rawB = rawA
        rT16 = AR.alloc([128, 512], BF16); vT16 = AR.alloc([128, 512], BF16); kT = AR.alloc([128, 512])
        alias_ptr = AR.ptr
        rrk = AR.alloc([128, 512], BF16)
        kk = AR.alloc([128, 512])
        Vtok2 = [AR.alloc([128, 4, 128], BF16) for _ in range(2)]
        sg32 = AR.alloc([128, 4, 128])
        Ei = AR.alloc([128, 512]); Ex = AR.alloc([128, 512]); En = AR.alloc([128, 512])
        shtmp = Ex
        aT = AR.alloc([128, 512]); t1 = AR.alloc([128, 512]); bT = AR.alloc([128, 512])
        sq, rs = t1, bT
        kd = [AR.alloc([128, 512]) for _ in range(2)]
        AR16 = AR.alloc([128, 4, 256], BF16); BT16 = AR.alloc([128, 512], BF16); KT16 = AR.alloc([128, 512], BF16)
        AZ = AR.alloc([128, 4, 2, 128], BF16); Btok = AR.alloc([128, 4, 128], BF16); Ktok = AR.alloc([128, 4, 128], BF16)
        NGS = 4
        GB = []
        gb_ptr = {}
        for i_ in range(NGS):
            gb_ptr[i_] = AR.ptr
            if i_ == 3:
                gb3_ptr = AR.ptr
            if i_ < 2:
                mb_ = AR.alloc([128, 4, 2, 2, 128], BF16)
            else:
                mb_ = wring[0][:, (i_ - 2) * 2048:(i_ - 1) * 2048].rearrange("p (m x e c) -> p m x e c", m=4, x=2, e=2)
                bmb = bf("a_MB_%d" % i_)
                bmb.r = bmb.r + list(wringb[0].r) + ([wringb[0].w] if wringb[0].w is not None else [])
            g_ = dict(NL=AR.alloc([128, 2, 2, 128], BF16), ARB=AR.alloc([128, 2, 128], BF16), KA=AR.alloc([128, 2, 256], BF16),
                      MB=mb_,
                      RR=[AR.alloc([128, 2, 2, 128], BF16) for _ in range(2)], LN=[AR.alloc([128, 2, 2, 128], BF16) for _ in range(2)],
                      XX=AR.alloc([128, 2, 2, 128], BF16), WU=AR.alloc([128, 2, 128], BF16))
            GB.append(g_)
        AR16m = [AR.alloc([128, 4, 256], BF16) for _ in range(2)]
        HS = [dict(Hst=AR.alloc([128, 4, 64]), Hs16=AR.alloc([128, 4, 64], BF16), Htmp=AR.alloc([128, 4, 64]))]
        sp__ = AR.ptr
        AR.ptr = gb3_ptr
        for _ in range(3):
            HS.append(dict(Hst=AR.alloc([128, 4, 64]), Hs16=AR.alloc([128, 4, 64], BF16), Htmp=AR.alloc([128, 4, 64])))
        assert AR.ptr <= gb3_ptr + 2048
        AR.ptr = sp__
        GB3_NAMES = ["a_%s_3" % k_ for k_ in ("NL", "ARB", "KA", "RR0", "RR1", "LN0", "LN1", "XX", "WU")]
        GB01_NAMES = ["a_%s_%d" % (k_, i_) for i_ in range(2) for k_ in ("NL", "ARB", "KA", "MB", "RR0", "RR1", "LN0", "LN1", "XX", "WU")]
        OT = []
        sp2__ = AR.ptr
        for k_ in range(4):
            if k_ % 2 == 0:
                AR.ptr = gb_ptr[k_ // 2]
            y_ = AR.alloc([128, 512]); q_ = AR.alloc([128, 512]); n16_ = AR.alloc([128, 512], BF16); g_s = AR.alloc([128, 48])
            OT.append(dict(ytok=y_, ysq=q_, yn16=n16_, gst=g_s, ynT=q_.rearrange("p (c t) -> p c t", t=128)))
            assert AR.ptr <= gb_ptr[k_ // 2] + 3072
        AR.ptr = sp2__
        OT_NAMES = ["a_o%s_%d" % (k_, i_) for i_ in range(4) for k_ in ("ytok", "ysq", "yn16", "gst")]
        HS_NAMES = ["a_%s%d" % (k_, i_) for i_ in range(1, 4) for k_ in ("Hst", "Hs16", "Htmp")]

        def inherit(dst_names, src_names):
            refs = []
            for nm in src_names:
                b_ = bf(nm)
                if b_.w is not None:
                    refs.append(b_.w)
                refs.extend(b_.r)
            for nm in dst_names:
                b_ = bf(nm)
                b_.r = list(b_.r) + refs
        memset("dve", rawA[0], 0.0, [bf("a_rawA0")])

        free_banks = list(range(6))

        def take(nb):
            while len(free_banks) < nb:
                yield
            out = []
            for _ in range(nb):
                i = free_banks.pop(0)
                out.append((psF[i], psFb[i], i))
            return out

        def give(*idx):
            free_banks.extend(idx)

        freeB = [0, 1]

        def takeB():
            while not freeB:
                yield
            i = freeB.pop(0)
            return psB[i], psBb[i], i

        def group_gen(blk, ct, d, t, bs):
            sto = STO
            G_ = GB[bs]
            Vtok, VtB = Vtok2[ct % 2], bf("a_Vtok%d" % (ct % 2))
            n = lambda s_: bf("a_%s_%d" % (s_, bs))
            NL, ARB, KA, MB, RR, LN, XX, WU = (G_[k_] for k_ in ("NL", "ARB", "KA", "MB", "RR", "LN", "XX", "WU"))
            v2 = lambda q: q.rearrange("p (a b) -> p a b", b=128)
            eA, eB = "act", "act"
            v22 = lambda q: q.rearrange("p (x a b) -> p x a b", x=2, b=128)
            tcols = slice(t * 128, (t + 1) * 128)
            (pA, pAb, iA), (pK, pKb, iK), (pC, pCb, iC) = yield from take(3)
            for e in range(2):
                mm(pA[:, e * 256:(e + 1) * 256], BT16[:, tcols], AR16m[e][:, t, :], True, True, [bf("a_BT16"), bf("a_AR16m")], [pAb])
                mm(pC[:, e * 128:(e + 1) * 128], AR16m[e][:, t, 0:128], BT16[:, tcols], True, True, [bf("a_AR16m"), bf("a_BT16")], [pCb])
                mm(pK[:, e * 256:(e + 1) * 256], KT16[:, tcols], AR16m[e][:, t, :], True, True, [bf("a_KT16"), bf("a_AR16m")], [pKb])
            yield
            pA3 = pA.rearrange("p (a b) -> p a b", b=256)
            pK3 = pK.rearrange("p (a b) -> p a b", b=256)
            tt("dve", NL[:, 0], pA3[:, :, 0:128], msk[:, d, 0:2, 0:128], ALU.mult, [pAb, bf("msk")], [n("NL")])
            tt("dve", NL[:, 1], v2(pC[:, 0:256]), msk[:, d, 0:2, 256:384], ALU.mult, [pCb, bf("msk")], [n("NL")])
            yield
            tt("dve", MB.rearrange("p m x e c -> p m x (e c)"),
               NL.rearrange("p x e c -> p x (e c)").unsqueeze(1).to_broadcast([128, 4, 2, 256]),
               mk.rearrange("p m e c -> p m (e c)").unsqueeze(2).to_broadcast([128, 4, 2, 256]), ALU.mult, [n("NL"), bf("mk")], [n("MB")])
            tt("dve", RR[0], MB[:, 0], id4[:, 0:2].unsqueeze(1).to_broadcast([128, 2, 2, 128]), ALU.add, [n("MB"), bf("id4")], [n("RR0")])
            tt("dve", ARB, pA3[:, :, 128:256], msk[:, d, 0:2, 128:256], ALU.mult, [pAb, bf("msk")], [n("ARB")])
            tt("dve", KA, pK3, msk[:, d, 0:2, 0:256], ALU.mult, [pKb, bf("msk")], [n("KA")])
            give(iA, iK, iC)
            yield
            Nc, Lc, NLb = MB[:, 0, 0], MB[:, 0, 1], n("MB")
            for lev in range(3):
                if lev == 0:
                    (p1, p1b, i1), (pZ, pZb, iZ) = yield from take(2)
                    for e in range(2):
                        mm(pZ[:, e * 64:(e + 1) * 64], KA[:, e, 0:128], Vtok[:, t, e * 64:(e + 1) * 64], True, True, [n("KA"), VtB], [pZb])
                else:
                    ((p1, p1b, i1),) = yield from take(1)
                for e in range(2):
                    mm(p1[:, e * 128:(e + 1) * 128], Nc[:, e, :], Lc[:, e, :], True, True, [NLb], [p1b])
                    mm(p1[:, 256 + e * 128:256 + (e + 1) * 128], Lc[:, e, :], Nc[:, e, :], True, True, [NLb], [p1b])
                yield
                LNn, LNb = LN[lev % 2], n("LN%d" % (lev % 2))
                cp(eB, LNn, v22(p1), [p1b], [LNb])
                give(i1)
                if lev == 0:
                    cp("act", AZ[:, t, :, 64:128], pZ[:, 0:128].rearrange("p (e k) -> p e k", k=64), [pZb], [bf("a_AZ%d" % t)])
                    give(iZ)
                yield
                c_, n_ = lev % 2, (lev + 1) % 2
                ((p2, p2b, i2),) = yield from take(1)
                for e in range(2):
                    mm(p2[:, e * 128:(e + 1) * 128], LNn[:, 0, e, :], RR[c_][:, 0, e, :], True, False, [LNb, n("RR%d" % c_)], [p2b])
                    mm(p2[:, e * 128:(e + 1) * 128], ident16, RR[c_][:, 0, e, :], False, True, [bf("ident16"), n("RR%d" % c_)], [p2b])
                    mm(p2[:, 256 + e * 128:256 + (e + 1) * 128], LNn[:, 1, e, :], RR[c_][:, 1, e, :], True, False, [LNb, n("RR%d" % c_)], [p2b])
                    mm(p2[:, 256 + e * 128:256 + (e + 1) * 128], ident16, RR[c_][:, 1, e, :], False, True, [bf("ident16"), n("RR%d" % c_)], [p2b])
                yield
                cp(eA, RR[n_], v22(p2), [p2b], [n("RR%d" % n_)])
                give(i2)
                yield
                Nc, Lc, NLb = LNn[:, 1], LNn[:, 0], LNb
            cur = 1
            for li in range(3):
                nx = 1 - cur
                D_, Dt_, Db_ = RR[cur][:, 0], RR[cur][:, 1], n("RR%d" % cur)
                O_, Ot_, Ob_ = MB[:, 1 + li, 0], MB[:, 1 + li, 1], n("MB")
                ((p1, p1b, i1),) = yield from take(1)
                for e in range(2):
                    mm(p1[:, e * 128:(e + 1) * 128], Ot_[:, e, :], D_[:, e, :], True, True, [Ob_, Db_], [p1b])
                    if li < 2:
                        mm(p1[:, 256 + e * 128:256 + (e + 1) * 128], O_[:, e, :], Dt_[:, e, :], True, True, [Ob_, Db_], [p1b])
                yield
                if li < 2:
                    cp(eB, XX, v22(p1), [p1b], [n("XX")])
                else:
                    cp(eB, XX[:, 0], v2(p1[:, 0:256]), [p1b], [n("XX")])
                give(i1)
                yield
                ((p2, p2b, i2),) = yield from take(1)
                for e in range(2):
                    mm(p2[:, e * 128:(e + 1) * 128], Dt_[:, e, :], XX[:, 0, e, :], True, False, [Db_, n("XX")], [p2b])
                    mm(p2[:, e * 128:(e + 1) * 128], ident16, D_[:, e, :], False, True, [bf("ident16"), Db_], [p2b])
                    if li < 2:
                        mm(p2[:, 256 + e * 128:256 + (e + 1) * 128], D_[:, e, :], XX[:, 1, e, :], True, False, [Db_, n("XX")], [p2b])
                        mm(p2[:, 256 + e * 128:256 + (e + 1) * 128], ident16, Dt_[:, e, :], False, True, [bf("ident16"), Db_], [p2b])
                yield
                if li < 2:
                    cp(eA, RR[nx], v22(p2), [p2b], [n("RR%d" % nx)])
                else:
                    cp(eA, RR[nx][:, 0], v2(p2[:, 0:256]), [p2b], [n("RR%d" % nx)])
                give(i2)
                yield
                cur = nx
            TT, TTb = RR[0][:, 0], n("RR0")
            ((pW, pWb, iW),) = yield from take(1)
            for e in range(2):
                mm(pW[:, e * 128:(e + 1) * 128], TT[:, e, :], AZ[:, t, e, :], True, True, [TTb, bf("a_AZ%d" % t), bf("a_AZ")], [pWb])
            yield
            cp(eA, WU, v2(pW[:, 0:256]), [pWb], [n("WU")])
            give(iW)
            yield
            (pP, pPb, iP), (pF, pFb, iF) = yield from take(2)
            for e in range(2):
                es = slice(e * 64, e * 64 + 64)
                tpo = (0, 64) if e else None
                bk = Btok[:, t, e * 64:(e + 1) * 64]
                kkk = Ktok[:, t, e * 64:(e + 1) * 64]
                vv = Vtok[:, t, e * 64:(e + 1) * 64]
                mm(pP[es, 0:64], WU[:, e, 0:64], bk, True, True, [n("WU"), bf("a_Btok")], [pPb], tp=tpo)
                mm(pF[es, 0:64], bk, WU[:, e, 64:128], True, False, [n("WU"), bf("a_Btok")], [pFb], tp=tpo)
                mm(pF[es, 0:64], kkk, vv, False, True, [bf("a_Ktok"), VtB], [pFb], tp=tpo)
                mm(pF[es, 64:192], WU[:, e, 0:64], ARB[:, e, :], True, True, [n("WU"), n("ARB")], [pFb], tp=tpo)
                mm(pF[:, 192 + e * 64:192 + (e + 1) * 64], ARB[:, e, :], WU[:, e, 64:128], True, False, [n("WU"), n("ARB")], [pFb])
                mm(pF[:, 192 + e * 64:192 + (e + 1) * 64], KA[:, e, 128:256], vv, False, True, [n("KA"), VtB], [pFb])
            yield
            cp("act", sto["PT"][:, ct, t, d * 64:(d + 1) * 64], pP[:, 0:64], [pPb], [bf("a_PT")])
            ts("dve", sto["G"][:, ct, t, d * 64:(d + 1) * 64], pF[:, 0:64], sto["Gam"][:, ct, t, d:d + 1], None, ALU.mult, None, [pFb, bf("a_Gam")], [bf("a_G")])
            tt("dve", sto["QT"][:, ct, t, d * 128:(d + 1) * 128], pF[:, 64:192], AR16[:, t, 128:256], ALU.add, [pFb, bf("a_AR16")], [bf("a_QT")])
            ydst = sto["Y0"][:, t, ct * 128:(ct + 1) * 128]
            if d == 0:
                cp("dve", ydst, pF[:, 192:320], [pFb], [bf("a_Y0_%d" % t)])
            else:
                tt("dve", ydst, pF[:, 192:320], ydst, ALU.add, [pFb, bf("a_Y0_%d" % t)], [bf("a_Y0_%d" % t)])
            give(iP, iF)
            yield

        def take_now(nb):
            assert len(free_banks) >= nb, "PSUM bank pool exhausted outside a generator"
            out = []
            for _ in range(nb):
                i = free_banks.pop(0)
                out.append((psF[i], psFb[i], i))
            return out

        def run_gens(gens):
            alive = list(gens)
            while alive:
                for g_ in list(alive):
                    try:
                        next(g_)
                    except StopIteration:
                        alive.remove(g_)

        def stage_ct(blk, ct):
            c0 = 0 if blk == "A" else 512
            raw, rb = rawA[0], bf("a_rawA0")
            Vt, Vtb = Vtok2[ct % 2], bf("a_Vtok%d" % (ct % 2))
            for wi, (ft, dst, dn) in enumerate(((ct, rT16, "a_rT"), (4 + ct, kT, "a_kT"), (8 + ct, vT16, "a_vT"))):
                slab, slb = wrkv[ft // 4]
                fo = (ft % 4) * 128
                ((p, pb, ip),) = yield from take(1)
                for kc in range(8):
                    mm(p, slab[:, kc, fo:fo + 128], xnT[:, kc, c0:c0 + 512], kc == 0, kc == 7, [slb, XN[c0 // 512]], [pb])
                if blk == "A":
                    r3 = raw.rearrange("p (s c) -> p s c", c=258)
                    cp("act", r3[:, :, 1:257], p.rearrange("p (s c) -> p s c", c=256), [pb], [rb])
                    give(ip)
                    prev, main, nxt = r3[:, :, 0:256], r3[:, :, 1:257], r3[:, :, 2:258]
                    v3 = lambda a_: a_.rearrange("p (s c) -> p s c", c=256)
                else:
                    ((p2, p2b, ip2),) = yield from take(1)
                    for kc in range(8):
                        mm(p2[:, 0:2], slab[:, kc, fo:fo + 128], xnTh[:, kc, :], kc == 0, kc == 7, [slb, bf("xnTh")], [p2b])
                    cp("act", raw[:, 1:513], p, [pb], [rb])
                    cp("act", raw[:, 0:1], p2[:, 0:1], [p2b], [rb])
                    cp("act", raw[:, 513:514], p2[:, 1:2], [p2b], [rb])
                    give(ip, ip2)
                    prev, main, nxt = raw[:, 0:512], raw[:, 1:513], raw[:, 2:514]
                    v3 = lambda a_: a_
                yield
                act(v3(shtmp), main, AF.Identity, [rb, bf("c0v")], [bf("a_Ex")], scale=c0v[:, ft:ft + 1])
                yield
                stt("dve", v3(shtmp), prev, pv("mu_p", ft), v3(shtmp), ALU.mult, ALU.add, [rb, bf("pvec"), bf("a_Ex")], [bf("a_Ex")])
                stt("dve", v3(dst), nxt, pv("mu_n", ft), v3(shtmp), ALU.mult, ALU.add, [rb, bf("pvec"), bf("a_Ex")], [bf(dn)])
                yield
            act(sq, kT, AF.Square, [bf("a_kT"), bf("pvec")], [bf("a_t1")], scale=pv("k_k", ct))
            ((p, pb, ip),) = yield from take(1)
            mm(p, blk1, sq, True, True, [bf("cst"), bf("a_t1")], [pb])
            yield
            ts("dve", rs, p, 1e-24, None, ALU.max, None, [pb], [bf("a_bT")])
            give(ip)
            act(rs, rs, AF.Sqrt, [bf("a_bT")], [bf("a_bT")])
            yield
            recip(rs, rs, [bf("a_bT")], [bf("a_bT")])
            stt("dve", kk, kT, pv("k_k", ct), rs, ALU.mult, ALU.mult, [bf("a_kT"), bf("pvec"), bf("a_bT")], [bf("a_kk")])
            act(rrk, rT16, AF.Copy, [bf("a_rT"), bf("pvec")], [bf("a_rrk")], scale=pv("r_k", ct))
            p, pb, ib = yield from takeB()
            for t in range(4):
                tr(p[:, t * 128:(t + 1) * 128], vT16[:, t * 128:(t + 1) * 128], ident16, [bf("a_vT"), bf("ident16")], [pb])
            yield
            cp("act", Vt, p[:, 0:512].rearrange("p (a b) -> p a b", b=128), [pb], [Vtb])
            freeB.append(ib)
            yield

        def prep_first(blk, ct, d):
            c0 = 0 if blk == "A" else 512
            sto = STO
            ds = slice(d * 64, d * 64 + 64)
            tpd = (64, 0) if d else None
            ((p, pb, ip),) = yield from take(1)
            for t in range(4):
                mm(p[:, t * 128:(t + 1) * 128], hTd[ds, c0 + t * 128:c0 + (t + 1) * 128], w2dec[ds, ct * 128:(ct + 1) * 128],
                   True, False, [bf("hTd"), bf("w2dec")], [pb], tp=tpd)
                mm(p[:, t * 128:(t + 1) * 128], ones32[0:1, :], w0row[0:1, d * 512 + ct * 128:d * 512 + (ct + 1) * 128],
                   False, True, [bf("ones32"), bf("w0row")], [pb])
            yield
            act(sg32, p.rearrange("p (a b) -> p a b", b=128), AF.Tanh, [pb], [bf("a_sg32")], scale=0.5)
            give(ip)
            yield
            ts("dve", sg32, sg32, 0.5, 0.5, ALU.mult, ALU.add, [bf("a_sg32")], [bf("a_sg32")])
            yield
            (pI, pIb, iI), (pX, pXb, iX), (pa_, pab_, ia_) = yield from take(3)
            for t in range(4):
                mm(pI[:, t * 128:(t + 1) * 128], sg32[:, t, :], cst[:, 1 + 2 * d, :], True, True, [bf("a_sg32"), bf("cst")], [pIb])
                mm(pX[:, t * 128:(t + 1) * 128], sg32[:, t, :], cst[:, 2 + 2 * d, :], True, True, [bf("a_sg32"), bf("cst")], [pXb])
            mm(pa_, a2cat[ds, ct * 128:(ct + 1) * 128], hTi[ds, c0:c0 + 512], True, True, [bf("a2cat"), bf("hTi")], [pab_], tp=tpd)
            yield
            act(Ei, pI, AF.Exp, [pIb], [bf("a_Ei")])
            act(En, pI, AF.Exp, [pIb], [bf("a_En")], scale=-1.0)
            act(Ex, pX, AF.Exp, [pXb], [bf("a_Ex")])
            act(aT, pa_, AF.Tanh, [pab_, bf("a0h")], [bf("a_aT")], bias=a0h[:, d * 4 + ct:d * 4 + ct + 1], scale=0.5)
            give(iI, iX, ia_)
            yield
            lastc = 127 if d == 0 else 0
            cp("dve", sto["Gam"][:, ct, :, d], Ei.rearrange("p (t c) -> p t c", c=128)[:, :, lastc], [bf("a_Ei")], [bf("a_Gam")])
            ts("dve", t1, aT, -1.0, kah[:, ct:ct + 1], ALU.add, ALU.mult, [bf("a_aT"), bf("kah")], [bf("a_t1")])
            yield
            stt("dve", kd[d], t1, 1.0, kT, ALU.add, ALU.mult, [bf("a_t1"), bf("a_kT")], [bf("a_kd%d" % d)])
            stt("dve", bT, aT, 1.0, kk, ALU.add, ALU.mult, [bf("a_kk"), bf("a_aT")], [bf("a_bT")])
            yield

        def prep_second(blk, ct, d):
            v4 = lambda a_: a_.rearrange("p (t c) -> p t c", c=128)
            stt("dve", AR16[:, :, 0:128], v4(kk), -1.0, v4(Ex), ALU.mult, ALU.mult, [bf("a_kk"), bf("a_Ex")], [bf("a_AR16")])
            tt("dve", AR16[:, :, 128:256], v4(rT16), v4(Ei), ALU.mult, [bf("a_rT"), bf("a_Ei")], [bf("a_AR16")])
            stt("dve", BT16, bT, 0.5, En, ALU.mult, ALU.mult, [bf("a_bT"), bf("a_En")], [bf("a_BT16")])
            for e_ in range(2):
                act(AR16m[e_], AR16, AF.Copy, [bf("a_AR16"), bf("hsel")], [bf("a_AR16m")], scale=hsel[:, e_:e_ + 1])
            tt("dve", KT16, kd[d], En, ALU.mult, [bf("a_kd%d" % d), bf("a_En")], [bf("a_KT16")])

        def prep_late(blk, ct, d):
            p, pb, ib = yield from takeB()
            for t in range(4):
                tr(p[:, t * 128:(t + 1) * 128], AR16[:, t, 0:128], ident16, [bf("a_AR16"), bf("ident16")], [pb])
            yield
            for e_ in range(2):
                cp("act", AZ[:, :, e_, 0:64], p[:, 0:512].rearrange("p (t c) -> p t c", c=128)[:, :, e_ * 64:(e_ + 1) * 64], [pb], [bf("a_AZ")])
            freeB.append(ib)
            yield
            p, pb, ib = yield from takeB()
            for t in range(4):
                tr(p[:, t * 128:(t + 1) * 128], BT16[:, t * 128:(t + 1) * 128], ident16, [bf("a_BT16"), bf("ident16")], [pb])
            yield
            cp("dve", Btok, p[:, 0:512].rearrange("p (a b) -> p a b", b=128), [pb], [bf("a_Btok")])
            freeB.append(ib)
            yield
            p, pb, ib = yield from takeB()
            for t in range(4):
                tr(p[:, t * 128:(t + 1) * 128], KT16[:, t * 128:(t + 1) * 128], ident16, [bf("a_KT16"), bf("ident16")], [pb])
            yield
            cp("act", Ktok, p[:, 0:512].rearrange("p (a b) -> p a b", b=128), [pb], [bf("a_Ktok")])
            freeB.append(ib)
            yield

        def bonus_ct(ct):
            sto = STO
            tt("dve", t1, kd[0], kd[1], ALU.add, [bf("a_kd0"), bf("a_kd1")], [bf("a_t1")])
            tt("dve", t1, t1, rrk, ALU.mult, [bf("a_t1"), bf("a_rrk")], [bf("a_t1")])
            ((p, pb, ip),) = take_now(1)
            mm(p, blk1, t1, True, True, [bf("cst"), bf("a_t1")], [pb])
            tt("dve", sto["bonus"][:, ct, :], p, vT16, ALU.mult, [pb, bf("a_vT")], [bf("a_bonus")])
            give(ip)

        def first_gen(blk, ct, d):
            if d == 0:
                yield from stage_ct(blk, ct)
            yield from prep_first(blk, ct, d)

        def rwkv_block(blk):
            seq = [(ct, d) for ct in range(4) for d in range(2)]
            run_gens([first_gen(blk, 0, 0)])
            for i_, (ct, d) in enumerate(seq):
                prep_second(blk, ct, d)
                if d == 1:
                    bonus_ct(ct)
                gens = [prep_late(blk, ct, d)] + [group_gen(blk, ct, d, t_, j_) for j_, t_ in enumerate(range(4))]
                if i_ + 1 < len(seq):
                    gens.append(first_gen(blk, *seq[i_ + 1]))
                run_gens(gens)

        def recur_gen(blk, tiles, d, init, useG=True, saveH=True, hs=0, after=None):
            sto = STO
            Hst, Hs16, Htmp = HS[hs]["Hst"], HS[hs]["Hs16"], HS[hs]["Htmp"]
            HB, H16B, HTB = bf("a_Hst%d" % hs), bf("a_Hs16%d" % hs), bf("a_Htmp%d" % hs)
            if init is None:
                memset("dve", Hst, 0.0, [HB])
            else:
                cp("dve", Hst, init[0], [init[1]], [HB])
            for t in tiles:
                cp("act", Hs16, Hst, [HB], [H16B])
                if saveH:
                    cp("act", sto["H0"][:, :, t, d * 64:(d + 1) * 64], Hst, [HB], [bf("a_H0_%d_%d" % (t, d))])
                (p0, p0b, i0), (p1, p1b, i1) = yield from take(2)
                pe2 = [(p0, p0b), (p1, p1b)]
                for ct in range(4):
                    for e in range(2):
                        es = slice(e * 64, e * 64 + 64)
                        mm(pe2[e][0][es, ct * 64:(ct + 1) * 64], sto["PT"][es, ct, t, d * 64:(d + 1) * 64], Hs16[es, ct, :], True, True,
                           [bf("a_PT"), H16B], [pe2[e][1]], tp=(64, 64) if e else None)
                yield
                for e in range(2):
                    es = slice(e * 64, e * 64 + 64)
                    tt("dve", Htmp[es], pe2[e][0][es, 0:256].rearrange("p (a b) -> p a b", b=64), Hst[es], ALU.add, [pe2[e][1], HB], [HTB])
                give(i0, i1)
                yield
                for ct in range(4):
                    if useG:
                        stt("dve", Hst[:, ct, :], Htmp[:, ct, :], sto["Gam"][:, ct, t, d:d + 1], sto["G"][:, ct, t, d * 64:(d + 1) * 64],
                            ALU.mult, ALU.add, [HTB, bf("a_Gam"), bf("a_G")], [HB])
                    else:
                        ts("dve", Hst[:, ct, :], Htmp[:, ct, :], sto["Gam"][:, ct, t, d:d + 1], None, ALU.mult, None,
                           [HTB, bf("a_Gam")], [HB])
                yield
            if after is not None:
                yield from after(Hst, HB)

        def rwkv_out_gen(blk, t, k):
            sto = STO
            c0 = 0 if blk == "A" else 512
            T_ = OT[k]
            ytok, ysq, yn16, gst, ynT = T_["ytok"], T_["ysq"], T_["yn16"], T_["gst"], T_["ynT"]
            n = lambda s_: bf("a_o%s_%d" % (s_, k))
            (p0, p0b, i0), (p1, p1b, i1) = yield from take(2)
            py2 = [(p0, p0b), (p1, p1b)]
            for h in range(8):
                ct, e = h // 2, h % 2
                es = slice(e * 64, e * 64 + 64)
                for d in range(2):
                    mm(py2[e][0][:, ct * 64:(ct + 1) * 64], sto["QT"][es, ct, t, d * 128:(d + 1) * 128], sto["H0"][es, ct, t, d * 64:(d + 1) * 64],
                       d == 0, d == 1, [bf("a_QT"), bf("a_H0_%d_%d" % (t, d))], [py2[e][1]], tp=(64, 0) if e else None)
            yield
            yt4 = ytok.rearrange("p (c e v) -> p c e v", e=2, v=64)
            y04 = sto["Y0"][:, t, :].rearrange("p (c e v) -> p c e v", e=2, v=64)
            for e in range(2):
                tt("dve", yt4[:, :, e, :], py2[e][0][:, 0:256].rearrange("p (c v) -> p c v", v=64), y04[:, :, e, :], ALU.add,
                   [py2[e][1], bf("a_Y0_%d" % t)], [n("ytok")])
            give(i0, i1)
            yield
            y3 = ytok.rearrange("p (h v) -> p h v", v=64)
            S.op("dve", lambda e_, y3=y3, gst=gst: e_.reduce_sum(gst[:, 0:8], y3, AX.X), reads=[n("ytok")], writes=[n("gst")])
            act(ysq, ytok, AF.Square, [n("ytok")], [n("ysq")])
            yield
            S.op("dve", lambda e_, ysq=ysq, gst=gst: e_.reduce_sum(gst[:, 8:16], ysq.rearrange("p (h v) -> p h v", v=64), AX.X), reads=[n("ysq")], writes=[n("gst")])
            ts("dve", gst[:, 16:24], gst[:, 0:8], 1.0 / 64, None, ALU.mult, None, [n("gst")], [n("gst")])
            tt("dve", gst[:, 24:32], gst[:, 16:24], gst[:, 16:24], ALU.mult, [n("gst")], [n("gst")])
            stt("dve", gst[:, 32:40], gst[:, 8:16], 1.0 / 64, gst[:, 24:32], ALU.mult, ALU.subtract, [n("gst")], [n("gst")])
            yield
            act(gst[:, 40:48], gst[:, 32:40], AF.Sqrt, [n("gst"), bf("epsr")], [n("gst")], bias=epsr[:, 1:2])
            yield
            recip(gst[:, 40:48], gst[:, 40:48], [n("gst")], [n("gst")])
            mb = gst[:, 16:24].unsqueeze(2).to_broadcast([128, 8, 64])
            rb_ = gst[:, 40:48].unsqueeze(2).to_broadcast([128, 8, 64])
            tt("dve", y3, y3, mb, ALU.subtract, [n("ytok"), n("gst")], [n("ytok")])
            tt("dve", yn16.rearrange("p (h v) -> p h v", v=64), y3, rb_, ALU.mult, [n("ytok"), n("gst")], [n("yn16")])
            yield
            pT, pTb, iT = yield from takeB()
            ((pg_, pgb_, ig),) = yield from take(1)
            for ct in range(4):
                tr(pT[:, ct * 128:(ct + 1) * 128], yn16[:, ct * 128:(ct + 1) * 128], ident16, [n("yn16"), bf("ident16")], [pTb])
            for ct in range(4):
                mm(pg_[:, ct * 128:(ct + 1) * 128], g2[:, ct * 128:(ct + 1) * 128], hTg[:, c0 + t * 128:c0 + (t + 1) * 128], True, True,
                   [bf("g2"), bf("hTg")], [pgb_])
            yield
            for ct in range(4):
                act(ynT[:, ct, :], pT[:, ct * 128:(ct + 1) * 128], AF.Identity, [pTb, bf("pvec")], [n("ysq")],
                    bias=pv("lnx_b", ct), scale=pv("lnx_g", ct))
            freeB.append(iT)
            yield
            tt("dve", ynT, ynT, sto["bonus"][:, :, t * 128:(t + 1) * 128], ALU.add, [n("ysq"), bf("a_bonus")], [n("ysq")])
            tt("dve", outT16[:, :, c0 + t * 128:c0 + (t + 1) * 128], ynT, pg_.rearrange("p (c t) -> p c t", t=128), ALU.mult,
               [n("ysq"), pgb_], [bf("outT16")])
            give(ig)
            yield

        def rwkv_out(blk, tiles):
            inherit(OT_NAMES, GB01_NAMES)
            run_gens([rwkv_out_gen(blk, t, k) for k, t in enumerate(tiles)])

        rwkv_block("A")
        chk("blockA")
        dump("PT", STO["PT"].rearrange("p a b c -> p (a b c)"), [bf("a_PT")]); dump("G", STO["G"].rearrange("p a b c -> p (a b c)"), [bf("a_G")])
        dump("QT", STO["QT"].rearrange("p a b c -> p (a b c)"), [bf("a_QT")]); dump("Y0", STO["Y0"].rearrange("p a b -> p (a b)"), [bf("a_Y0")]); dump("Gam", STO["Gam"].rearrange("p a b c -> p (a b c)"), [bf("a_Gam")])
        stv = st_d.rearrange("s d p c v -> s d p (c v)")
        inherit(HS_NAMES, GB3_NAMES)

        def out_state(seg, d):
            def f(Hst_, HB_):
                S.dma("sp", stv[seg, d], Hst_.rearrange("p c v -> p (c v)"), reads=[HB_], is_output=True)
                return
                yield
            return f
        run_gens([recur_gen("A", ([2 * seg, 2 * seg + 1] if d == 0 else [2 * seg + 1, 2 * seg]), d, None, hs=seg * 2 + d, after=out_state(seg, d))
                  for seg in range(2) for d in range(2)])
        chk("recurA")
        rwkv_out("A", range(4))
        chk("outA")
        inherit(GB3_NAMES, HS_NAMES)
        inherit(GB01_NAMES, OT_NAMES)
        rwkv_block("B")
        inherit(HS_NAMES, GB3_NAMES)
        wa, wab = loadw(w_in_v[:, :, 1536:2048], lambda w: w.rearrange("p (kc n) -> p kc n", kc=8), slot=1)
        wg, wgb = loadw(w_in_v[:, :, 2048:2560], lambda w: w.rearrange("p (kc n) -> p kc n", kc=8), slot=2)
        save_ptr = AR.ptr
        AR.ptr = alias_ptr
        XS = AR.alloc([128, 2, 2, 4, 64])
        G4 = AR.alloc([128, 4, 1024])
        ctmp = AR.alloc([128, 4, 64])
        assert AR.ptr <= alias_ptr + 5888
        AR.ptr = save_ptr
        retired = [bf(n) for n in ("a_rrk", "a_kk", "a_Vtok0", "a_Vtok1", "a_sg32", "a_Ei", "a_Ex", "a_En", "a_aT", "a_t1", "a_bT", "a_kd0", "a_kd1")]
        def after_N(d):
            def f(Hst_, HB_):
                cp("dve", XS[:, d, 1], Hst_, [HB_], [bf("a_XS")] + retired)
                return
                yield
            return f

        def after_M(d):
            def f(Hst_, HB_):
                (p0, p0b, i0), (p1, p1b, i1) = yield from take(2)
                pe2 = [(p0, p0b), (p1, p1b)]
                for ct in range(4):
                    for e in range(2):
                        es = slice(e * 64, e * 64 + 64)
                        mm(pe2[e][0][es, ct * 64:(ct + 1) * 64], Hst_[es, ct, :], cst[es, 0, e * 64:(e + 1) * 64], True, True,
                           [HB_, bf("cst")], [pe2[e][1]], tp=(64, 64) if e else None)
                yield
                for e in range(2):
                    es = slice(e * 64, e * 64 + 64)
                    cp("dve", XS[es, d, 0], pe2[e][0][es, 0:256].rearrange("p (a b) -> p a b", b=64), [pe2[e][1]], [bf("a_XS")] + retired)
                give(i0, i1)
            return f
        gl = []
        for d in range(2):
            tiles = [0, 1, 2, 3] if d == 0 else [3, 2, 1, 0]
            gl.append(recur_gen("B", tiles, d, None, useG=True, saveH=False, hs=2 * d, after=after_N(d)))
            gl.append(recur_gen("B", tiles, d, (idh, bf("idh")), useG=False, saveH=False, hs=2 * d + 1, after=after_M(d)))
        run_gens(gl)
        S.dma("pool", bounce_d, XS.rearrange("p a b c d -> p (a b c d)"), reads=[bf("a_XS")], writes=[bf("bounce")])
        S.coll(lambda en: en.collective_compute("AllGather", ALU.bypass, replica_groups=[[0, 1, 2, 3], [4, 5, 6, 7]],
                                                ins=[bounce_d.opt()], outs=[gath_d.opt()]),
               reads=[bf("bounce")], writes=[bf("gath")])
        S.dma("pool", G4, gath_d.rearrange("(r p) n -> p r n", p=128), reads=[bf("gath")], writes=[bf("a_G4")])
        G4v = G4.rearrange("p r (d m c v) -> p r d m c v", d=2, m=2, c=4)
        ctmps = [ctmp, HS[3]["Htmp"]]

        def compose_gen(d):
            HB = bf("Hin%d" % d)
            ct_, ctb_ = ctmps[d], bf("a_ctmp%d" % d)
            order = [0, 1, 2] if d == 0 else [3, 2, 1]
            for j in order:
                (p0, p0b, i0), (p1, p1b, i1) = yield from take(2)
                pe2 = [(p0, p0b), (p1, p1b)]
                for ct in range(4):
                    for e in range(2):
                        es = slice(e * 64, e * 64 + 64)
                        mm(pe2[e][0][es, ct * 64:(ct + 1) * 64], G4v[es, j, d, 0, ct, :], Hin[es, d, ct, :], True, True,
                           [bf("a_G4"), HB], [pe2[e][1]], tp=(64, 64) if e else None)
                yield
                for e in range(2):
                    es = slice(e * 64, e * 64 + 64)
                    tt("dve", ct_[es], pe2[e][0][es, 0:256].rearrange("p (a b) -> p a b", b=64), G4v[es, j, d, 1], ALU.add,
                       [pe2[e][1], bf("a_G4")], [ctb_])
                give(i0, i1)
                yield
                tt("dve", ct_, ct_, Hin[:, d], ALU.subtract, [ctb_, HB], [ctb_])
                stt("dve", Hin[:, d], ct_, selv[:, d * 4 + j:d * 4 + j + 1], Hin[:, d], ALU.mult, ALU.add, [ctb_, bf("selv"), HB], [HB])
                yield
        bf("a_ctmp1").r = list(bf("a_ctmp1").r) + list(bf("a_Htmp3").r) + ([bf("a_Htmp3").w] if bf("a_Htmp3").w is not None else [])
        run_gens([compose_gen(0), compose_gen(1)])
        bf("a_Htmp3").r = list(bf("a_Htmp3").r) + list(bf("a_ctmp1").r) + ([bf("a_ctmp1").w] if bf("a_ctmp1").w is not None else [])
        run_gens([recur_gen("B", ([0, 1, 2, 3] if d == 0 else [3, 2, 1, 0]), d, (Hin[:, d], bf("Hin%d" % d)), hs=d) for d in range(2)])
        rwkv_out("B", range(4))
        dump("outT", outT16.rearrange("p c n -> p (c n)"), [bf("outT16")])
        chk("rwkv")

        new_phase()
        x_sb = AR.alloc([128, 8, 1024])
        for t in range(8):
            S.dma("sp", x_sb[:, t, :], xm[t * 128:(t + 1) * 128, :], writes=[bf("xt%d" % t)])
        cv_ptr = AR.ptr
        cv = AR.alloc([128, 4, 1024])
        upA = [AR.alloc([128, 2, 286], BF16) for _ in range(2)]
        upB = [AR.alloc([128, 8, 94], BF16) for _ in range(2)]
        dgw_ptr = AR.ptr
        dgw = [AR.alloc([128, 31, 128], BF16) for _ in range(2)]
        ucT = AR.alloc([128, 4, 1024], BF16)
        mergedT = AR.alloc([128, 8, 1024], BF16)
        tmpa = [AR.alloc([128, 512]) for _ in range(2)]
        tmpb = [AR.alloc([128, 512]) for _ in range(2)]
        lnm = AR.alloc([128, 512]); lnr = AR.alloc([128, 512])
        g1rep = [AR.alloc([128, 1024]) for _ in range(2)]
        dg = AR.alloc([128, 128])
        for i in range(2):
            memset("dve", upA[i], 0.0, [bf("a_upA%d" % i)])
            memset("dve", upB[i], 0.0, [bf("a_upB%d" % i)])
        def glu_proj(ct, half):
            (pa, pab), (pg, pgb) = getF(), getF()
            for kc in range(8):
                mm(pa, wa[:, kc, ct * 128:(ct + 1) * 128], xnT[:, kc, half * 512:(half + 1) * 512], kc == 0, kc == 7, [wab, XN[half]], [pab])
            for kc in range(8):
                mm(pg, wg[:, kc, ct * 128:(ct + 1) * 128], xnT[:, kc, half * 512:(half + 1) * 512], kc == 0, kc == 7, [wgb, XN[half]], [pgb])
            sgt = tmpa[half]
            act(sgt, pg, AF.Sigmoid, [pgb], [bf("a_tmpa%d" % half)])
            if half == 0:
                up, upn, L = upA[ct % 2], "a_upA%d" % (ct % 2), 256
            else:
                up, upn, L = upB[ct % 2], "a_upB%d" % (ct % 2), 64
            v3 = lambda a_, L=L: a_.rearrange("p (r c) -> p r c", c=L)
            tt("dve", up[:, :, 15:15 + L], v3(pa), v3(sgt), ALU.mult, [pab, bf("a_tmpa%d" % half)], [bf(upn)])
            if half == 0:
                dw, dwb = dgw[ct % 2], bf("a_dgw%d" % (ct % 2))
                for j in range(31):
                    if j % 2:
                        act(dw[:, j, :], ident16, AF.Copy, [bf("ident16"), bf("pvec")], [dwb], scale=pv("conv_w", j * 4 + ct))
                    else:
                        ts("dve", dw[:, j, :], ident16, pv("conv_w", j * 4 + ct), None, ALU.mult, None, [bf("ident16"), bf("pvec")], [dwb])

        def conv_mm(ct, half):
            if half == 0:
                up, upn, L = upA[ct % 2], "a_upA%d" % (ct % 2), 256
            else:
                up, upn, L = upB[ct % 2], "a_upB%d" % (ct % 2), 64
            dw, dwb = dgw[ct % 2], bf("a_dgw%d" % (ct % 2))
            pcv, pcvb = getF()
            for j in range(31):
                mm(pcv, dw[:, j, :], up[:, :, j:j + L], j == 0, j == 30, [dwb, bf(upn)], [pcvb])
            act(cv[:, ct, half * 512:(half + 1) * 512], pcv, AF.Identity, [pcvb, bf("pvec")], [bf("a_cv%d_%d" % (ct, half))], bias=pv("conv_b", ct))

        seq_c = [(ct, half) for ct in range(4) for half in range(2)]
        glu_proj(*seq_c[0])
        for i_, ch_ in enumerate(seq_c):
            if i_ + 1 < len(seq_c):
                glu_proj(*seq_c[i_ + 1])
            conv_mm(*ch_)
        for half in range(2):
            hs = slice(half * 512, (half + 1) * 512)
            (pm, pmb), (pq, pqb) = getF(), getF()
            for ct in range(4):
                mm(pm, cst[:, 6, :], cv[:, ct, hs], ct == 0, ct == 3, [bf("cst"), bf("a_cv%d_%d" % (ct, half))], [pmb])
            for ct in range(4):
                sqt = tmpb[ct % 2]
                act(sqt, cv[:, ct, hs], AF.Square, [bf("a_cv%d_%d" % (ct, half))], [bf("a_tmpb%d" % (ct % 2))])
                mm(pq, cst[:, 6, :], sqt, ct == 0, ct == 3, [bf("cst"), bf("a_tmpb%d" % (ct % 2))], [pqb])
            cp("act", lnm, pm, [pmb], [bf("a_lnm")])
            tt("dve", lnr, lnm, lnm, ALU.mult, [bf("a_lnm")], [bf("a_lnr")])
            tt("dve", lnr, pq, lnr, ALU.subtract, [pqb, bf("a_lnr")], [bf("a_lnr")])
            act(lnr, lnr, AF.Sqrt, [bf("a_lnr"), bf("epsr")], [bf("a_lnr")], bias=epsr[:, 2:3])
            recip(lnr, lnr, [bf("a_lnr")], [bf("a_lnr")])
            for ct in range(4):
                tq = tmpb[ct % 2]; tqb = bf("a_tmpb%d" % (ct % 2))
                tt("dve", tq, cv[:, ct, hs], lnm, ALU.subtract, [bf("a_cv%d_%d" % (ct, half)), bf("a_lnm")], [tqb])
                tt("dve", tq, tq, lnr, ALU.mult, [tqb, bf("a_lnr")], [tqb])
                act(ucT[:, ct, hs], tq, AF.Silu, [tqb, bf("pvec")], [bf("a_ucT")], bias=pv("cln_b", ct), scale=pv("cln_g", ct))
        dump("ucT", ucT.rearrange("p c n -> p (c n)"), [bf("a_ucT")])
        wr, wrb = loadw(wor_d.rearrange("(kc p) n -> p kc n", p=128), lambda w: w.rearrange("p (kc n) -> p kc n", kc=4), slot=3)
        wc, wcb = loadw(woc_d.rearrange("(kc p) n -> p kc n", p=128), lambda w: w.rearrange("p (kc n) -> p kc n", kc=4), slot=0)
        for g in range(2):
            wgr, wgrb = loadw(w_in_v[:, :, 2560 + g * 512:2560 + (g + 1) * 512], lambda w: w.rearrange("p (kc n) -> p kc n", kc=8), slot=1)
            wgc, wgcb = loadw(w_in_v[:, :, 3584 + g * 512:3584 + (g + 1) * 512], lambda w: w.rearrange("p (kc n) -> p kc n", kc=8), slot=2)
            for f4 in range(4):
                fo = g * 4 + f4
                for half in range(2):
                    hs = slice(half * 512, (half + 1) * 512)
                    (pr, prb), (pc, pcb), (pgr, pgrb), (pgc, pgcb) = getF(), getF(), getF(), getF()
                    for kc in range(4):
                        mm(pr, wr[:, kc, fo * 128:(fo + 1) * 128], outT16[:, kc, hs], kc == 0, kc == 3, [wrb, bf("outT16")], [prb])
                    for kc in range(4):
                        mm(pc, wc[:, kc, fo * 128:(fo + 1) * 128], ucT[:, kc, hs], kc == 0, kc == 3, [wcb, bf("a_ucT")], [pcb])
                    for kc in range(8):
                        mm(pgr, wgr[:, kc, f4 * 128:(f4 + 1) * 128], xnT[:, kc, hs], kc == 0, kc == 7, [wgrb, XN[half]], [pgrb])
                    for kc in range(8):
                        mm(pgc, wgc[:, kc, f4 * 128:(f4 + 1) * 128], xnT[:, kc, hs], kc == 0, kc == 7, [wgcb, XN[half]], [pgcb])
                    ta, tab, tb_, tbb = tmpa[half], bf("a_tmpa%d" % half), tmpb[half], bf("a_tmpb%d" % half)
                    act(ta, pgr, AF.Sigmoid, [pgrb], [tab])
                    act(tb_, pgc, AF.Sigmoid, [pgcb], [tbb])
                    tt("dve", ta, pr, ta, ALU.mult, [prb, tab], [tab])
                    tt("dve", tb_, pc, tb_, ALU.mult, [pcb, tbb], [tbb])
                    tt("dve", mergedT[:, fo, hs], ta, tb_, ALU.add, [tab, tbb], [bf("a_mergedT")])

        def bcast_rows(dst_list, col0, tag):
            for j in range(2):
                for hh in range(2):
                    p, pb = getF()
                    for k4 in range(4):
                        kc = hh * 4 + k4
                        ts("dve", dg, ident32, mod[:, col0 + kc, j:j + 1], None, ALU.mult, None, [bf("cst"), bf("mod")], [bf("a_dg")])
                        mm(p[:, k4 * 128:(k4 + 1) * 128], cst[:, 7, :], dg, True, True, [bf("cst"), bf("a_dg")], [pb])
                    cp("act", dst_list[j][:, hh * 512:(hh + 1) * 512], p, [pb], [bf("a_%s%d" % (tag, j))])

        bcast_rows(g1rep, 16, "g1rep")
        wo_v = wo_d.rearrange("(kc p) n -> p kc n", p=128)
        sp3_ = AR.ptr
        AR.ptr = cv_ptr
        xs16c = AR.alloc([128, 8, 1024], BF16)
        AR.ptr = dgw_ptr
        junkc = AR.alloc([128, 1024])
        AR.ptr = sp3_
        inherit(["a_xs16c_%d" % t_ for t_ in range(8)] + ["a_junkc"],
                ["a_cv%d_%d" % (c_, h_) for c_ in range(4) for h_ in range(2)] + ["a_dgw0", "a_dgw1"])
        wos = [loadw(wo_v[:, :, nh * 512:(nh + 1) * 512], lambda w: w.rearrange("p (kc n) -> p kc n", kc=8), slot=3 * nh) for nh in range(2)]
        for t in range(8):
            j = 0 if t < 4 else 1
            for nh in range(2):
                ns = slice(nh * 512, (nh + 1) * 512)
                wo, wob = wos[nh]
                p, pb = getF()
                for kc in range(8):
                    mm(p, mergedT[:, kc, t * 128:(t + 1) * 128], wo[:, kc, :], kc == 0, kc == 7, [bf("a_mergedT"), wob], [pb])
                ta, tab = tmpa[nh], bf("a_tmpa%d" % nh)
                tt("dve", ta, p, g1rep[j][:, ns], ALU.mult, [pb, bf("a_g1rep%d" % j)], [tab])
                tt("dve", x_sb[:, t, ns], ta, x_sb[:, t, ns], ALU.add, [tab, bf("xt%d" % t)], [bf("xt%d" % t)])
            sb_ = bf("a_ssn%d" % t)
            memset("dve", ss[:, t:t + 1], 0.0, [sb_])
            act(junkc, x_sb[:, t, :], AF.Square, [bf("xt%d" % t)], [bf("a_junkc"), sb_], accum=ss[:, t:t + 1])
            act(rstd[:, t:t + 1], ss[:, t:t + 1], AF.Sqrt, [sb_, bf("epsr")], [sb_], bias=epsr[:, 0:1], scale=1.0 / 1024)
            recip(rstd[:, t:t + 1], rstd[:, t:t + 1], [sb_], [sb_])
            if t % 2 == 0:
                ts("dve", xs16c[:, t, :], x_sb[:, t, :], rstd[:, t:t + 1], None, ALU.mult, None, [bf("xt%d" % t), sb_], [bf("a_xs16c_%d" % t)])
            else:
                act(xs16c[:, t, :], x_sb[:, t, :], AF.Copy, [bf("xt%d" % t), sb_], [bf("a_xs16c_%d" % t)], scale=rstd[:, t:t + 1])
            if t % 4 == 3:
                half = t // 4
                for kc in range(8):
                    p, pb = getB()
                    for q in range(4):
                        t_ = half * 4 + q
                        tr(p[:, q * 128:(q + 1) * 128], xs16c[:, t_, kc * 128:(kc + 1) * 128], ident16, [bf("a_xs16c_%d" % t_), bf("ident16")], [pb])
                    act(xnT[:, kc, half * 512:(half + 1) * 512], p[:, 0:512], AF.Identity, [pb, bf("A2"), bf("mod")],
                        [bf("xnT%d" % half)], bias=mod[:, 24 + kc, half:half + 1], scale=A2[:, kc, half:half + 1])
        chk("phaseC")

        new_phase()
        x_sb = AR.alloc([128, 8, 1024])
        h16T = AR.alloc([128, 32, 1024], BF16)
        g2rep = [AR.alloc([128, 1024]) for _ in range(2)]
        fgrep = AR.alloc([128, 1024])
        rtmp = [AR.alloc([128, 512]) for _ in range(2)]
        dg = AR.alloc([128, 128])
        ytile = [AR.alloc([128, 1024]) for _ in range(2)]
        S.dma("sp", fgrep, fgrep_d, writes=[bf("a_fgrep")])
        bcast_rows(g2rep, 40, "g2rep")
        w1_v = w1_d.rearrange("(kc p) n -> p kc n", p=128)
        k_ = 0
        for s_ in range(8):
            w1s, w1b = loadw(w1_v[:, :, s_ * 512:(s_ + 1) * 512], lambda w: w.rearrange("p (kc n) -> p kc n", kc=8))
            for m in range(4):
                ff = s_ * 4 + m
                for half in range(2):
                    hs = slice(half * 512, (half + 1) * 512)
                    p, pb = getF()
                    for kc in range(8):
                        mm(p, w1s[:, kc, m * 128:(m + 1) * 128], xnT[:, kc, hs], kc == 0, kc == 7, [w1b, XN[0], XN[1]], [pb])
                    rt, rtb = rtmp[k_ % 2], bf("a_rtmp%d" % (k_ % 2))
                    act(rt, p, AF.Relu, [pb], [rtb])
                    tt("dve", h16T[:, ff, hs], rt, rt, ALU.mult, [rtb], [bf("a_h16T%d" % ff)])
                    k_ += 1
        w2_v = w2_d.rearrange("(fc p) n -> p fc n", p=128)
        for nh in range(2):
            ns = slice(nh * 512, (nh + 1) * 512)
            slabs = [loadw(w2_v[:, 8 * q_:8 * q_ + 8, ns], lambda w: w.rearrange("p (fc n) -> p fc n", fc=8), slot=q_) for q_ in range(4)]
            for t in range(8):
                j = 0 if t < 4 else 1
                p, pb = getF()
                for ff in range(32):
                    w2s, w2b = slabs[ff // 8]
                    mm(p, h16T[:, ff, t * 128:(t + 1) * 128], w2s[:, ff % 8, :], ff == 0, ff == 31, [bf("a_h16T%d" % ff), w2b], [pb])
                rt, rtb = rtmp[k_ % 2], bf("a_rtmp%d" % (k_ % 2))
                tt("dve", rt, p, g2rep[j][:, ns], ALU.mult, [pb, bf("a_g2rep%d" % j)], [rtb])
                tt("dve", x_sb[:, t, ns], rt, x_sb[:, t, ns], ALU.add, [rtb, bf("xt%d" % t)], [bf("xt%d" % t)])
                k_ += 1
                if nh == 1:
                    yt, ytb = ytile[t % 2], bf("a_ytile%d" % (t % 2))
                    sb_ = bf("a_ssf%d" % t)
                    memset("dve", ss[:, 8 + t:9 + t], 0.0, [sb_])
                    act(yt, x_sb[:, t, :], AF.Square, [bf("xt%d" % t)], [ytb, sb_], accum=ss[:, 8 + t:9 + t])
                    act(rstd[:, 8 + t:9 + t], ss[:, 8 + t:9 + t], AF.Sqrt, [sb_, bf("epsr")], [sb_], bias=epsr[:, 0:1], scale=1.0 / 1024)
                    recip(rstd[:, 8 + t:9 + t], rstd[:, 8 + t:9 + t], [sb_], [sb_])
                    stt("dve", yt, x_sb[:, t, :], rstd[:, 8 + t:9 + t], fgrep, ALU.mult, ALU.mult, [bf("xt%d" % t), sb_, bf("a_fgrep")], [ytb])
                    S.dma("sp", y_d[t * 128:(t + 1) * 128, :], yt, reads=[ytb], is_output=True)


    try:
        _rest()
    except _Stop:
        pass
    S.emit()
    st.close()
    return nc


def prep_inputs(inp):
    f = lambda k: np.asarray(inp[k], np.float32)
    xp, xs = f("x_prompt"), f("x_sample")
    pv = np.zeros((128, NPV), np.float32)

    def put(name, arr):
        a = _fm(arr)
        pv[:, PV_OFF[name]:PV_OFF[name] + a.shape[1]] = a
    put("ada_b", f("ada_b")[0]); put("n1g", f("norm1_g")[0]); put("n2g", f("norm2_g")[0])
    put("mu_p", f("mu_prev")[0]); put("mu_n", f("mu_next")[0])
    put("a0f", f("iclr_a0")[0, 0]); put("a0b", f("iclr_a0")[0, 1])
    put("k_k", f("k_k")[0]); put("k_a", f("k_a")[0]); put("r_k", f("r_k")[0].reshape(-1))
    put("lnx_g", f("lnx_g")[0]); put("lnx_b", f("lnx_b")[0]); put("conv_b", f("conv_b")[0])
    put("cln_g", f("conv_ln_g")[0]); put("cln_b", f("conv_ln_b")[0])
    cw = f("conv_w")[0]
    cwp = np.concatenate([_fm(cw[j]) for j in range(31)], axis=1)
    pv[:, PV_OFF["conv_w"]:PV_OFF["conv_w"] + 124] = cwp
    shared = dict(
        pvec=pv,
        w0row=np.ascontiguousarray(f("decay_w0")[0].reshape(1, 1024)),
        fgrep=np.ascontiguousarray(np.broadcast_to(f("final_g")[None, :], (128, 1024))),
        ident=np.eye(128, dtype=np.float32),
        w1cat=np.ascontiguousarray(np.concatenate([f("decay_w1")[0, 0], f("decay_w1")[0, 1], f("iclr_a1")[0, 0],
                                                   f("iclr_a1")[0, 1], f("gate_g1")[0]], axis=1)),
        w2dec=np.ascontiguousarray(np.concatenate([f("decay_w2")[0, 0], f("decay_w2")[0, 1]], axis=0)),
        a2cat=np.ascontiguousarray(np.concatenate([f("iclr_a2")[0, 0], f("iclr_a2")[0, 1]], axis=0)),
        g2=f("gate_g2")[0],
        w_in=f("w_in")[0],
        w_out_rwkv=f("w_out_rwkv")[0], w_out_conv=f("w_out_conv")[0], w_o=f("w_o")[0],
        mlp_w1=f("mlp_w1")[0], mlp_w2=f("mlp_w2")[0],
    )
    cst, msk, id4, mk = _consts()
    shared.update(cst=cst, msk=msk, id4=id4, mk=mk)
    shared.pop("ident")
    in_maps = []
    for c in range(NCORES):
        b, q = c // 4, c % 4
        xmc = np.concatenate([xp[2 * c], xp[2 * c + 1], xs[b, q * 512:(q + 1) * 512]], axis=0)
        xhc = np.zeros((2, 1024), np.float32)
        hmk = np.zeros((128, 8, 2), np.float32)
        if q > 0:
            xhc[0] = xs[b, q * 512 - 1]; hmk[:, :, 0] = 1.0
        if q < 3:
            xhc[1] = xs[b, (q + 1) * 512]; hmk[:, :, 1] = 1.0
        cond = np.stack([f("c_ctx"), f("c")[0], f("c")[1]], axis=1)
        cT = np.ascontiguousarray(cond.reshape(8, 128, 3).transpose(1, 0, 2).reshape(128, 24))
        m_ada = dict(ada_w=np.ascontiguousarray(f("ada_w")[0][:, q * 1536:(q + 1) * 1536]),
                     adab=_fm(f("ada_b")[0][q * 1536:(q + 1) * 1536]),
                     selb=np.ascontiguousarray(np.broadcast_to(np.array([1.0 - b, float(b)], np.float32)[None, :], (128, 2))))
        s0T = np.stack([np.ascontiguousarray(
            f(nm)[b, 0].transpose(0, 2, 1).reshape(4, 2, 64, 64).transpose(1, 2, 0, 3).reshape(128, 4, 64))
            for nm in ("state_fwd", "state_bwd")], axis=0)
        m = dict(shared)
        m.update(m_ada)
        m["s0T"] = s0T
        sel = np.zeros((128, 8), np.float32)
        for j in range(4):
            sel[:, j] = 1.0 if j < q else 0.0
            sel[:, 4 + j] = 1.0 if j > q else 0.0
        m["sel"] = sel
        m["idh"] = np.ascontiguousarray(np.tile(np.eye(64, dtype=np.float32)[:, None, :], (2, 4, 1)).reshape(128, 256))
        m.update(xm=np.ascontiguousarray(xmc), xh=xhc, hmask=hmk.reshape(128, 16), condT=cT)
        in_maps.append(m)
    return in_maps


def kernel(**inputs):
    in_maps = prep_inputs(inputs)
    nc = build()
    res = run_bass_kernel_spmd(nc, in_maps, core_ids=list(range(NCORES)))
    y_prompt = np.zeros((16, 256, 1024), np.float32)
    y_sample = np.zeros((2, 2048, 1024), np.float32)
    nsf = np.zeros((16, 1, 8, 64, 64), np.float32)
    nsb = np.zeros((16, 1, 8, 64, 64), np.float32)
    for c, r in enumerate(res.results):
        b, q = c // 4, c % 4
        y = np.asarray(r["y"], np.float32)
        y_prompt[2 * c] = y[0:256]
        y_prompt[2 * c + 1] = y[256:512]
        y_sample[b, q * 512:(q + 1) * 512] = y[512:1024]
        stt_ = np.asarray(r["st"], np.float32).reshape(2, 2, 2, 64, 4, 64).transpose(0, 1, 4, 2, 5, 3).reshape(2, 2, 8, 64, 64)
        nsf[2 * c:2 * c + 2, 0] = stt_[:, 0]
        nsb[2 * c:2 * c + 2, 0] = stt_[:, 1]
    return (y_prompt, y_sample, nsf, nsb)
```

```python
import contextlib
import os
import numpy as np
import concourse.bass as bass
import concourse.mybir as mybir
from concourse.bass_utils import run_bass_kernel_spmd

F32 = mybir.dt.float32
BF16 = mybir.dt.bfloat16
AF = mybir.ActivationFunctionType
ALU = mybir.AluOpType
AX = mybir.AxisListType

SAME_ENGINE_SYNC = True
N_DMA_SEMS = 6
NCORES = 8
EM05 = float(np.exp(-0.5))


class Buf:
    __slots__ = ("name", "w", "r", "parts")

    def __init__(self, name=""):
        self.name = name
        self.w = None
        self.r = []
        self.parts = None


def _flat(bufs):
    out = []
    for b in bufs:
        if b.parts:
            out.extend(b.parts)
        else:
            out.append(b)
    return out


class Sched:
    ENGS = ("pe", "act", "dve", "pool", "sp")

    def __init__(self, nc):
        self.nc = nc
        self.ops = {e: [] for e in self.ENGS}
        self.dma_rr = {e: 0 for e in self.ENGS}
        self.dma_hist = {e: [[] for _ in range(N_DMA_SEMS)] for e in self.ENGS}
        self.out_dmas = []

    def _deps(self, reads, writes):
        reads, writes = _flat(reads), _flat(writes)
        deps = []
        for b in reads:
            if b.w is not None:
                deps.append(b.w)
        for b in writes:
            if b.w is not None:
                deps.append(b.w)
            deps.extend(b.r)
        return deps

    def _commit(self, ref, reads, writes):
        reads, writes = _flat(reads), _flat(writes)
        for b in reads:
            b.r.append(ref)
        for b in writes:
            b.w = ref
            b.r = []

    def op(self, eng, fn, reads=(), writes=()):
        deps = self._deps(reads, writes)
        idx = len(self.ops[eng])
        self.ops[eng].append(dict(kind="op", fn=fn, deps=deps, sig=False, cnt=None))
        ref = ("op", eng, idx)
        self._commit(ref, reads, writes)
        return ref

    def dma(self, eng, out, in_, reads=(), writes=(), is_output=False):
        deps = self._deps(reads, writes)
        k = self.dma_rr[eng]
        self.dma_rr[eng] = (k + 1) % N_DMA_SEMS
        hist = self.dma_hist[eng][k]
        if hist:
            deps.append(hist[-1])
        val = 16 * (len(hist) + 1)
        ref = ("dma", eng, k, val)
        hist.append(ref)
        self.ops[eng].append(dict(kind="dma", out=out, in_=in_, deps=deps, semk=k, val=val))
        self._commit(ref, reads, writes)
        if is_output:
            self.out_dmas.append(ref)
        return ref

    def coll(self, fn, reads=(), writes=()):
        deps = self._deps(reads, writes)
        self.n_coll = getattr(self, "n_coll", 0) + 1
        ref = ("dma", "pool", N_DMA_SEMS, self.n_coll)
        self.ops["pool"].append(dict(kind="coll", fn=fn, deps=deps))
        self._commit(ref, reads, writes)
        return ref

    def emit(self):
        nc = self.nc

        def skip_same(d, e):
            return d[1] == e and (not SAME_ENGINE_SYNC or e == "pe")

        for e in self.ENGS:
            for o in self.ops[e]:
                last = {}
                for d in o["deps"]:
                    if d[0] == "op" and not skip_same(d, e):
                        if d[2] > last.get(d[1], -1):
                            last[d[1]] = d[2]
                o["last"] = last
                for pe_, idx_ in last.items():
                    self.ops[pe_][idx_]["sig"] = True
        for e in self.ENGS:
            c = 0
            for o in self.ops[e]:
                if o["kind"] == "op" and o["sig"]:
                    c += 1
                    o["cnt"] = c
        with contextlib.ExitStack() as st:
            esem = {e: st.enter_context(nc.semaphore("s_" + e)) for e in self.ENGS}
            dsem = {e: [st.enter_context(nc.semaphore("d_%s%d" % (e, k))) for k in range(N_DMA_SEMS + 1)]
                    for e in ("sp", "act", "pool")}
            block = st.enter_context(nc.Block())
            sched = self

            def run(e, eng):
                waited = {}
                for o in sched.ops[e]:
                    need = {}
                    for pe_, idx_ in o["last"].items():
                        need[("op", pe_)] = sched.ops[pe_][idx_]["cnt"]
                    for d in o["deps"]:
                        if d[0] != "op":
                            key = ("dma", d[1], d[2])
                            if d[3] > need.get(key, 0):
                                need[key] = d[3]
                    for key, v in need.items():
                        if waited.get(key, 0) >= v:
                            continue
                        waited[key] = v
                        s = esem[key[1]] if key[0] == "op" else dsem[key[1]][key[2]]
                        eng.wait_ge(s, v)
                    if o["kind"] == "op":
                        ins = o["fn"](eng)
                        if o["sig"]:
                            ins.then_inc(esem[e], 1)
                    elif o["kind"] == "coll":
                        o["fn"](eng).then_inc(dsem["pool"][N_DMA_SEMS])
                    else:
                        eng.dma_start(out=o["out"], in_=o["in_"]).then_inc(dsem[e][o["semk"]], 16)
                if e == "sp":
                    for ref in sched.out_dmas:
                        eng.wait_ge(dsem[ref[1]][ref[2]], ref[3])

            block.tensor(lambda eng: run("pe", eng))
            block.scalar(lambda eng: run("act", eng))
            block.vector(lambda eng: run("dve", eng))
            block.gpsimd(lambda eng: run("pool", eng))
            block.sync(lambda eng: run("sp", eng))


PV_FIELDS = [("ada_b", 48), ("n1g", 8), ("n2g", 8), ("mu_p", 12), ("mu_n", 12), ("a0f", 4), ("a0b", 4),
             ("k_k", 4), ("k_a", 4), ("r_k", 4), ("lnx_g", 4), ("lnx_b", 4), ("conv_b", 4), ("cln_g", 4),
             ("cln_b", 4), ("conv_w", 124)]
PV_OFF = {}
_o = 0
for _n, _c in PV_FIELDS:
    PV_OFF[_n] = _o
    _o += _c
NPV = _o


def _fm(v):
    v = np.asarray(v, np.float32).reshape(-1)
    return np.ascontiguousarray(v.reshape(-1, 128).T)


def _consts():
    idx = np.arange(128)
    s, t = idx[:, None], idx[None, :]
    cst = np.zeros((128, 8, 128), np.float32)
    cst[:, 0] = np.eye(128)
    cst[:, 1] = -EM05 * (s <= t)
    cst[:, 2] = -EM05 * (s < t)
    cst[:, 3] = -EM05 * (s >= t)
    cst[:, 4] = -EM05 * (s > t)
    cst[:, 5] = ((s // 64) == (t // 64))
    cst[:, 6] = 1.0 / 512
    cst[:, 7] = 1.0
    msk = np.zeros((128, 2, 2, 384), np.float32)
    for d in range(2):
        strict = (s < t) if d == 0 else (s > t)
        incl = (s <= t) if d == 0 else (s >= t)
        msk[:, d, :, 0:128] = strict[:, None, :]
        msk[:, d, :, 128:256] = incl[:, None, :]
        msk[:, d, :, 256:384] = strict.T[:, None, :]
    id4 = np.zeros((128, 2, 128), np.float32)
    id4[:] = np.eye(128)[:, None, :]
    mk = np.zeros((128, 4, 2, 128), np.float32)
    mk[:, 0] = (s // 16 == t // 16)[:, None, :]
    for li, b in enumerate((16, 32, 64)):
        mk[:, 1 + li] = ((s // (2 * b) == t // (2 * b)) & (s // b != t // b))[:, None, :]
    return cst.reshape(128, 1024), msk.reshape(128, 1536), id4.reshape(128, 256), mk.reshape(128, 1024)


class Arena:
    def __init__(self, t, words):
        self.t, self.words, self.ptr = t, words, 0

    def alloc(self, shape, dt=F32):
        free = int(np.prod(shape[1:]))
        words = free if dt == F32 else (free + 1) // 2
        assert self.ptr + words <= self.words, ("arena overflow", self.ptr, words, self.words)
        ap = self.t[0:shape[0], self.ptr:self.ptr + words]
        self.ptr += words
        if dt == BF16:
            ap = ap.bitcast(BF16)
        if len(shape) == 3:
            ap = ap.rearrange("p (a b) -> p a b", b=shape[2])
        elif len(shape) == 4:
            ap = ap.rearrange("p (a b c) -> p a b c", b=shape[2], c=shape[3])
        elif len(shape) == 5:
            ap = ap.rearrange("p (a b c d) -> p a b c d", b=shape[2], c=shape[3], d=shape[4])
        return ap


def build(dbg=(), stop_after=None):
    nc = bass.Bass("TRN2", target_bir_lowering=False)
    S = Sched(nc)
    st = contextlib.ExitStack()

    def din(name, shape, dt=F32):
        return nc.dram_tensor(name, list(shape), dt, kind="ExternalInput").ap()

    def dout(name, shape):
        return nc.dram_tensor(name, list(shape), F32, kind="ExternalOutput").ap()

    def sb(name, shape, dt=F32):
        t = st.enter_context(nc.sbuf_tensor(name, list(shape), dt))
        return t[:]

    xm = din("xm", [1024, 1024])
    xh = din("xh", [2, 1024])
    hmask = din("hmask", [128, 16])
    condT = din("condT", [128, 24])
    pvec_d = din("pvec", [128, NPV])
    w0row_d = din("w0row", [1, 1024])
    fgrep_d = din("fgrep", [128, 1024])
    cst_d = din("cst", [128, 1024])
    msk_d = din("msk", [128, 1536])
    id4_d = din("id4", [128, 256])
    mk_d = din("mk", [128, 1024])
    s0T_d = din("s0T", [2, 128, 4, 64])
    sel_d = din("sel", [128, 8])
    idh_d = din("idh", [128, 256])
    bounce_d = nc.dram_tensor("bounce", [128, 1024], F32).ap()
    gath_d = nc.dram_tensor("gath", [512, 1024], F32).ap()
    w1cat_d = din("w1cat", [1024, 384])
    w2dec_d = din("w2dec", [128, 512])
    a2cat_d = din("a2cat", [128, 512])
    g2_d = din("g2", [128, 512])
    ada_w_d = din("ada_w", [1024, 1536])
    adab_d = din("adab", [128, 12])
    selb_d = din("selb", [128, 2])
    abounce_d = nc.dram_tensor("abounce", [128, 36], F32).ap()
    agath_d = nc.dram_tensor("agath", [512, 36], F32).ap()
    w_in_d = din("w_in", [1024, 4608])
    wor_d = din("w_out_rwkv", [512, 1024])
    woc_d = din("w_out_conv", [512, 1024])
    wo_d = din("w_o", [1024, 1024])
    w1_d = din("mlp_w1", [1024, 4096])
    w2_d = din("mlp_w2", [4096, 1024])
    y_d = dout("y", [1024, 1024])
    st_d = dout("st", [2, 2, 128, 4, 64])
    dbg_d = {n: dout("dbg_" + n, shp) for n, shp in dbg}

    xnT = sb("xnT", [128, 8, 1024], BF16)
    xnTh = sb("xnTh", [128, 8, 2], BF16)
    pvec = sb("pvec_sb", [128, NPV])
    hm = sb("hm", [128, 16])
    cT = sb("cT", [128, 24]); scT = sb("scT", [128, 24])
    adab = sb("adab_sb", [128, 12]); selb = sb("selb_sb", [128, 2])
    mod = sb("mod", [128, 48, 2])
    A1 = sb("A1", [128, 8, 2]); A2 = sb("A2", [128, 8, 2])
    cst = sb("cst_sb", [128, 8, 128])
    ident32 = cst[:, 0, :]
    blk1 = cst[:, 5, :]
    ident16 = sb("ident16", [128, 128], BF16)
    id4 = sb("id4_sb", [128, 2, 128], BF16)
    msk = sb("msk_sb", [128, 2, 2, 384], BF16)
    mk = sb("mk_sb", [128, 4, 2, 128], BF16)
    w0row = sb("w0row_sb", [1, 1024])
    ones32 = sb("ones32", [1, 128])
    ss = sb("ss", [128, 16]); rstd = sb("rstd", [128, 16])
    epsr = sb("epsr", [128, 4])
    c0v = sb("c0v", [128, 12])
    a0h = sb("a0h", [128, 8]); kah = sb("kah", [128, 4])
    wring = [sb("wring%d" % i, [128, 4096], BF16) for i in range(4)]
    wringb = [Buf("wring%d" % i) for i in range(4)]
    w2dec = sb("w2dec_sb", [128, 512], BF16); a2cat = sb("a2cat_sb", [128, 512], BF16); g2 = sb("g2w", [128, 512], BF16)
    hTd = sb("hTd", [128, 1024], BF16); hTi = sb("hTi", [128, 1024], BF16); hTg = sb("hTg", [128, 1024], BF16)
    outT16 = sb("outT16", [128, 4, 1024], BF16)
    selv = sb("selv", [128, 8])
    hsel = sb("hsel", [128, 2])
    idh = sb("idh_sb", [128, 4, 64])
    Hin = sb("Hin", [128, 2, 4, 64])
    ARENA_WORDS = 31400
    arena_t = sb("arena", [128, ARENA_WORDS])
    AR = Arena(arena_t, ARENA_WORDS)

    psF = [st.enter_context(nc.psum_tensor("psF%d" % i, [128, 512], F32))[:] for i in range(6)]
    psB = [st.enter_context(nc.psum_tensor("psB%d" % i, [128, 512], BF16))[:] for i in range(2)]
    psFb = [Buf("psF%d" % i) for i in range(6)]
    psBb = [Buf("psB%d" % i) for i in range(2)]
    rr = {"F": 0, "B": 0, "W": 0}

    psHb = [Buf("psH%d" % i) for i in range(12)]
    for i in range(6):
        psFb[i].parts = [psHb[i], psHb[i + 6]]
    rr["H"] = 0

    def getF():
        i = rr["F"]; rr["F"] = (i + 1) % 6
        return psF[i], psFb[i]

    def getH():
        k = rr["H"]; rr["H"] = (k + 1) % 12
        return psF[k % 6][:, (k // 6) * 256:(k // 6) * 256 + 256], psHb[k]

    def getB():
        i = rr["B"]; rr["B"] = (i + 1) % 2
        return psB[i], psBb[i]

    def pv(name, i=0, n=1):
        o = PV_OFF[name] + i
        return pvec[:, o:o + n]

    def mm(out, lhsT, rhs, start, stop, r, w, tp=None):
        kw = {} if tp is None else dict(tile_position=tp)
        return S.op("pe", lambda e: e.matmul(out, lhsT, rhs, start=start, stop=stop, **kw), reads=r, writes=w)

    def tr(out, in_, idn, r, w):
        return S.op("pe", lambda e: e.transpose(out, in_, idn), reads=r, writes=w)

    def act(out, in_, func, r, w, bias=None, scale=None, accum=None):
        kw = {}
        if bias is not None: kw["bias"] = bias
        if scale is not None: kw["scale"] = scale
        if accum is not None: kw["accum_out"] = accum
        return S.op("act", lambda e: e.activation(out, in_, func, **kw), reads=r, writes=w)

    def tt(eng, out, a, b, op, r, w):
        return S.op(eng, lambda e: e.tensor_tensor(out, a, b, op), reads=r, writes=w)

    def ts(eng, out, a, s1, s2, op0, op1, r, w):
        if op1 is None:
            return S.op(eng, lambda e: e.tensor_scalar(out, a, s1, None, op0), reads=r, writes=w)
        return S.op(eng, lambda e: e.tensor_scalar(out, a, s1, s2, op0, op1), reads=r, writes=w)

    def stt(eng, out, a, s, b, op0, op1, r, w):
        return S.op(eng, lambda e: e.scalar_tensor_tensor(out, a, s, b, op0, op1), reads=r, writes=w)

    def cp(eng, out, in_, r, w):
        if eng == "act":
            return S.op("act", lambda e: e.copy(out, in_), reads=r, writes=w)
        return S.op(eng, lambda e: e.tensor_copy(out, in_), reads=r, writes=w)

    def recip(out, in_, r, w):
        return S.op("dve", lambda e: e.reciprocal(out, in_), reads=r, writes=w)

    def memset(eng, ap, val, w):
        return S.op(eng, lambda e: e.memset(ap, val), writes=w)

    B = {}
    carry = {"refs": []}

    def bf(name):
        if name not in B:
            b = Buf(name)
            b.r = list(carry["refs"])
            B[name] = b
        return B[name]

    def new_phase():
        refs = set()
        for b in list(B.values()) + psHb + psBb + wringb:
            if b.w is not None:
                refs.add(b.w)
            refs.update(b.r)
        best = {}
        for rf in refs:
            key = rf[:2] if rf[0] == "op" else rf[:3]
            val = rf[2] if rf[0] == "op" else rf[3]
            if key not in best or val > (best[key][2] if rf[0] == "op" else best[key][3]):
                best[key] = rf
        carry["refs"] = list(best.values())
        for wb_ in wringb:
            wb_.r = list(wb_.r) + list(carry["refs"])
        AR.ptr = 0
        for k in [k for k in B if k.startswith("a_")]:
            del B[k]

    def dump(name, ap, r):
        if name in dbg_d:
            S.dma("pool", dbg_d[name], ap, reads=r, is_output=True)

    def loadw(src_ap, view_fn, slot=None):
        if slot is None:
            i = rr["W"]; rr["W"] = (i + 1) % 4
        else:
            i = slot
        v = view_fn(wring[i])
        S.dma("pool", v, src_ap, writes=[wringb[i]])
        return v, wringb[i]

    S.dma("sp", pvec, pvec_d, writes=[bf("pvec")])
    S.dma("sp", cT, condT, writes=[bf("cT")])
    S.dma("sp", hm, hmask, writes=[bf("hm")])
    S.dma("sp", cst.rearrange("p a b -> p (a b)"), cst_d, writes=[bf("cst")])
    S.dma("sp", w0row, w0row_d, writes=[bf("w0row")])
    S.dma("sp", Hin[:, 0], s0T_d[0], writes=[bf("Hin0")])
    S.dma("sp", Hin[:, 1], s0T_d[1], writes=[bf("Hin1")])
    S.dma("sp", selv, sel_d, writes=[bf("selv")])
    S.dma("sp", idh.rearrange("p a b -> p (a b)"), idh_d, writes=[bf("idh")])
    S.dma("pool", ident16, cst_d[:, 0:128], writes=[bf("ident16")])
    S.dma("pool", id4.rearrange("p a b -> p (a b)"), id4_d, writes=[bf("id4")])
    S.dma("pool", msk.rearrange("p a b c -> p (a b c)"), msk_d, writes=[bf("msk")])
    S.dma("pool", mk.rearrange("p a b c -> p (a b c)"), mk_d, writes=[bf("mk")])
    S.dma("pool", w2dec, w2dec_d, writes=[bf("w2dec")])
    S.dma("pool", a2cat, a2cat_d, writes=[bf("a2cat")])
    S.dma("pool", g2, g2_d, writes=[bf("g2")])
    x_sb = AR.alloc([128, 8, 1024])
    xs16 = AR.alloc([128, 8, 1024], BF16)
    junk = AR.alloc([128, 1024])
    xh_sb = AR.alloc([2, 1024]); xh16 = AR.alloc([2, 1024], BF16); xht = AR.alloc([128, 8, 2])
    S.dma("sp", xh_sb, xh, writes=[bf("a_xh")])
    for t in range(8):
        S.dma("sp", x_sb[:, t, :], xm[t * 128:(t + 1) * 128, :], writes=[bf("a_x%d" % t)])
    memset("dve", epsr[:, 0:1], 1e-6, [bf("epsr")])
    memset("dve", epsr[:, 1:2], 64e-5, [bf("epsr")])
    memset("dve", epsr[:, 2:3], 1e-5, [bf("epsr")])
    memset("dve", ss, 0.0, [bf("ss")])
    memset("dve", ones32, 1.0, [bf("ones32")])
    memset("dve", hsel, 0.0, [bf("hsel")])
    memset("dve", hsel[0:64, 0:1], 1.0, [bf("hsel")])
    memset("dve", hsel[64:128, 1:2], 1.0, [bf("hsel")])
    ts("dve", c0v, pv("mu_p", 0, 12), -1.0, 1.0, ALU.mult, ALU.add, [bf("pvec")], [bf("c0v")])
    tt("dve", c0v, c0v, pv("mu_n", 0, 12), ALU.subtract, [bf("c0v"), bf("pvec")], [bf("c0v")])
    ts("dve", a0h, pv("a0f", 0, 8), 0.5, None, ALU.mult, None, [bf("pvec")], [bf("a0h")])
    ts("dve", kah, pv("k_a", 0, 4), 0.5, None, ALU.mult, None, [bf("pvec")], [bf("kah")])

    def rmsnorm_T(x_sb, xs16, junk, Aw, Awname, shc, xn="a_x", part=None):
        if part in (None, 1):
            memset("dve", ss[:, 0:8], 0.0, [bf("ss")])
            for t in range(8):
                act(junk, x_sb[:, t, :], AF.Square, [bf(xn + "%d" % t)], [bf("a_junk"), bf("ss")], accum=ss[:, t:t + 1])
            act(rstd[:, 0:8], ss[:, 0:8], AF.Sqrt, [bf("ss"), bf("epsr")], [bf("rstd")], bias=epsr[:, 0:1], scale=1.0 / 1024)
            recip(rstd[:, 0:8], rstd[:, 0:8], [bf("rstd")], [bf("rstd")])
            for t in range(8):
                if t % 2 == 0:
                    ts("dve", xs16[:, t, :], x_sb[:, t, :], rstd[:, t:t + 1], None, ALU.mult, None,
                       [bf(xn + "%d" % t), bf("rstd")], [bf("a_xs16_%d" % t)])
                else:
                    act(xs16[:, t, :], x_sb[:, t, :], AF.Copy, [bf(xn + "%d" % t), bf("rstd")], [bf("a_xs16_%d" % t)], scale=rstd[:, t:t + 1])
        if part in (None, 2):
            for kc in range(8):
                for half in range(2):
                    p, pb = getB()
                    for q in range(4):
                        t = half * 4 + q
                        tr(p[:, q * 128:(q + 1) * 128], xs16[:, t, kc * 128:(kc + 1) * 128], ident16,
                           [bf("a_xs16_%d" % t), bf("ident16")], [pb])
                    act(xnT[:, kc, half * 512:(half + 1) * 512], p[:, 0:512], AF.Identity, [pb, bf(Awname), bf("mod")],
                        [bf("xnT%d" % half)], bias=mod[:, shc + kc, half:half + 1], scale=Aw[:, kc, half:half + 1])

    wl1v, wl1b = loadw(w1cat_d.rearrange("(kc p) n -> p kc n", p=128),
                     lambda w: w[:, 0:3072].rearrange("p (kc n) -> p kc n", kc=8))
    w_in_v = w_in_d.rearrange("(kc p) n -> p kc n", p=128)
    wrkv = []
    for i in range(3):
        v, b_ = loadw(w_in_v[:, :, i * 512:(i + 1) * 512], lambda w: w.rearrange("p (kc n) -> p kc n", kc=8))
        wrkv.append((v, b_))

    S.dma("sp", adab, adab_d, writes=[bf("adab")])
    S.dma("sp", selb, selb_d, writes=[bf("selb")])
    act(scT, cT, AF.Silu, [bf("cT")], [bf("scT")])
    ada_v = ada_w_d.rearrange("(kc p) n -> p kc n", p=128)
    adaw = AR.alloc([128, 8, 1536])
    S.dma("sp", adaw[:, 0:4, :], ada_v[:, 0:4, :], writes=[bf("a_adaw0")])
    S.dma("act", adaw[:, 4:8, :], ada_v[:, 4:8, :], writes=[bf("a_adaw1")])
    modrow = AR.alloc([3, 1536])
    for nchunk in range(3):
        pr_, prb_ = getF()
        for kc in range(8):
            mm(pr_[0:3, :], scT[:, kc * 3:kc * 3 + 3], adaw[:, kc, nchunk * 512:(nchunk + 1) * 512], kc == 0, kc == 7,
               [bf("a_adaw%d" % (kc // 4)), bf("scT")], [prb_])
        cp("act", modrow[:, nchunk * 512:(nchunk + 1) * 512], pr_[0:3, :], [prb_], [bf("a_modrow")])
    modp, modpb = getF()
    for f in range(12):
        mm(modp[:, f * 3:f * 3 + 3], modrow[:, f * 128:(f + 1) * 128], cst[0:3, 0, 0:3], True, True, [bf("a_modrow"), bf("cst")], [modpb])
    modpart = AR.alloc([128, 12, 3])
    for j in range(3):
        tt("dve", modpart[:, :, j], modp[:, 0:36].rearrange("p (f j) -> p f j", j=3)[:, :, j], adab, ALU.add, [modpb, bf("adab")], [bf("a_modpart")])
    S.dma("pool", abounce_d, modpart.rearrange("p f j -> p (f j)"), reads=[bf("a_modpart")], writes=[bf("abounce")])
    S.coll(lambda en: en.collective_compute("AllGather", ALU.bypass, replica_groups=[[0, 1, 2, 3], [4, 5, 6, 7]],
                                            ins=[abounce_d.opt()], outs=[agath_d.opt()]),
           reads=[bf("abounce")], writes=[bf("agath")])
    modall = AR.alloc([128, 48, 3])
    S.dma("pool", modall.rearrange("p (r f) j -> p r (f j)", r=4), agath_d.rearrange("(r p) n -> p r n", p=128), reads=[bf("agath")], writes=[bf("a_modall")])
    rmsnorm_T(x_sb, xs16, junk, A1, "A1", 0, part=1)
    cp("dve", mod[:, :, 0], modall[:, :, 0], [bf("a_modall")], [bf("mod")])
    ts("dve", mod[:, :, 1], modall[:, :, 1], selb[:, 0:1], None, ALU.mult, None, [bf("a_modall"), bf("selb")], [bf("mod")])
    stt("dve", mod[:, :, 1], modall[:, :, 2], selb[:, 1:2], mod[:, :, 1], ALU.mult, ALU.add, [bf("a_modall"), bf("selb"), bf("mod")], [bf("mod")])
    for j in range(2):
        stt("dve", A1[:, :, j], mod[:, 8:16, j], 1.0, pv("n1g", 0, 8), ALU.add, ALU.mult, [bf("mod"), bf("pvec")], [bf("A1")])
        stt("dve", A2[:, :, j], mod[:, 32:40, j], 1.0, pv("n2g", 0, 8), ALU.add, ALU.mult, [bf("mod"), bf("pvec")], [bf("A2")])
    dump("mod", mod.rearrange("p f j -> p (f j)"), [bf("mod")])

    rmsnorm_T(x_sb, xs16, junk, A1, "A1", 0, part=2)
    memset("dve", ss[0:2, 8:9], 0.0, [bf("ssh")])
    act(junk[0:2, :], xh_sb, AF.Square, [bf("a_xh")], [bf("a_junk"), bf("ssh")], accum=ss[0:2, 8:9])
    act(rstd[0:2, 8:9], ss[0:2, 8:9], AF.Sqrt, [bf("ssh"), bf("epsr")], [bf("rstdh")], bias=epsr[0:2, 0:1], scale=1.0 / 1024)
    recip(rstd[0:2, 8:9], rstd[0:2, 8:9], [bf("rstdh")], [bf("rstdh")])
    ts("dve", xh16, xh_sb, rstd[0:2, 8:9], None, ALU.mult, None, [bf("a_xh"), bf("rstdh")], [bf("a_xh16")])
    p, pb = getB()
    for kc in range(8):
        tr(p[:, kc * 2:kc * 2 + 2], xh16[:, kc * 128:(kc + 1) * 128], ident16[0:2, 0:2], [bf("a_xh16"), bf("ident16")], [pb])
    for kc in range(8):
        act(xht[:, kc, :], p[:, kc * 2:kc * 2 + 2], AF.Identity, [pb, bf("A1"), bf("mod")], [bf("a_xht")],
            bias=mod[:, kc, 1:2], scale=A1[:, kc, 1:2])
    tt("dve", xnTh, xht, hm.rearrange("p (k j) -> p k j", j=2), ALU.mult, [bf("a_xht"), bf("hm")], [bf("xnTh")])
    XN = [bf("xnT0"), bf("xnT1")]
    dump("xnT", xnT.rearrange("p k n -> p (k n)"), XN)

    class _Stop(Exception):
        pass

    def chk(tag):
        if stop_after == tag:
            raise _Stop()

    def _rest():
        for mt, (dst, fn, nm) in enumerate(((hTd, AF.Tanh, "hTd"), (hTi, AF.Identity, "hTi"), (hTg, AF.Sigmoid, "hTg"))):
            for half in range(2):
                p, pb = getF()
                for kc in range(8):
                    mm(p, wl1v[:, kc, mt * 128:(mt + 1) * 128], xnT[:, kc, half * 512:(half + 1) * 512], kc == 0, kc == 7,
                       [wl1b, XN[half]], [pb])
                act(dst[:, half * 512:(half + 1) * 512], p, fn, [pb], [bf(nm)])
        dump("hTd", hTd, [bf("hTd")])
        chk("p3")

        new_phase()
        STO = dict(
            QT=AR.alloc([128, 4, 4, 2 * 128], BF16),
            PT=AR.alloc([128, 4, 4, 2 * 64], BF16),
            G=AR.alloc([128, 4, 4, 2 * 64]),
            Y0=AR.alloc([128, 4, 512]),
            Gam=AR.alloc([128, 4, 4, 2]),
            H0=AR.alloc([128, 4, 4, 2 * 64], BF16),
            bonus=AR.alloc([128, 4, 512], BF16),
            )
        rawA = [AR.alloc([128, 516])] * 2
        rawB = rawA
        rT16 = AR.alloc([128, 512], BF16); vT16 = AR.alloc([128, 512], BF16); kT = AR.alloc([128, 512])
        alias_ptr = AR.ptr
        rrk = AR.alloc([128, 512], BF16)
        kk = AR.alloc([128, 512])
        Vtok2 = [AR.alloc([128, 4, 128], BF16) for _ in range(2)]
        sg32 = AR.alloc([128, 4, 128])
        Ei = AR.alloc([128, 512]); Ex = AR.alloc([128, 512]); En = AR.alloc([128, 512])
        shtmp = Ex
        aT = AR.alloc([128, 512]); t1 = AR.alloc([128, 512]); bT = AR.alloc([128, 512])
        sq, rs = t1, bT
        kd = [AR.alloc([128, 512]) for _ in range(2)]
        AR16 = AR.alloc([128, 4, 256], BF16); BT16 = AR.alloc([128, 512], BF16); KT16 = AR.alloc([128, 512], BF16)
        AZ = AR.alloc([128, 4, 2, 128], BF16); Btok = AR.alloc([128, 4, 128], BF16); Ktok = AR.alloc([128, 4, 128], BF16)
        NGS = 4
        GB = []
        gb_ptr = {}
        for i_ in range(NGS):
            gb_ptr[i_] = AR.ptr
            if i_ == 3:
                gb3_ptr = AR.ptr
            if i_ < 2:
                mb_ = AR.alloc([128, 4, 2, 2, 128], BF16)
            else:
                mb_ = wring[0][:, (i_ - 2) * 2048:(i_ - 1) * 2048].rearrange("p (m x e c) -> p m x e c", m=4, x=2, e=2)
                bmb = bf("a_MB_%d" % i_)
                bmb.r = bmb.r + list(wringb[0].r) + ([wringb[0].w] if wringb[0].w is not None else [])
            g_ = dict(NL=AR.alloc([128, 2, 2, 128], BF16), ARB=AR.alloc([128, 2, 128], BF16), KA=AR.alloc([128, 2, 256], BF16),
                      MB=mb_,
                      RR=[AR.alloc([128, 2, 2, 128], BF16) for _ in range(2)], LN=[AR.alloc([128, 2, 2, 128], BF16) for _ in range(2)],
                      XX=AR.alloc([128, 2, 2, 128], BF16), WU=AR.alloc([128, 2, 128], BF16))
            GB.append(g_)
        AR16m = [AR.alloc([128, 4, 256], BF16) for _ in range(2)]
        HS = [dict(Hst=AR.alloc([128, 4, 64]), Hs16=AR.alloc([128, 4, 64], BF16), Htmp=AR.alloc([128, 4, 64]))]
        sp__ = AR.ptr
        AR.ptr = gb3_ptr
        for _ in range(3):
            HS.append(dict(Hst=AR.alloc([128, 4, 64]), Hs16=AR.alloc([128, 4, 64], BF16), Htmp=AR.alloc([128, 4, 64])))
        assert AR.ptr <= gb3_ptr + 2048
        AR.ptr = sp__
        GB3_NAMES = ["a_%s_3" % k_ for k_ in ("NL", "ARB", "KA", "RR0", "RR1", "LN0", "LN1", "XX", "WU")]
        GB01_NAMES = ["a_%s_%d" % (k_, i_) for i_ in range(2) for k_ in ("NL", "ARB", "KA", "MB", "RR0", "RR1", "LN0", "LN1", "XX", "WU")]
        OT = []
        sp2__ = AR.ptr
        for k_ in range(4):
            if k_ % 2 == 0:
                AR.ptr = gb_ptr[k_ // 2]
            y_ = AR.alloc([128, 512]); q_ = AR.alloc([128, 512]); n16_ = AR.alloc([128, 512], BF16); g_s = AR.alloc([128, 48])
            OT.append(dict(ytok=y_, ysq=q_, yn16=n16_, gst=g_s, ynT=q_.rearrange("p (c t) -> p c t", t=128)))
            assert AR.ptr <= gb_ptr[k_ // 2] + 3072
        AR.ptr = sp2__
        OT_NAMES = ["a_o%s_%d" % (k_, i_) for i_ in range(4) for k_ in ("ytok", "ysq", "yn16", "gst")]
        HS_NAMES = ["a_%s%d" % (k_, i_) for i_ in range(1, 4) for k_ in ("Hst", "Hs16", "Htmp")]

        def inherit(dst_names, src_names):
            refs = []
            for nm in src_names:
                b_ = bf(nm)
                if b_.w is not None:
                    refs.append(b_.w)
                refs.extend(b_.r)
            for nm in dst_names:
                b_ = bf(nm)
                b_.r = list(b_.r) + refs
        memset("dve", rawA[0], 0.0, [bf("a_rawA0")])

        free_banks = list(range(6))

        def take(nb):
            while len(free_banks) < nb:
                yield
            out = []
            for _ in range(nb):
                i = free_banks.pop(0)
                out.append((psF[i], psFb[i], i))
            return out

        def give(*idx):
            free_banks.extend(idx)

        freeB = [0, 1]

        def takeB():
            while not freeB:
                yield
            i = freeB.pop(0)
            return psB[i], psBb[i], i

        def group_gen(blk, ct, d, t, bs):
            sto = STO
            G_ = GB[bs]
            Vtok, VtB = Vtok2[ct % 2], bf("a_Vtok%d" % (ct % 2))
            n = lambda s_: bf("a_%s_%d" % (s_, bs))
            NL, ARB, KA, MB, RR, LN, XX, WU = (G_[k_] for k_ in ("NL", "ARB", "KA", "MB", "RR", "LN", "XX", "WU"))
            v2 = lambda q: q.rearrange("p (a b) -> p a b", b=128)
            v22 = lambda q: q.rearrange("p (x a b) -> p x a b", x=2, b=128)
            tcols = slice(t * 128, (t + 1) * 128)
            (pA, pAb, iA), (pK, pKb, iK), (pC, pCb, iC) = yield from take(3)
            for e in range(2):
                mm(pA[:, e * 256:(e + 1) * 256], BT16[:, tcols], AR16m[e][:, t, :], True, True, [bf("a_BT16"), bf("a_AR16m")], [pAb])
                mm(pC[:, e * 128:(e + 1) * 128], AR16m[e][:, t, 0:128], BT16[:, tcols], True, True, [bf("a_AR16m"), bf("a_BT16")], [pCb])
                mm(pK[:, e * 256:(e + 1) * 256], KT16[:, tcols], AR16m[e][:, t, :], True, True, [bf("a_KT16"), bf("a_AR16m")], [pKb])
            yield
            pA3 = pA.rearrange("p (a b) -> p a b", b=256)
            pK3 = pK.rearrange("p (a b) -> p a b", b=256)
            tt("dve", NL[:, 0], pA3[:, :, 0:128], msk[:, d, 0:2, 0:128], ALU.mult, [pAb, bf("msk")], [n("NL")])
            tt("dve", NL[:, 1], v2(pC[:, 0:256]), msk[:, d, 0:2, 256:384], ALU.mult, [pCb, bf("msk")], [n("NL")])
            yield
            tt("dve", MB.rearrange("p m x e c -> p m x (e c)"),
               NL.rearrange("p x e c -> p x (e c)").unsqueeze(1).to_broadcast([128, 4, 2, 256]),
               mk.rearrange("p m e c -> p m (e c)").unsqueeze(2).to_broadcast([128, 4, 2, 256]), ALU.mult, [n("NL"), bf("mk")], [n("MB")])
            tt("dve", RR[0], MB[:, 0], id4[:, 0:2].unsqueeze(1).to_broadcast([128, 2, 2, 128]), ALU.add, [n("MB"), bf("id4")], [n("RR0")])
            tt("dve", ARB, pA3[:, :, 128:256], msk[:, d, 0:2, 128:256], ALU.mult, [pAb, bf("msk")], [n("ARB")])
            tt("dve", KA, pK3, msk[:, d, 0:2, 0:256], ALU.mult, [pKb, bf("msk")], [n("KA")])
            give(iA, iK, iC)
            yield
            Nc, Lc, NLb = MB[:, 0, 0], MB[:, 0, 1], n("MB")
            pend = None
            for lev in range(4):
                nbk = (1 if lev < 3 else 0) + (1 if lev == 0 else 0) + (1 if pend is not None else 0)
                bks = list((yield from take(nbk)))
                if lev < 3:
                    p1, p1b, i1 = bks.pop(0)
                    for e in range(2):
                        mm(p1[:, e * 128:(e + 1) * 128], Nc[:, e, :], Lc[:, e, :], True, True, [NLb], [p1b])
                        mm(p1[:, 256 + e * 128:256 + (e + 1) * 128], Lc[:, e, :], Nc[:, e, :], True, True, [NLb], [p1b])
                if lev == 0:
                    pZ, pZb, iZ = bks.pop(0)
                    for e in range(2):
                        mm(pZ[:, e * 64:(e + 1) * 64], KA[:, e, 0:128], Vtok[:, t, e * 64:(e + 1) * 64], True, True, [n("KA"), VtB], [pZb])
                if pend is not None:
                    p2, p2b, i2 = bks.pop(0)
                    LNp, LNpb, c_, n_ = pend
                    for e in range(2):
                        mm(p2[:, e * 128:(e + 1) * 128], LNp[:, 0, e, :], RR[c_][:, 0, e, :], True, False, [LNpb, n("RR%d" % c_)], [p2b])
                        mm(p2[:, e * 128:(e + 1) * 128], ident16, RR[c_][:, 0, e, :], False, True, [bf("ident16"), n("RR%d" % c_)], [p2b])
                        mm(p2[:, 256 + e * 128:256 + (e + 1) * 128], LNp[:, 1, e, :], RR[c_][:, 1, e, :], True, False, [LNpb, n("RR%d" % c_)], [p2b])
                        mm(p2[:, 256 + e * 128:256 + (e + 1) * 128], ident16, RR[c_][:, 1, e, :], False, True, [bf("ident16"), n("RR%d" % c_)], [p2b])
                yield
                if pend is not None:
                    cp("act", RR[n_], v22(p2), [p2b], [n("RR%d" % n_)])
                    give(i2)
                    pend = None
                if lev < 3:
                    LNn, LNb = LN[lev % 2], n("LN%d" % (lev % 2))
                    cp("dve", LNn, v22(p1), [p1b], [LNb])
                    give(i1)
                    pend = (LNn, LNb, lev % 2, (lev + 1) % 2)
                    Nc, Lc, NLb = LNn[:, 1], LNn[:, 0], LNb
                if lev == 0:
                    cp("act", AZ[:, t, :, 64:128], pZ[:, 0:128].rearrange("p (e k) -> p e k", k=64), [pZb], [bf("a_AZ%d" % t)])
                    give(iZ)
                yield
            cur = 1
            for li in range(3):
                nx = 1 - cur
                D_, Dt_, Db_ = RR[cur][:, 0], RR[cur][:, 1], n("RR%d" % cur)
                O_, Ot_, Ob_ = MB[:, 1 + li, 0], MB[:, 1 + li, 1], n("MB")
                ((p1, p1b, i1),) = yield from take(1)
                for e in range(2):
                    mm(p1[:, e * 128:(e + 1) * 128], Ot_[:, e, :], D_[:, e, :], True, True, [Ob_, Db_], [p1b])
                    if li < 2:
                        mm(p1[:, 256 + e * 128:256 + (e + 1) * 128], O_[:, e, :], Dt_[:, e, :], True, True, [Ob_, Db_], [p1b])
                yield
                if li < 2:
                    cp("act", XX, v22(p1), [p1b], [n("XX")])
                else:
                    cp("act", XX[:, 0], v2(p1[:, 0:256]), [p1b], [n("XX")])
                give(i1)
                yield
                ((p2, p2b, i2),) = yield from take(1)
                for e in range(2):
                    mm(p2[:, e * 128:(e + 1) * 128], Dt_[:, e, :], XX[:, 0, e, :], True, False, [Db_, n("XX")], [p2b])
                    mm(p2[:, e * 128:(e + 1) * 128], ident16, D_[:, e, :], False, True, [bf("ident16"), Db_], [p2b])
                    if li < 2:
                        mm(p2[:, 256 + e * 128:256 + (e + 1) * 128], D_[:, e, :], XX[:, 1, e, :], True, False, [Db_, n("XX")], [p2b])
                        mm(p2[:, 256 + e * 128:256 + (e + 1) * 128], ident16, Dt_[:, e, :], False, True, [bf("ident16"), Db_], [p2b])
                yield
                if li < 2:
                    cp("act", RR[nx], v22(p2), [p2b], [n("RR%d" % nx)])
                else:
                    cp("act", RR[nx][:, 0], v2(p2[:, 0:256]), [p2b], [n("RR%d" % nx)])
                give(i2)
                yield
                cur = nx
            TT, TTb = RR[0][:, 0], n("RR0")
            ((pW, pWb, iW),) = yield from take(1)
            for e in range(2):
                mm(pW[:, e * 128:(e + 1) * 128], TT[:, e, :], AZ[:, t, e, :], True, True, [TTb, bf("a_AZ%d" % t), bf("a_AZ")], [pWb])
            yield
            cp("act", WU, v2(pW[:, 0:256]), [pWb], [n("WU")])
            give(iW)
            yield
            (pP, pPb, iP), (pF, pFb, iF) = yield from take(2)
            for e in range(2):
                es = slice(e * 64, e * 64 + 64)
                tpo = (0, 64) if e else None
                bk = Btok[:, t, e * 64:(e + 1) * 64]
                kkk = Ktok[:, t, e * 64:(e + 1) * 64]
                vv = Vtok[:, t, e * 64:(e + 1) * 64]
                mm(pP[es, 0:64], WU[:, e, 0:64], bk, True, True, [n("WU"), bf("a_Btok")], [pPb], tp=tpo)
                mm(pF[es, 0:64], bk, WU[:, e, 64:128], True, False, [n("WU"), bf("a_Btok")], [pFb], tp=tpo)
                mm(pF[es, 0:64], kkk, vv, False, True, [bf("a_Ktok"), VtB], [pFb], tp=tpo)
                mm(pF[es, 64:192], WU[:, e, 0:64], ARB[:, e, :], True, True, [n("WU"), n("ARB")], [pFb], tp=tpo)
                mm(pF[:, 192 + e * 64:192 + (e + 1) * 64], ARB[:, e, :], WU[:, e, 64:128], True, False, [n("WU"), n("ARB")], [pFb])
                mm(pF[:, 192 + e * 64:192 + (e + 1) * 64], KA[:, e, 128:256], vv, False, True, [n("KA"), VtB], [pFb])
            yield
            cp("act", sto["PT"][:, ct, t, d * 64:(d + 1) * 64], pP[:, 0:64], [pPb], [bf("a_PT")])
            ts("dve", sto["G"][:, ct, t, d * 64:(d + 1) * 64], pF[:, 0:64], sto["Gam"][:, ct, t, d:d + 1], None, ALU.mult, None, [pFb, bf("a_Gam")], [bf("a_G")])
            tt("dve", sto["QT"][:, ct, t, d * 128:(d + 1) * 128], pF[:, 64:192], AR16[:, t, 128:256], ALU.add, [pFb, bf("a_AR16")], [bf("a_QT")])
            ydst = sto["Y0"][:, t, ct * 128:(ct + 1) * 128]
            if d == 0:
                cp("dve", ydst, pF[:, 192:320], [pFb], [bf("a_Y0_%d" % t)])
            else:
                tt("dve", ydst, pF[:, 192:320], ydst, ALU.add, [pFb, bf("a_Y0_%d" % t)], [bf("a_Y0_%d" % t)])
            give(iP, iF)
            yield

        def take_now(nb):
            assert len(free_banks) >= nb, "PSUM bank pool exhausted outside a generator"
            out = []
            for _ in range(nb):
                i = free_banks.pop(0)
                out.append((psF[i], psFb[i], i))
            return out

        def run_gens(gens):
            alive = list(gens)
            while alive:
                for g_ in list(alive):
                    try:
                        next(g_)
                    except StopIteration:
                        alive.remove(g_)

        def stage_ct(blk, ct):
            c0 = 0 if blk == "A" else 512
            raw, rb = rawA[0], bf("a_rawA0")
            Vt, Vtb = Vtok2[ct % 2], bf("a_Vtok%d" % (ct % 2))
            for wi, (ft, dst, dn) in enumerate(((ct, rT16, "a_rT"), (4 + ct, kT, "a_kT"), (8 + ct, vT16, "a_vT"))):
                slab, slb = wrkv[ft // 4]
                fo = (ft % 4) * 128
                ((p, pb, ip),) = yield from take(1)
                for kc in range(8):
                    mm(p, slab[:, kc, fo:fo + 128], xnT[:, kc, c0:c0 + 512], kc == 0, kc == 7, [slb, XN[c0 // 512]], [pb])
                if blk == "A":
                    r3 = raw.rearrange("p (s c) -> p s c", c=258)
                    cp("act", r3[:, :, 1:257], p.rearrange("p (s c) -> p s c", c=256), [pb], [rb])
                    give(ip)
                    prev, main, nxt = r3[:, :, 0:256], r3[:, :, 1:257], r3[:, :, 2:258]
                    v3 = lambda a_: a_.rearrange("p (s c) -> p s c", c=256)
                else:
                    ((p2, p2b, ip2),) = yield from take(1)
                    for kc in range(8):
                        mm(p2[:, 0:2], slab[:, kc, fo:fo + 128], xnTh[:, kc, :], kc == 0, kc == 7, [slb, bf("xnTh")], [p2b])
                    cp("act", raw[:, 1:513], p, [pb], [rb])
                    cp("act", raw[:, 0:1], p2[:, 0:1], [p2b], [rb])
                    cp("act", raw[:, 513:514], p2[:, 1:2], [p2b], [rb])
                    give(ip, ip2)
                    prev, main, nxt = raw[:, 0:512], raw[:, 1:513], raw[:, 2:514]
                    v3 = lambda a_: a_
                yield
                act(v3(shtmp), main, AF.Identity, [rb, bf("c0v")], [bf("a_Ex")], scale=c0v[:, ft:ft + 1])
                yield
                stt("dve", v3(shtmp), prev, pv("mu_p", ft), v3(shtmp), ALU.mult, ALU.add, [rb, bf("pvec"), bf("a_Ex")], [bf("a_Ex")])
                stt("dve", v3(dst), nxt, pv("mu_n", ft), v3(shtmp), ALU.mult, ALU.add, [rb, bf("pvec"), bf("a_Ex")], [bf(dn)])
                yield
            act(sq, kT, AF.Square, [bf("a_kT"), bf("pvec")], [bf("a_t1")], scale=pv("k_k", ct))
            ((p, pb, ip),) = yield from take(1)
            mm(p, blk1, sq, True, True, [bf("cst"), bf("a_t1")], [pb])
            yield
            ts("dve", rs, p, 1e-24, None, ALU.max, None, [pb], [bf("a_bT")])
            give(ip)
            act(rs, rs, AF.Sqrt, [bf("a_bT")], [bf("a_bT")])
            yield
            recip(rs, rs, [bf("a_bT")], [bf("a_bT")])
            stt("dve", kk, kT, pv("k_k", ct), rs, ALU.mult, ALU.mult, [bf("a_kT"), bf("pvec"), bf("a_bT")], [bf("a_kk")])
            act(rrk, rT16, AF.Copy, [bf("a_rT"), bf("pvec")], [bf("a_rrk")], scale=pv("r_k", ct))
            p, pb, ib = yield from takeB()
            for t in range(4):
                tr(p[:, t * 128:(t + 1) * 128], vT16[:, t * 128:(t + 1) * 128], ident16, [bf("a_vT"), bf("ident16")], [pb])
            yield
            cp("act", Vt, p[:, 0:512].rearrange("p (a b) -> p a b", b=128), [pb], [Vtb])
            freeB.append(ib)
            yield

        def prep_first(blk, ct, d):
            c0 = 0 if blk == "A" else 512
            sto = STO
            ds = slice(d * 64, d * 64 + 64)
            tpd = (64, 0) if d else None
            ((p, pb, ip),) = yield from take(1)
            for t in range(4):
                mm(p[:, t * 128:(t + 1) * 128], hTd[ds, c0 + t * 128:c0 + (t + 1) * 128], w2dec[ds, ct * 128:(ct + 1) * 128],
                   True, False, [bf("hTd"), bf("w2dec")], [pb], tp=tpd)
                mm(p[:, t * 128:(t + 1) * 128], ones32[0:1, :], w0row[0:1, d * 512 + ct * 128:d * 512 + (ct + 1) * 128],
                   False, True, [bf("ones32"), bf("w0row")], [pb])
            yield
            act(sg32, p.rearrange("p (a b) -> p a b", b=128), AF.Tanh, [pb], [bf("a_sg32")], scale=0.5)
            give(ip)
            yield
            ts("dve", sg32, sg32, 0.5, 0.5, ALU.mult, ALU.add, [bf("a_sg32")], [bf("a_sg32")])
            yield
            (pI, pIb, iI), (pX, pXb, iX), (pa_, pab_, ia_) = yield from take(3)
            for t in range(4):
                mm(pI[:, t * 128:(t + 1) * 128], sg32[:, t, :], cst[:, 1 + 2 * d, :], True, True, [bf("a_sg32"), bf("cst")], [pIb])
                mm(pX[:, t * 128:(t + 1) * 128], sg32[:, t, :], cst[:, 2 + 2 * d, :], True, True, [bf("a_sg32"), bf("cst")], [pXb])
            mm(pa_, a2cat[ds, ct * 128:(ct + 1) * 128], hTi[ds, c0:c0 + 512], True, True, [bf("a2cat"), bf("hTi")], [pab_], tp=tpd)
            yield
            act(Ei, pI, AF.Exp, [pIb], [bf("a_Ei")])
            act(En, pI, AF.Exp, [pIb], [bf("a_En")], scale=-1.0)
            act(Ex, pX, AF.Exp, [pXb], [bf("a_Ex")])
            act(aT, pa_, AF.Tanh, [pab_, bf("a0h")], [bf("a_aT")], bias=a0h[:, d * 4 + ct:d * 4 + ct + 1], scale=0.5)
            give(iI, iX, ia_)
            yield
            lastc = 127 if d == 0 else 0
            cp("dve", sto["Gam"][:, ct, :, d], Ei.rearrange("p (t c) -> p t c", c=128)[:, :, lastc], [bf("a_Ei")], [bf("a_Gam")])
            ts("dve", t1, aT, -1.0, kah[:, ct:ct + 1], ALU.add, ALU.mult, [bf("a_aT"), bf("kah")], [bf("a_t1")])
            yield
            stt("dve", kd[d], t1, 1.0, kT, ALU.add, ALU.mult, [bf("a_t1"), bf("a_kT")], [bf("a_kd%d" % d)])
            stt("dve", bT, aT, 1.0, kk, ALU.add, ALU.mult, [bf("a_kk"), bf("a_aT")], [bf("a_bT")])
            yield

        def prep_second(blk, ct, d):
            v4 = lambda a_: a_.rearrange("p (t c) -> p t c", c=128)
            stt("dve", AR16[:, :, 0:128], v4(kk), -1.0, v4(Ex), ALU.mult, ALU.mult, [bf("a_kk"), bf("a_Ex")], [bf("a_AR16")])
            tt("dve", AR16[:, :, 128:256], v4(rT16), v4(Ei), ALU.mult, [bf("a_rT"), bf("a_Ei")], [bf("a_AR16")])
            stt("dve", BT16, bT, 0.5, En, ALU.mult, ALU.mult, [bf("a_bT"), bf("a_En")], [bf("a_BT16")])
            for e_ in range(2):
                act(AR16m[e_], AR16, AF.Copy, [bf("a_AR16"), bf("hsel")], [bf("a_AR16m")], scale=hsel[:, e_:e_ + 1])
            tt("dve", KT16, kd[d], En, ALU.mult, [bf("a_kd%d" % d), bf("a_En")], [bf("a_KT16")])

        def prep_late(blk, ct, d):
            p, pb, ib = yield from takeB()
            for t in range(4):
                tr(p[:, t * 128:(t + 1) * 128], AR16[:, t, 0:128], ident16, [bf("a_AR16"), bf("ident16")], [pb])
            yield
            for e_ in range(2):
                cp("act", AZ[:, :, e_, 0:64], p[:, 0:512].rearrange("p (t c) -> p t c", c=128)[:, :, e_ * 64:(e_ + 1) * 64], [pb], [bf("a_AZ")])
            freeB.append(ib)
            yield
            p, pb, ib = yield from takeB()
            for t in range(4):
                tr(p[:, t * 128:(t + 1) * 128], BT16[:, t * 128:(t + 1) * 128], ident16, [bf("a_BT16"), bf("ident16")], [pb])
            yield
            cp("dve", Btok, p[:, 0:512].rearrange("p (a b) -> p a b", b=128), [pb], [bf("a_Btok")])
            freeB.append(ib)
            yield
            p, pb, ib = yield from takeB()
            for t in range(4):
                tr(p[:, t * 128:(t + 1) * 128], KT16[:, t * 128:(t + 1) * 128], ident16, [bf("a_KT16"), bf("ident16")], [pb])
            yield
            cp("act", Ktok, p[:, 0:512].rearrange("p (a b) -> p a b", b=128), [pb], [bf("a_Ktok")])
            freeB.append(ib)
            yield

        def bonus_ct(ct):
            sto = STO
            tt("dve", t1, kd[0], kd[1], ALU.add, [bf("a_kd0"), bf("a_kd1")], [bf("a_t1")])
            tt("dve", t1, t1, rrk, ALU.mult, [bf("a_t1"), bf("a_rrk")], [bf("a_t1")])
            ((p, pb, ip),) = take_now(1)
            mm(p, blk1, t1, True, True, [bf("cst"), bf("a_t1")], [pb])
            tt("dve", sto["bonus"][:, ct, :], p, vT16, ALU.mult, [pb, bf("a_vT")], [bf("a_bonus")])
            give(ip)

        def first_gen(blk, ct, d):
            if d == 0:
                yield from stage_ct(blk, ct)
            yield from prep_first(blk, ct, d)

        def rwkv_block(blk):
            seq = [(ct, d) for ct in range(4) for d in range(2)]
            run_gens([first_gen(blk, 0, 0)])
            for i_, (ct, d) in enumerate(seq):
                prep_second(blk, ct, d)
                if d == 1:
                    bonus_ct(ct)
                gens = [prep_late(blk, ct, d)] + [group_gen(blk, ct, d, t_, j_) for j_, t_ in enumerate(range(4))]
                if i_ + 1 < len(seq):
                    gens.append(first_gen(blk, *seq[i_ + 1]))
                run_gens(gens)

        def recur_gen(blk, tiles, d, init, useG=True, saveH=True, hs=0, after=None):
            sto = STO
            Hst, Hs16, Htmp = HS[hs]["Hst"], HS[hs]["Hs16"], HS[hs]["Htmp"]
            HB, H16B, HTB = bf("a_Hst%d" % hs), bf("a_Hs16%d" % hs), bf("a_Htmp%d" % hs)
            if init is None:
                memset("dve", Hst, 0.0, [HB])
            else:
                cp("dve", Hst, init[0], [init[1]], [HB])
            for t in tiles:
                cp("act", Hs16, Hst, [HB], [H16B])
                if saveH:
                    cp("act", sto["H0"][:, :, t, d * 64:(d + 1) * 64], Hst, [HB], [bf("a_H0_%d_%d" % (t, d))])
                (p0, p0b, i0), (p1, p1b, i1) = yield from take(2)
                pe2 = [(p0, p0b), (p1, p1b)]
                for ct in range(4):
                    for e in range(2):
                        es = slice(e * 64, e * 64 + 64)
                        mm(pe2[e][0][es, ct * 64:(ct + 1) * 64], sto["PT"][es, ct, t, d * 64:(d + 1) * 64], Hs16[es, ct, :], True, True,
                           [bf("a_PT"), H16B], [pe2[e][1]], tp=(64, 64) if e else None)
                yield
                for e in range(2):
                    es = slice(e * 64, e * 64 + 64)
                    tt("dve", Htmp[es], pe2[e][0][es, 0:256].rearrange("p (a b) -> p a b", b=64), Hst[es], ALU.add, [pe2[e][1], HB], [HTB])
                give(i0, i1)
                yield
                for ct in range(4):
                    if useG:
                        stt("dve", Hst[:, ct, :], Htmp[:, ct, :], sto["Gam"][:, ct, t, d:d + 1], sto["G"][:, ct, t, d * 64:(d + 1) * 64],
                            ALU.mult, ALU.add, [HTB, bf("a_Gam"), bf("a_G")], [HB])
                    else:
                        ts("dve", Hst[:, ct, :], Htmp[:, ct, :], sto["Gam"][:, ct, t, d:d + 1], None, ALU.mult, None,
                           [HTB, bf("a_Gam")], [HB])
                yield
            if after is not None:
                yield from after(Hst, HB)

        def rwkv_out_gen(blk, t, k):
            sto = STO
            c0 = 0 if blk == "A" else 512
            T_ = OT[k]
            ytok, ysq, yn16, gst, ynT = T_["ytok"], T_["ysq"], T_["yn16"], T_["gst"], T_["ynT"]
            n = lambda s_: bf("a_o%s_%d" % (s_, k))
            (p0, p0b, i0), (p1, p1b, i1) = yield from take(2)
            py2 = [(p0, p0b), (p1, p1b)]
            for h in range(8):
                ct, e = h // 2, h % 2
                es = slice(e * 64, e * 64 + 64)
                for d in range(2):
                    mm(py2[e][0][:, ct * 64:(ct + 1) * 64], sto["QT"][es, ct, t, d * 128:(d + 1) * 128], sto["H0"][es, ct, t, d * 64:(d + 1) * 64],
                       d == 0, d == 1, [bf("a_QT"), bf("a_H0_%d_%d" % (t, d))], [py2[e][1]], tp=(64, 0) if e else None)
            yield
            yt4 = ytok.rearrange("p (c e v) -> p c e v", e=2, v=64)
            y04 = sto["Y0"][:, t, :].rearrange("p (c e v) -> p c e v", e=2, v=64)
            for e in range(2):
                tt("dve", yt4[:, :, e, :], py2[e][0][:, 0:256].rearrange("p (c v) -> p c v", v=64), y04[:, :, e, :], ALU.add,
                   [py2[e][1], bf("a_Y0_%d" % t)], [n("ytok")])
            give(i0, i1)
            yield
            y3 = ytok.rearrange("p (h v) -> p h v", v=64)
            S.op("dve", lambda e_, y3=y3, gst=gst: e_.reduce_sum(gst[:, 0:8], y3, AX.X), reads=[n("ytok")], writes=[n("gst")])
            act(ysq, ytok, AF.Square, [n("ytok")], [n("ysq")])
            yield
            S.op("dve", lambda e_, ysq=ysq, gst=gst: e_.reduce_sum(gst[:, 8:16], ysq.rearrange("p (h v) -> p h v", v=64), AX.X), reads=[n("ysq")], writes=[n("gst")])
            ts("dve", gst[:, 16:24], gst[:, 0:8], 1.0 / 64, None, ALU.mult, None, [n("gst")], [n("gst")])
            tt("dve", gst[:, 24:32], gst[:, 16:24], gst[:, 16:24], ALU.mult, [n("gst")], [n("gst")])
            stt("dve", gst[:, 32:40], gst[:, 8:16], 1.0 / 64, gst[:, 24:32], ALU.mult, ALU.subtract, [n("gst")], [n("gst")])
            yield
            act(gst[:, 40:48], gst[:, 32:40], AF.Sqrt, [n("gst"), bf("epsr")], [n("gst")], bias=epsr[:, 1:2])
            yield
            recip(gst[:, 40:48], gst[:, 40:48], [n("gst")], [n("gst")])
            mb = gst[:, 16:24].unsqueeze(2).to_broadcast([128, 8, 64])
            rb_ = gst[:, 40:48].unsqueeze(2).to_broadcast([128, 8, 64])
            tt("dve", y3, y3, mb, ALU.subtract, [n("ytok"), n("gst")], [n("ytok")])
            tt("dve", yn16.rearrange("p (h v) -> p h v", v=64), y3, rb_, ALU.mult, [n("ytok"), n("gst")], [n("yn16")])
            yield
            pT, pTb, iT = yield from takeB()
            ((pg_, pgb_, ig),) = yield from take(1)
            for ct in range(4):
                tr(pT[:, ct * 128:(ct + 1) * 128], yn16[:, ct * 128:(ct + 1) * 128], ident16, [n("yn16"), bf("ident16")], [pTb])
            for ct in range(4):
                mm(pg_[:, ct * 128:(ct + 1) * 128], g2[:, ct * 128:(ct + 1) * 128], hTg[:, c0 + t * 128:c0 + (t + 1) * 128], True, True,
                   [bf("g2"), bf("hTg")], [pgb_])
            yield
            for ct in range(4):
                act(ynT[:, ct, :], pT[:, ct * 128:(ct + 1) * 128], AF.Identity, [pTb, bf("pvec")], [n("ysq")],
                    bias=pv("lnx_b", ct), scale=pv("lnx_g", ct))
            freeB.append(iT)
            yield
            tt("dve", ynT, ynT, sto["bonus"][:, :, t * 128:(t + 1) * 128], ALU.add, [n("ysq"), bf("a_bonus")], [n("ysq")])
            tt("dve", outT16[:, :, c0 + t * 128:c0 + (t + 1) * 128], ynT, pg_.rearrange("p (c t) -> p c t", t=128), ALU.mult,
               [n("ysq"), pgb_], [bf("outT16")])
            give(ig)
            yield

        def rwkv_out(blk, tiles):
            inherit(OT_NAMES, GB01_NAMES)
            run_gens([rwkv_out_gen(blk, t, k) for k, t in enumerate(tiles)])

        rwkv_block("A")
        chk("blockA")
        dump("PT", STO["PT"].rearrange("p a b c -> p (a b c)"), [bf("a_PT")]); dump("G", STO["G"].rearrange("p a b c -> p (a b c)"), [bf("a_G")])
        dump("QT", STO["QT"].rearrange("p a b c -> p (a b c)"), [bf("a_QT")]); dump("Y0", STO["Y0"].rearrange("p a b -> p (a b)"), [bf("a_Y0")]); dump("Gam", STO["Gam"].rearrange("p a b c -> p (a b c)"), [bf("a_Gam")])
        stv = st_d.rearrange("s d p c v -> s d p (c v)")
        inherit(HS_NAMES, GB3_NAMES)

        def out_state(seg, d):
            def f(Hst_, HB_):
                S.dma("sp", stv[seg, d], Hst_.rearrange("p c v -> p (c v)"), reads=[HB_], is_output=True)
                return
                yield
            return f
        run_gens([recur_gen("A", ([2 * seg, 2 * seg + 1] if d == 0 else [2 * seg + 1, 2 * seg]), d, None, hs=seg * 2 + d, after=out_state(seg, d))
                  for seg in range(2) for d in range(2)])
        chk("recurA")
        rwkv_out("A", range(4))
        chk("outA")
        inherit(GB3_NAMES, HS_NAMES)
        inherit(GB01_NAMES, OT_NAMES)
        rwkv_block("B")
        inherit(HS_NAMES, GB3_NAMES)
        wa, wab = loadw(w_in_v[:, :, 1536:2048], lambda w: w.rearrange("p (kc n) -> p kc n", kc=8), slot=1)
        wg, wgb = loadw(w_in_v[:, :, 2048:2560], lambda w: w.rearrange("p (kc n) -> p kc n", kc=8), slot=2)
        save_ptr = AR.ptr
        AR.ptr = alias_ptr
        XS = AR.alloc([128, 2, 2, 4, 64])
        G4 = AR.alloc([128, 4, 1024])
        ctmp = AR.alloc([128, 4, 64])
        assert AR.ptr <= alias_ptr + 5888
        AR.ptr = save_ptr
        retired = [bf(n) for n in ("a_rrk", "a_kk", "a_Vtok0", "a_Vtok1", "a_sg32", "a_Ei", "a_Ex", "a_En", "a_aT", "a_t1", "a_bT", "a_kd0", "a_kd1")]
        def after_N(d):
            def f(Hst_, HB_):
                cp("dve", XS[:, d, 1], Hst_, [HB_], [bf("a_XS")] + retired)
                return
                yield
            return f

        def after_M(d):
            def f(Hst_, HB_):
                (p0, p0b, i0), (p1, p1b, i1) = yield from take(2)
                pe2 = [(p0, p0b), (p1, p1b)]
                for ct in range(4):
                    for e in range(2):
                        es = slice(e * 64, e * 64 + 64)
                        mm(pe2[e][0][es, ct * 64:(ct + 1) * 64], Hst_[es, ct, :], cst[es, 0, e * 64:(e + 1) * 64], True, True,
                           [HB_, bf("cst")], [pe2[e][1]], tp=(64, 64) if e else None)
                yield
                for e in range(2):
                    es = slice(e * 64, e * 64 + 64)
                    cp("dve", XS[es, d, 0], pe2[e][0][es, 0:256].rearrange("p (a b) -> p a b", b=64), [pe2[e][1]], [bf("a_XS")] + retired)
                give(i0, i1)
            return f
        gl = []
        for d in range(2):
            tiles = [0, 1, 2, 3] if d == 0 else [3, 2, 1, 0]
            gl.append(recur_gen("B", tiles, d, None, useG=True, saveH=False, hs=2 * d, after=after_N(d)))
            gl.append(recur_gen("B", tiles, d, (idh, bf("idh")), useG=False, saveH=False, hs=2 * d + 1, after=after_M(d)))
        run_gens(gl)
        S.dma("pool", bounce_d, XS.rearrange("p a b c d -> p (a b c d)"), reads=[bf("a_XS")], writes=[bf("bounce")])
        S.coll(lambda en: en.collective_compute("AllGather", ALU.bypass, replica_groups=[[0, 1, 2, 3], [4, 5, 6, 7]],
                                                ins=[bounce_d.opt()], outs=[gath_d.opt()]),
               reads=[bf("bounce")], writes=[bf("gath")])
        S.dma("pool", G4, gath_d.rearrange("(r p) n -> p r n", p=128), reads=[bf("gath")], writes=[bf("a_G4")])
        G4v = G4.rearrange("p r (d m c v) -> p r d m c v", d=2, m=2, c=4)
        ctmps = [ctmp, HS[3]["Htmp"]]

        def compose_gen(d):
            HB = bf("Hin%d" % d)
            ct_, ctb_ = ctmps[d], bf("a_ctmp%d" % d)
            order = [0, 1, 2] if d == 0 else [3, 2, 1]
            for j in order:
                (p0, p0b, i0), (p1, p1b, i1) = yield from take(2)
                pe2 = [(p0, p0b), (p1, p1b)]
                for ct in range(4):
                    for e in range(2):
                        es = slice(e * 64, e * 64 + 64)
                        mm(pe2[e][0][es, ct * 64:(ct + 1) * 64], G4v[es, j, d, 0, ct, :], Hin[es, d, ct, :], True, True,
                           [bf("a_G4"), HB], [pe2[e][1]], tp=(64, 64) if e else None)
                yield
                for e in range(2):
                    es = slice(e * 64, e * 64 + 64)
                    tt("dve", ct_[es], pe2[e][0][es, 0:256].rearrange("p (a b) -> p a b", b=64), G4v[es, j, d, 1], ALU.add,
                       [pe2[e][1], bf("a_G4")], [ctb_])
                give(i0, i1)
                yield
                tt("dve", ct_, ct_, Hin[:, d], ALU.subtract, [ctb_, HB], [ctb_])
                stt("dve", Hin[:, d], ct_, selv[:, d * 4 + j:d * 4 + j + 1], Hin[:, d], ALU.mult, ALU.add, [ctb_, bf("selv"), HB], [HB])
                yield
        bf("a_ctmp1").r = list(bf("a_ctmp1").r) + list(bf("a_Htmp3").r) + ([bf("a_Htmp3").w] if bf("a_Htmp3").w is not None else [])
        run_gens([compose_gen(0), compose_gen(1)])
        bf("a_Htmp3").r = list(bf("a_Htmp3").r) + list(bf("a_ctmp1").r) + ([bf("a_ctmp1").w] if bf("a_ctmp1").w is not None else [])
        run_gens([recur_gen("B", ([0, 1, 2, 3] if d == 0 else [3, 2, 1, 0]), d, (Hin[:, d], bf("Hin%d" % d)), hs=d) for d in range(2)])
        rwkv_out("B", range(4))
        dump("outT", outT16.rearrange("p c n -> p (c n)"), [bf("outT16")])
        chk("rwkv")

        new_phase()
        x_sb = AR.alloc([128, 8, 1024])
        for t in range(8):
            S.dma("sp", x_sb[:, t, :], xm[t * 128:(t + 1) * 128, :], writes=[bf("xt%d" % t)])
        cv_ptr = AR.ptr
        cv = AR.alloc([128, 4, 1024])
        upA = [AR.alloc([128, 2, 286], BF16) for _ in range(2)]
        upB = [AR.alloc([128, 8, 94], BF16) for _ in range(2)]
        dgw_ptr = AR.ptr
        dgw = [AR.alloc([128, 31, 128], BF16) for _ in range(2)]
        ucT = AR.alloc([128, 4, 1024], BF16)
        mergedT = AR.alloc([128, 8, 1024], BF16)
        tmpa = [AR.alloc([128, 512]) for _ in range(2)]
        tmpb = [AR.alloc([128, 512]) for _ in range(2)]
        lnm = AR.alloc([128, 512]); lnr = AR.alloc([128, 512])
        g1rep = [AR.alloc([128, 1024]) for _ in range(2)]
        dg = AR.alloc([128, 128])
        for i in range(2):
            memset("dve", upA[i], 0.0, [bf("a_upA%d" % i)])
            memset("dve", upB[i], 0.0, [bf("a_upB%d" % i)])
        def glu_proj(ct, half):
            (pa, pab), (pg, pgb) = getF(), getF()
            for kc in range(8):
                mm(pa, wa[:, kc, ct * 128:(ct + 1) * 128], xnT[:, kc, half * 512:(half + 1) * 512], kc == 0, kc == 7, [wab, XN[half]], [pab])
            for kc in range(8):
                mm(pg, wg[:, kc, ct * 128:(ct + 1) * 128], xnT[:, kc, half * 512:(half + 1) * 512], kc == 0, kc == 7, [wgb, XN[half]], [pgb])
            sgt = tmpa[half]
            act(sgt, pg, AF.Sigmoid, [pgb], [bf("a_tmpa%d" % half)])
            if half == 0:
                up, upn, L = upA[ct % 2], "a_upA%d" % (ct % 2), 256
            else:
                up, upn, L = upB[ct % 2], "a_upB%d" % (ct % 2), 64
            v3 = lambda a_, L=L: a_.rearrange("p (r c) -> p r c", c=L)
            tt("dve", up[:, :, 15:15 + L], v3(pa), v3(sgt), ALU.mult, [pab, bf("a_tmpa%d" % half)], [bf(upn)])
            if half == 0:
                dw, dwb = dgw[ct % 2], bf("a_dgw%d" % (ct % 2))
                for j in range(31):
                    if j % 2:
                        act(dw[:, j, :], ident16, AF.Copy, [bf("ident16"), bf("pvec")], [dwb], scale=pv("conv_w", j * 4 + ct))
                    else:
                        ts("dve", dw[:, j, :], ident16, pv("conv_w", j * 4 + ct), None, ALU.mult, None, [bf("ident16"), bf("pvec")], [dwb])

        def conv_mm(ct, half):
            if half == 0:
                up, upn, L = upA[ct % 2], "a_upA%d" % (ct % 2), 256
            else:
                up, upn, L = upB[ct % 2], "a_upB%d" % (ct % 2), 64
            dw, dwb = dgw[ct % 2], bf("a_dgw%d" % (ct % 2))
            pcv, pcvb = getF()
            for j in range(31):
                mm(pcv, dw[:, j, :], up[:, :, j:j + L], j == 0, j == 30, [dwb, bf(upn)], [pcvb])
            act(cv[:, ct, half * 512:(half + 1) * 512], pcv, AF.Identity, [pcvb, bf("pvec")], [bf("a_cv%d_%d" % (ct, half))], bias=pv("conv_b", ct))

        seq_c = [(ct, half) for ct in range(4) for half in range(2)]
        glu_proj(*seq_c[0])
        for i_, ch_ in enumerate(seq_c):
            if i_ + 1 < len(seq_c):
                glu_proj(*seq_c[i_ + 1])
            conv_mm(*ch_)
        for half in range(2):
            hs = slice(half * 512, (half + 1) * 512)
            (pm, pmb), (pq, pqb) = getF(), getF()
            for ct in range(4):
                mm(pm, cst[:, 6, :], cv[:, ct, hs], ct == 0, ct == 3, [bf("cst"), bf("a_cv%d_%d" % (ct, half))], [pmb])
            for ct in range(4):
                sqt = tmpb[ct % 2]
                act(sqt, cv[:, ct, hs], AF.Square, [bf("a_cv%d_%d" % (ct, half))], [bf("a_tmpb%d" % (ct % 2))])
                mm(pq, cst[:, 6, :], sqt, ct == 0, ct == 3, [bf("cst"), bf("a_tmpb%d" % (ct % 2))], [pqb])
            cp("act", lnm, pm, [pmb], [bf("a_lnm")])
            tt("dve", lnr, lnm, lnm, ALU.mult, [bf("a_lnm")], [bf("a_lnr")])
            tt("dve", lnr, pq, lnr, ALU.subtract, [pqb, bf("a_lnr")], [bf("a_lnr")])
            act(lnr, lnr, AF.Sqrt, [bf("a_lnr"), bf("epsr")], [bf("a_lnr")], bias=epsr[:, 2:3])
            recip(lnr, lnr, [bf("a_lnr")], [bf("a_lnr")])
            for ct in range(4):
                tq = tmpb[ct % 2]; tqb = bf("a_tmpb%d" % (ct % 2))
                tt("dve", tq, cv[:, ct, hs], lnm, ALU.subtract, [bf("a_cv%d_%d" % (ct, half)), bf("a_lnm")], [tqb])
                tt("dve", tq, tq, lnr, ALU.mult, [tqb, bf("a_lnr")], [tqb])
                act(ucT[:, ct, hs], tq, AF.Silu, [tqb, bf("pvec")], [bf("a_ucT")], bias=pv("cln_b", ct), scale=pv("cln_g", ct))
        dump("ucT", ucT.rearrange("p c n -> p (c n)"), [bf("a_ucT")])
        wr, wrb = loadw(wor_d.rearrange("(kc p) n -> p kc n", p=128), lambda w: w.rearrange("p (kc n) -> p kc n", kc=4), slot=3)
        wc, wcb = loadw(woc_d.rearrange("(kc p) n -> p kc n", p=128), lambda w: w.rearrange("p (kc n) -> p kc n", kc=4), slot=0)
        for g in range(2):
            wgr, wgrb = loadw(w_in_v[:, :, 2560 + g * 512:2560 + (g + 1) * 512], lambda w: w.rearrange("p (kc n) -> p kc n", kc=8), slot=1)
            wgc, wgcb = loadw(w_in_v[:, :, 3584 + g * 512:3584 + (g + 1) * 512], lambda w: w.rearrange("p (kc n) -> p kc n", kc=8), slot=2)
            for f4 in range(4):
                fo = g * 4 + f4
                for half in range(2):
                    hs = slice(half * 512, (half + 1) * 512)
                    (pr, prb), (pc, pcb), (pgr, pgrb), (pgc, pgcb) = getF(), getF(), getF(), getF()
                    for kc in range(4):
                        mm(pr, wr[:, kc, fo * 128:(fo + 1) * 128], outT16[:, kc, hs], kc == 0, kc == 3, [wrb, bf("outT16")], [prb])
                    for kc in range(4):
                        mm(pc, wc[:, kc, fo * 128:(fo + 1) * 128], ucT[:, kc, hs], kc == 0, kc == 3, [wcb, bf("a_ucT")], [pcb])
                    for kc in range(8):
                        mm(pgr, wgr[:, kc, f4 * 128:(f4 + 1) * 128], xnT[:, kc, hs], kc == 0, kc == 7, [wgrb, XN[half]], [pgrb])
                    for kc in range(8):
                        mm(pgc, wgc[:, kc, f4 * 128:(f4 + 1) * 128], xnT[:, kc, hs], kc == 0, kc == 7, [wgcb, XN[half]], [pgcb])
                    ta, tab, tb_, tbb = tmpa[half], bf("a_tmpa%d" % half), tmpb[half], bf("a_tmpb%d" % half)
                    act(ta, pgr, AF.Sigmoid, [pgrb], [tab])
                    act(tb_, pgc, AF.Sigmoid, [pgcb], [tbb])
                    tt("dve", ta, pr, ta, ALU.mult, [prb, tab], [tab])
                    tt("dve", tb_, pc, tb_, ALU.mult, [pcb, tbb], [tbb])
                    tt("dve", mergedT[:, fo, hs], ta, tb_, ALU.add, [tab, tbb], [bf("a_mergedT")])

        def bcast_rows(dst_list, col0, tag):
            for j in range(2):
                for hh in range(2):
                    p, pb = getF()
                    for k4 in range(4):
                        kc = hh * 4 + k4
                        ts("dve", dg, ident32, mod[:, col0 + kc, j:j + 1], None, ALU.mult, None, [bf("cst"), bf("mod")], [bf("a_dg")])
                        mm(p[:, k4 * 128:(k4 + 1) * 128], cst[:, 7, :], dg, True, True, [bf("cst"), bf("a_dg")], [pb])
                    cp("act", dst_list[j][:, hh * 512:(hh + 1) * 512], p, [pb], [bf("a_%s%d" % (tag, j))])

        bcast_rows(g1rep, 16, "g1rep")
        wo_v = wo_d.rearrange("(kc p) n -> p kc n", p=128)
        sp3_ = AR.ptr
        AR.ptr = cv_ptr
        xs16c = AR.alloc([128, 8, 1024], BF16)
        AR.ptr = dgw_ptr
        junkc = AR.alloc([128, 1024])
        AR.ptr = sp3_
        inherit(["a_xs16c_%d" % t_ for t_ in range(8)] + ["a_junkc"],
                ["a_cv%d_%d" % (c_, h_) for c_ in range(4) for h_ in range(2)] + ["a_dgw0", "a_dgw1"])
        wos = [loadw(wo_v[:, :, nh * 512:(nh + 1) * 512], lambda w: w.rearrange("p (kc n) -> p kc n", kc=8), slot=3 * nh) for nh in range(2)]
        for t in range(8):
            j = 0 if t < 4 else 1
            for nh in range(2):
                ns = slice(nh * 512, (nh + 1) * 512)
                wo, wob = wos[nh]
                p, pb = getF()
                for kc in range(8):
                    mm(p, mergedT[:, kc, t * 128:(t + 1) * 128], wo[:, kc, :], kc == 0, kc == 7, [bf("a_mergedT"), wob], [pb])
                ta, tab = tmpa[nh], bf("a_tmpa%d" % nh)
                tt("dve", ta, p, g1rep[j][:, ns], ALU.mult, [pb, bf("a_g1rep%d" % j)], [tab])
                tt("dve", x_sb[:, t, ns], ta, x_sb[:, t, ns], ALU.add, [tab, bf("xt%d" % t)], [bf("xt%d" % t)])
            sb_ = bf("a_ssn%d" % t)
            memset("dve", ss[:, t:t + 1], 0.0, [sb_])
            act(junkc, x_sb[:, t, :], AF.Square, [bf("xt%d" % t)], [bf("a_junkc"), sb_], accum=ss[:, t:t + 1])
            act(rstd[:, t:t + 1], ss[:, t:t + 1], AF.Sqrt, [sb_, bf("epsr")], [sb_], bias=epsr[:, 0:1], scale=1.0 / 1024)
            recip(rstd[:, t:t + 1], rstd[:, t:t + 1], [sb_], [sb_])
            if t % 2 == 0:
                ts("dve", xs16c[:, t, :], x_sb[:, t, :], rstd[:, t:t + 1], None, ALU.mult, None, [bf("xt%d" % t), sb_], [bf("a_xs16c_%d" % t)])
            else:
                act(xs16c[:, t, :], x_sb[:, t, :], AF.Copy, [bf("xt%d" % t), sb_], [bf("a_xs16c_%d" % t)], scale=rstd[:, t:t + 1])
            if t % 4 == 3:
                half = t // 4
                for kc in range(8):
                    p, pb = getB()
                    for q in range(4):
                        t_ = half * 4 + q
                        tr(p[:, q * 128:(q + 1) * 128], xs16c[:, t_, kc * 128:(kc + 1) * 128], ident16, [bf("a_xs16c_%d" % t_), bf("ident16")], [pb])
                    act(xnT[:, kc, half * 512:(half + 1) * 512], p[:, 0:512], AF.Identity, [pb, bf("A2"), bf("mod")],
                        [bf("xnT%d" % half)], bias=mod[:, 24 + kc, half:half + 1], scale=A2[:, kc, half:half + 1])
        chk("phaseC")

        new_phase()
        x_sb = AR.alloc([128, 8, 1024])
        h16T = AR.alloc([128, 32, 1024], BF16)
        g2rep = [AR.alloc([128, 1024]) for _ in range(2)]
        fgrep = AR.alloc([128, 1024])
        rtmp = [AR.alloc([128, 512]) for _ in range(2)]
        dg = AR.alloc([128, 128])
        ytile = [AR.alloc([128, 1024]) for _ in range(2)]
        S.dma("sp", fgrep, fgrep_d, writes=[bf("a_fgrep")])
        bcast_rows(g2rep, 40, "g2rep")
        w1_v = w1_d.rearrange("(kc p) n -> p kc n", p=128)
        k_ = 0
        for s_ in range(8):
            w1s, w1b = loadw(w1_v[:, :, s_ * 512:(s_ + 1) * 512], lambda w: w.rearrange("p (kc n) -> p kc n", kc=8))
            for m in range(4):
                ff = s_ * 4 + m
                for half in range(2):
                    hs = slice(half * 512, (half + 1) * 512)
                    p, pb = getF()
                    for kc in range(8):
                        mm(p, w1s[:, kc, m * 128:(m + 1) * 128], xnT[:, kc, hs], kc == 0, kc == 7, [w1b, XN[0], XN[1]], [pb])
                    rt, rtb = rtmp[k_ % 2], bf("a_rtmp%d" % (k_ % 2))
                    act(rt, p, AF.Relu, [pb], [rtb])
                    tt("dve", h16T[:, ff, hs], rt, rt, ALU.mult, [rtb], [bf("a_h16T%d" % ff)])
                    k_ += 1
        w2_v = w2_d.rearrange("(fc p) n -> p fc n", p=128)
        for nh in range(2):
            ns = slice(nh * 512, (nh + 1) * 512)
            slabs = [loadw(w2_v[:, 8 * q_:8 * q_ + 8, ns], lambda w: w.rearrange("p (fc n) -> p fc n", fc=8), slot=q_) for q_ in range(4)]
            for t in range(8):
                j = 0 if t < 4 else 1
                p, pb = getF()
                for ff in range(32):
                    w2s, w2b = slabs[ff // 8]
                    mm(p, h16T[:, ff, t * 128:(t + 1) * 128], w2s[:, ff % 8, :], ff == 0, ff == 31, [bf("a_h16T%d" % ff), w2b], [pb])
                rt, rtb = rtmp[k_ % 2], bf("a_rtmp%d" % (k_ % 2))
                tt("dve", rt, p, g2rep[j][:, ns], ALU.mult, [pb, bf("a_g2rep%d" % j)], [rtb])
                tt("dve", x_sb[:, t, ns], rt, x_sb[:, t, ns], ALU.add, [rtb, bf("xt%d" % t)], [bf("xt%d" % t)])
                k_ += 1
                if nh == 1:
                    yt, ytb = ytile[t % 2], bf("a_ytile%d" % (t % 2))
                    sb_ = bf("a_ssf%d" % t)
                    memset("dve", ss[:, 8 + t:9 + t], 0.0, [sb_])
                    act(yt, x_sb[:, t, :], AF.Square, [bf("xt%d" % t)], [ytb, sb_], accum=ss[:, 8 + t:9 + t])
                    act(rstd[:, 8 + t:9 + t], ss[:, 8 + t:9 + t], AF.Sqrt, [sb_, bf("epsr")], [sb_], bias=epsr[:, 0:1], scale=1.0 / 1024)
                    recip(rstd[:, 8 + t:9 + t], rstd[:, 8 + t:9 + t], [sb_], [sb_])
                    stt("dve", yt, x_sb[:, t, :], rstd[:, 8 + t:9 + t], fgrep, ALU.mult, ALU.mult, [bf("xt%d" % t), sb_, bf("a_fgrep")], [ytb])
                    S.dma("sp", y_d[t * 128:(t + 1) * 128, :], yt, reads=[ytb], is_output=True)


    try:
        _rest()
    except _Stop:
        pass
    S.emit()
    st.close()
    return nc


def prep_inputs(inp):
    f = lambda k: np.asarray(inp[k], np.float32)
    xp, xs = f("x_prompt"), f("x_sample")
    pv = np.zeros((128, NPV), np.float32)

    def put(name, arr):
        a = _fm(arr)
        pv[:, PV_OFF[name]:PV_OFF[name] + a.shape[1]] = a
    put("ada_b", f("ada_b")[0]); put("n1g", f("norm1_g")[0]); put("n2g", f("norm2_g")[0])
    put("mu_p", f("mu_prev")[0]); put("mu_n", f("mu_next")[0])
    put("a0f", f("iclr_a0")[0, 0]); put("a0b", f("iclr_a0")[0, 1])
    put("k_k", f("k_k")[0]); put("k_a", f("k_a")[0]); put("r_k", f("r_k")[0].reshape(-1))
    put("lnx_g", f("lnx_g")[0]); put("lnx_b", f("lnx_b")[0]); put("conv_b", f("conv_b")[0])
    put("cln_g", f("conv_ln_g")[0]); put("cln_b", f("conv_ln_b")[0])
    cw = f("conv_w")[0]
    cwp = np.concatenate([_fm(cw[j]) for j in range(31)], axis=1)
    pv[:, PV_OFF["conv_w"]:PV_OFF["conv_w"] + 124] = cwp
    shared = dict(
        pvec=pv,
        w0row=np.ascontiguousarray(f("decay_w0")[0].reshape(1, 1024)),
        fgrep=np.ascontiguousarray(np.broadcast_to(f("final_g")[None, :], (128, 1024))),
        ident=np.eye(128, dtype=np.float32),
        w1cat=np.ascontiguousarray(np.concatenate([f("decay_w1")[0, 0], f("decay_w1")[0, 1], f("iclr_a1")[0, 0],
                                                   f("iclr_a1")[0, 1], f("gate_g1")[0]], axis=1)),
        w2dec=np.ascontiguousarray(np.concatenate([f("decay_w2")[0, 0], f("decay_w2")[0, 1]], axis=0)),
        a2cat=np.ascontiguousarray(np.concatenate([f("iclr_a2")[0, 0], f("iclr_a2")[0, 1]], axis=0)),
        g2=f("gate_g2")[0],
        w_in=f("w_in")[0],
        w_out_rwkv=f("w_out_rwkv")[0], w_out_conv=f("w_out_conv")[0], w_o=f("w_o")[0],
        mlp_w1=f("mlp_w1")[0], mlp_w2=f("mlp_w2")[0],
    )
    cst, msk, id4, mk = _consts()
    shared.update(cst=cst, msk=msk, id4=id4, mk=mk)
    shared.pop("ident")
    in_maps = []
    for c in range(NCORES):
        b, q = c // 4, c % 4
        xmc = np.concatenate([xp[2 * c], xp[2 * c + 1], xs[b, q * 512:(q + 1) * 512]], axis=0)
        xhc = np.zeros((2, 1024), np.float32)
        hmk = np.zeros((128, 8, 2), np.float32)
        if q > 0:
            xhc[0] = xs[b, q * 512 - 1]; hmk[:, :, 0] = 1.0
        if q < 3:
            xhc[1] = xs[b, (q + 1) * 512]; hmk[:, :, 1] = 1.0
        cond = np.stack([f("c_ctx"), f("c")[0], f("c")[1]], axis=1)
        cT = np.ascontiguousarray(cond.reshape(8, 128, 3).transpose(1, 0, 2).reshape(128, 24))
        m_ada = dict(ada_w=np.ascontiguousarray(f("ada_w")[0][:, q * 1536:(q + 1) * 1536]),
                     adab=_fm(f("ada_b")[0][q * 1536:(q + 1) * 1536]),
                     selb=np.ascontiguousarray(np.broadcast_to(np.array([1.0 - b, float(b)], np.float32)[None, :], (128, 2))))
        s0T = np.stack([np.ascontiguousarray(
            f(nm)[b, 0].transpose(0, 2, 1).reshape(4, 2, 64, 64).transpose(1, 2, 0, 3).reshape(128, 4, 64))
            for nm in ("state_fwd", "state_bwd")], axis=0)
        m = dict(shared)
        m.update(m_ada)
        m["s0T"] = s0T
        sel = np.zeros((128, 8), np.float32)
        for j in range(4):
            sel[:, j] = 1.0 if j < q else 0.0
            sel[:, 4 + j] = 1.0 if j > q else 0.0
        m["sel"] = sel
        m["idh"] = np.ascontiguousarray(np.tile(np.eye(64, dtype=np.float32)[:, None, :], (2, 4, 1)).reshape(128, 256))
        m.update(xm=np.ascontiguousarray(xmc), xh=xhc, hmask=hmk.reshape(128, 16), condT=cT)
        in_maps.append(m)
    return in_maps


def kernel(**inputs):
    in_maps = prep_inputs(inputs)
    nc = build()
    res = run_bass_kernel_spmd(nc, in_maps, core_ids=list(range(NCORES)))
    y_prompt = np.zeros((16, 256, 1024), np.float32)
    y_sample = np.zeros((2, 2048, 1024), np.float32)
    nsf = np.zeros((16, 1, 8, 64, 64), np.float32)
    nsb = np.zeros((16, 1, 8, 64, 64), np.float32)
    for c, r in enumerate(res.results):
        b, q = c // 4, c % 4
        y = np.asarray(r["y"], np.float32)
        y_prompt[2 * c] = y[0:256]
        y_prompt[2 * c + 1] = y[256:512]
        y_sample[b, q * 512:(q + 1) * 512] = y[512:1024]
        stt_ = np.asarray(r["st"], np.float32).reshape(2, 2, 2, 64, 4, 64).transpose(0, 1, 4, 2, 5, 3).reshape(2, 2, 8, 64, 64)
        nsf[2 * c:2 * c + 2, 0] = stt_[:, 0]
        nsb[2 * c:2 * c + 2, 0] = stt_[:, 1]
    return (y_prompt, y_sample, nsf, nsb)
```

```python
import contextlib
import os
import numpy as np
import concourse.bass as bass
import concourse.mybir as mybir
from concourse.bass_utils import run_bass_kernel_spmd

F32 = mybir.dt.float32
BF16 = mybir.dt.bfloat16
AF = mybir.ActivationFunctionType
ALU = mybir.AluOpType
AX = mybir.AxisListType

SAME_ENGINE_SYNC = True
N_DMA_SEMS = 6
NCORES = 8
EM05 = float(np.exp(-0.5))


class Buf:
    __slots__ = ("name", "w", "r", "parts")

    def __init__(self, name=""):
        self.name = name
        self.w = None
        self.r = []
        self.parts = None


def _flat(bufs):
    out = []
    for b in bufs:
        if b.parts:
            out.extend(b.parts)
        else:
            out.append(b)
    return out


class Sched:
    ENGS = ("pe", "act", "dve", "pool", "sp")

    def __init__(self, nc):
        self.nc = nc
        self.ops = {e: [] for e in self.ENGS}
        self.dma_rr = {e: 0 for e in self.ENGS}
        self.dma_hist = {e: [[] for _ in range(N_DMA_SEMS)] for e in self.ENGS}
        self.out_dmas = []

    def _deps(self, reads, writes):
        reads, writes = _flat(reads), _flat(writes)
        deps = []
        for b in reads:
            if b.w is not None:
                deps.append(b.w)
        for b in writes:
            if b.w is not None:
                deps.append(b.w)
            deps.extend(b.r)
        return deps

    def _commit(self, ref, reads, writes):
        reads, writes = _flat(reads), _flat(writes)
        for b in reads:
            b.r.append(ref)
        for b in writes:
            b.w = ref
            b.r = []

    def op(self, eng, fn, reads=(), writes=()):
        deps = self._deps(reads, writes)
        idx = len(self.ops[eng])
        self.ops[eng].append(dict(kind="op", fn=fn, deps=deps, sig=False, cnt=None))
        ref = ("op", eng, idx)
        self._commit(ref, reads, writes)
        return ref

    def dma(self, eng, out, in_, reads=(), writes=(), is_output=False):
        deps = self._deps(reads, writes)
        k = self.dma_rr[eng]
        self.dma_rr[eng] = (k + 1) % N_DMA_SEMS
        hist = self.dma_hist[eng][k]
        if hist:
            deps.append(hist[-1])
        val = 16 * (len(hist) + 1)
        ref = ("dma", eng, k, val)
        hist.append(ref)
        self.ops[eng].append(dict(kind="dma", out=out, in_=in_, deps=deps, semk=k, val=val))
        self._commit(ref, reads, writes)
        if is_output:
            self.out_dmas.append(ref)
        return ref

    def coll(self, fn, reads=(), writes=()):
        deps = self._deps(reads, writes)
        self.n_coll = getattr(self, "n_coll", 0) + 1
        ref = ("dma", "pool", N_DMA_SEMS, self.n_coll)
        self.ops["pool"].append(dict(kind="coll", fn=fn, deps=deps))
        self._commit(ref, reads, writes)
        return ref

    def emit(self):
        nc = self.nc

        def skip_same(d, e):
            return d[1] == e and (not SAME_ENGINE_SYNC or e == "pe")

        for e in self.ENGS:
            for o in self.ops[e]:
                last = {}
                for d in o["deps"]:
                    if d[0] == "op" and not skip_same(d, e):
                        if d[2] > last.get(d[1], -1):
                            last[d[1]] = d[2]
                o["last"] = last
                for pe_, idx_ in last.items():
                    self.ops[pe_][idx_]["sig"] = True
        for e in self.ENGS:
            c = 0
            for o in self.ops[e]:
                if o["kind"] == "op" and o["sig"]:
                    c += 1
                    o["cnt"] = c
        with contextlib.ExitStack() as st:
            esem = {e: st.enter_context(nc.semaphore("s_" + e)) for e in self.ENGS}
            dsem = {e: [st.enter_context(nc.semaphore("d_%s%d" % (e, k))) for k in range(N_DMA_SEMS + 1)]
                    for e in ("sp", "act", "pool")}
            block = st.enter_context(nc.Block())
            sched = self

            def run(e, eng):
                waited = {}
                for o in sched.ops[e]:
                    need = {}
                    for pe_, idx_ in o["last"].items():
                        need[("op", pe_)] = sched.ops[pe_][idx_]["cnt"]
                    for d in o["deps"]:
                        if d[0] != "op":
                            key = ("dma", d[1], d[2])
                            if d[3] > need.get(key, 0):
                                need[key] = d[3]
                    for key, v in need.items():
                        if waited.get(key, 0) >= v:
                            continue
                        waited[key] = v
                        s = esem[key[1]] if key[0] == "op" else dsem[key[1]][key[2]]
                        eng.wait_ge(s, v)
                    if o["kind"] == "op":
                        ins = o["fn"](eng)
                        if o["sig"]:
                            ins.then_inc(esem[e], 1)
                    elif o["kind"] == "coll":
                        o["fn"](eng).then_inc(dsem["pool"][N_DMA_SEMS])
                    else:
                        eng.dma_start(out=o["out"], in_=o["in_"]).then_inc(dsem[e][o["semk"]], 16)
                if e == "sp":
                    for ref in sched.out_dmas:
                        eng.wait_ge(dsem[ref[1]][ref[2]], ref[3])

            block.tensor(lambda eng: run("pe", eng))
            block.scalar(lambda eng: run("act", eng))
            block.vector(lambda eng: run("dve", eng))
            block.gpsimd(lambda eng: run("pool", eng))
            block.sync(lambda eng: run("sp", eng))


PV_FIELDS = [("ada_b", 48), ("n1g", 8), ("n2g", 8), ("mu_p", 12), ("mu_n", 12), ("a0f", 4), ("a0b", 4),
             ("k_k", 4), ("k_a", 4), ("r_k", 4), ("lnx_g", 4), ("lnx_b", 4), ("conv_b", 4), ("cln_g", 4),
             ("cln_b", 4), ("conv_w", 124)]
PV_OFF = {}
_o = 0
for _n, _c in PV_FIELDS:
    PV_OFF[_n] = _o
    _o += _c
NPV = _o


def _fm(v):
    v = np.asarray(v, np.float32).reshape(-1)
    return np.ascontiguousarray(v.reshape(-1, 128).T)


def _consts():
    idx = np.arange(128)
    s, t = idx[:, None], idx[None, :]
    cst = np.zeros((128, 8, 128), np.float32)
    cst[:, 0] = np.eye(128)
    cst[:, 1] = -EM05 * (s <= t)
    cst[:, 2] = -EM05 * (s < t)
    cst[:, 3] = -EM05 * (s >= t)
    cst[:, 4] = -EM05 * (s > t)
    cst[:, 5] = ((s // 64) == (t // 64))
    cst[:, 6] = 1.0 / 512
    cst[:, 7] = 1.0
    msk = np.zeros((128, 2, 2, 384), np.float32)
    for d in range(2):
        strict = (s < t) if d == 0 else (s > t)
        incl = (s <= t) if d == 0 else (s >= t)
        msk[:, d, :, 0:128] = strict[:, None, :]
        msk[:, d, :, 128:256] = incl[:, None, :]
        msk[:, d, :, 256:384] = strict.T[:, None, :]
    id4 = np.zeros((128, 2, 128), np.float32)
    id4[:] = np.eye(128)[:, None, :]
    mk = np.zeros((128, 4, 2, 128), np.float32)
    mk[:, 0] = (s // 16 == t // 16)[:, None, :]
    for li, b in enumerate((16, 32, 64)):
        mk[:, 1 + li] = ((s // (2 * b) == t // (2 * b)) & (s // b != t // b))[:, None, :]
    return cst.reshape(128, 1024), msk.reshape(128, 1536), id4.reshape(128, 256), mk.reshape(128, 1024)


class Arena:
    def __init__(self, t, words):
        self.t, self.words, self.ptr = t, words, 0

    def alloc(self, shape, dt=F32):
        free = int(np.prod(shape[1:]))
        words = free if dt == F32 else (free + 1) // 2
        assert self.ptr + words <= self.words, ("arena overflow", self.ptr, words, self.words)
        ap = self.t[0:shape[0], self.ptr:self.ptr + words]
        self.ptr += words
        if dt == BF16:
            ap = ap.bitcast(BF16)
        if len(shape) == 3:
            ap = ap.rearrange("p (a b) -> p a b", b=shape[2])
        elif len(shape) == 4:
            ap = ap.rearrange("p (a b c) -> p a b c", b=shape[2], c=shape[3])
        elif len(shape) == 5:
            ap = ap.rearrange("p (a b c d) -> p a b c d", b=shape[2], c=shape[3], d=shape[4])
        return ap


def build(dbg=(), stop_after=None):
    nc = bass.Bass("TRN2", target_bir_lowering=False)
    S = Sched(nc)
    st = contextlib.ExitStack()

    def din(name, shape, dt=F32):
        return nc.dram_tensor(name, list(shape), dt, kind="ExternalInput").ap()

    def dout(name, shape):
        return nc.dram_tensor(name, list(shape), F32, kind="ExternalOutput").ap()

    def sb(name, shape, dt=F32):
        t = st.enter_context(nc.sbuf_tensor(name, list(shape), dt))
        return t[:]

    xm = din("xm", [1024, 1024])
    xh = din("xh", [2, 1024])
    hmask = din("hmask", [128, 16])
    condT = din("condT", [128, 24])
    pvec_d = din("pvec", [128, NPV])
    w0row_d = din("w0row", [1, 1024])
    fgrep_d = din("fgrep", [128, 1024])
    cst_d = din("cst", [128, 1024])
    msk_d = din("msk", [128, 1536])
    id4_d = din("id4", [128, 256])
    mk_d = din("mk", [128, 1024])
    s0T_d = din("s0T", [2, 128, 4, 64])
    sel_d = din("sel", [128, 8])
    idh_d = din("idh", [128, 256])
    bounce_d = nc.dram_tensor("bounce", [128, 1024], F32).ap()
    gath_d = nc.dram_tensor("gath", [512, 1024], F32).ap()
    w1cat_d = din("w1cat", [1024, 384])
    w2dec_d = din("w2dec", [128, 512])
    a2cat_d = din("a2cat", [128, 512])
    g2_d = din("g2", [128, 512])
    ada_w_d = din("ada_w", [1024, 1536])
    adab_d = din("adab", [128, 12])
    selb_d = din("selb", [128, 2])
    abounce_d = nc.dram_tensor("abounce", [128, 36], F32).ap()
    agath_d = nc.dram_tensor("agath", [512, 36], F32).ap()
    w_in_d = din("w_in", [1024, 4608])
    wor_d = din("w_out_rwkv", [512, 1024])
    woc_d = din("w_out_conv", [512, 1024])
    wo_d = din("w_o", [1024, 1024])
    w1_d = din("mlp_w1", [1024, 4096])
    w2_d = din("mlp_w2", [4096, 1024])
    y_d = dout("y", [1024, 1024])
    st_d = dout("st", [2, 2, 128, 4, 64])
    dbg_d = {n: dout("dbg_" + n, shp) for n, shp in dbg}

    xnT = sb("xnT", [128, 8, 1024], BF16)
    xnTh = sb("xnTh", [128, 8, 2], BF16)
    pvec = sb("pvec_sb", [128, NPV])
    hm = sb("hm", [128, 16])
    cT = sb("cT", [128, 24]); scT = sb("scT", [128, 24])
    adab = sb("adab_sb", [128, 12]); selb = sb("selb_sb", [128, 2])
    mod = sb("mod", [128, 48, 2])
    A1 = sb("A1", [128, 8, 2]); A2 = sb("A2", [128, 8, 2])
    cst = sb("cst_sb", [128, 8, 128])
    ident32 = cst[:, 0, :]
    blk1 = cst[:, 5, :]
    ident16 = sb("ident16", [128, 128], BF16)
    id4 = sb("id4_sb", [128, 2, 128], BF16)
    msk = sb("msk_sb", [128, 2, 2, 384], BF16)
    mk = sb("mk_sb", [128, 4, 2, 128], BF16)
    w0row = sb("w0row_sb", [1, 1024])
    ones32 = sb("ones32", [1, 128])
    ss = sb("ss", [128, 16]); rstd = sb("rstd", [128, 16])
    epsr = sb("epsr", [128, 4])
    c0v = sb("c0v", [128, 12])
    a0h = sb("a0h", [128, 8]); kah = sb("kah", [128, 4])
    wring = [sb("wring%d" % i, [128, 4096], BF16) for i in range(4)]
    wringb = [Buf("wring%d" % i) for i in range(4)]
    w2dec = sb("w2dec_sb", [128, 512], BF16); a2cat = sb("a2cat_sb", [128, 512], BF16); g2 = sb("g2w", [128, 512], BF16)
    hTd = sb("hTd", [128, 1024], BF16); hTi = sb("hTi", [128, 1024], BF16); hTg = sb("hTg", [128, 1024], BF16)
    outT16 = sb("outT16", [128, 4, 1024], BF16)
    selv = sb("selv", [128, 8])
    hsel = sb("hsel", [128, 2])
    idh = sb("idh_sb", [128, 4, 64])
    Hin = sb("Hin", [128, 2, 4, 64])
    ARENA_WORDS = 31400
    arena_t = sb("arena", [128, ARENA_WORDS])
    AR = Arena(arena_t, ARENA_WORDS)

    psF = [st.enter_context(nc.psum_tensor("psF%d" % i, [128, 512], F32))[:] for i in range(6)]
    psB = [st.enter_context(nc.psum_tensor("psB%d" % i, [128, 512], BF16))[:] for i in range(2)]
    psFb = [Buf("psF%d" % i) for i in range(6)]
    psBb = [Buf("psB%d" % i) for i in range(2)]
    rr = {"F": 0, "B": 0, "W": 0}

    psHb = [Buf("psH%d" % i) for i in range(12)]
    for i in range(6):
        psFb[i].parts = [psHb[i], psHb[i + 6]]
    rr["H"] = 0

    def getF():
        i = rr["F"]; rr["F"] = (i + 1) % 6
        return psF[i], psFb[i]

    def getH():
        k = rr["H"]; rr["H"] = (k + 1) % 12
        return psF[k % 6][:, (k // 6) * 256:(k // 6) * 256 + 256], psHb[k]

    def getB():
        i = rr["B"]; rr["B"] = (i + 1) % 2
        return psB[i], psBb[i]

    def pv(name, i=0, n=1):
        o = PV_OFF[name] + i
        return pvec[:, o:o + n]

    def mm(out, lhsT, rhs, start, stop, r, w, tp=None):
        kw = {} if tp is None else dict(tile_position=tp)
        return S.op("pe", lambda e: e.matmul(out, lhsT, rhs, start=start, stop=stop, **kw), reads=r, writes=w)

    def tr(out, in_, idn, r, w):
        return S.op("pe", lambda e: e.transpose(out, in_, idn), reads=r, writes=w)

    def act(out, in_, func, r, w, bias=None, scale=None, accum=None):
        kw = {}
        if bias is not None: kw["bias"] = bias
        if scale is not None: kw["scale"] = scale
        if accum is not None: kw["accum_out"] = accum
        return S.op("act", lambda e: e.activation(out, in_, func, **kw), reads=r, writes=w)

    def tt(eng, out, a, b, op, r, w):
        return S.op(eng, lambda e: e.tensor_tensor(out, a, b, op), reads=r, writes=w)

    def ts(eng, out, a, s1, s2, op0, op1, r, w):
        if op1 is None:
            return S.op(eng, lambda e: e.tensor_scalar(out, a, s1, None, op0), reads=r, writes=w)
        return S.op(eng, lambda e: e.tensor_scalar(out, a, s1, s2, op0, op1), reads=r, writes=w)

    def stt(eng, out, a, s, b, op0, op1, r, w):
        return S.op(eng, lambda e: e.scalar_tensor_tensor(out, a, s, b, op0, op1), reads=r, writes=w)

    def cp(eng, out, in_, r, w):
        if eng == "act":
            return S.op("act", lambda e: e.copy(out, in_), reads=r, writes=w)
        return S.op(eng, lambda e: e.tensor_copy(out, in_), reads=r, writes=w)

    def recip(out, in_, r, w):
        return S.op("dve", lambda e: e.reciprocal(out, in_), reads=r, writes=w)

    def memset(eng, ap, val, w):
        return S.op(eng, lambda e: e.memset(ap, val), writes=w)

    B = {}
    carry = {"refs": []}

    def bf(name):
        if name not in B:
            b = Buf(name)
            b.r = list(carry["refs"])
            B[name] = b
        return B[name]

    def new_phase():
        refs = set()
        for b in list(B.values()) + psHb + psBb + wringb:
            if b.w is not None:
                refs.add(b.w)
            refs.update(b.r)
        best = {}
        for rf in refs:
            key = rf[:2] if rf[0] == "op" else rf[:3]
            val = rf[2] if rf[0] == "op" else rf[3]
            if key not in best or val > (best[key][2] if rf[0] == "op" else best[key][3]):
                best[key] = rf
        carry["refs"] = list(best.values())
        for wb_ in wringb:
            wb_.r = list(wb_.r) + list(carry["refs"])
        AR.ptr = 0
        for k in [k for k in B if k.startswith("a_")]:
            del B[k]

    def dump(name, ap, r):
        if name in dbg_d:
            S.dma("pool", dbg_d[name], ap, reads=r, is_output=True)

    def loadw(src_ap, view_fn, slot=None):
        if slot is None:
            i = rr["W"]; rr["W"] = (i + 1) % 4
        else:
            i = slot
        v = view_fn(wring[i])
        S.dma("pool", v, src_ap, writes=[wringb[i]])
        return v, wringb[i]

    S.dma("sp", pvec, pvec_d, writes=[bf("pvec")])
    S.dma("sp", cT, condT, writes=[bf("cT")])
    S.dma("sp", hm, hmask, writes=[bf("hm")])
    S.dma("sp", cst.rearrange("p a b -> p (a b)"), cst_d, writes=[bf("cst")])
    S.dma("sp", w0row, w0row_d, writes=[bf("w0row")])
    S.dma("sp", Hin[:, 0], s0T_d[0], writes=[bf("Hin0")])
    S.dma("sp", Hin[:, 1], s0T_d[1], writes=[bf("Hin1")])
    S.dma("sp", selv, sel_d, writes=[bf("selv")])
    S.dma("sp", idh.rearrange("p a b -> p (a b)"), idh_d, writes=[bf("idh")])
    S.dma("pool", ident16, cst_d[:, 0:128], writes=[bf("ident16")])
    S.dma("pool", id4.rearrange("p a b -> p (a b)"), id4_d, writes=[bf("id4")])
    S.dma("pool", msk.rearrange("p a b c -> p (a b c)"), msk_d, writes=[bf("msk")])
    S.dma("pool", mk.rearrange("p a b c -> p (a b c)"), mk_d, writes=[bf("mk")])
    S.dma("pool", w2dec, w2dec_d, writes=[bf("w2dec")])
    S.dma("pool", a2cat, a2cat_d, writes=[bf("a2cat")])
    S.dma("pool", g2, g2_d, writes=[bf("g2")])
    x_sb = AR.alloc([128, 8, 1024])
    xs16 = AR.alloc([128, 8, 1024], BF16)
    junk = AR.alloc([128, 1024])
    xh_sb = AR.alloc([2, 1024]); xh16 = AR.alloc([2, 1024], BF16); xht = AR.alloc([128, 8, 2])
    S.dma("sp", xh_sb, xh, writes=[bf("a_xh")])
    for t in range(8):
        S.dma("sp", x_sb[:, t, :], xm[t * 128:(t + 1) * 128, :], writes=[bf("a_x%d" % t)])
    memset("dve", epsr[:, 0:1], 1e-6, [bf("epsr")])
    memset("dve", epsr[:, 1:2], 64e-5, [bf("epsr")])
    memset("dve", epsr[:, 2:3], 1e-5, [bf("epsr")])
    memset("dve", ss, 0.0, [bf("ss")])
    memset("dve", ones32, 1.0, [bf("ones32")])
    memset("dve", hsel, 0.0, [bf("hsel")])
    memset("dve", hsel[0:64, 0:1], 1.0, [bf("hsel")])
    memset("dve", hsel[64:128, 1:2], 1.0, [bf("hsel")])
    ts("dve", c0v, pv("mu_p", 0, 12), -1.0, 1.0, ALU.mult, ALU.add, [bf("pvec")], [bf("c0v")])
    tt("dve", c0v, c0v, pv("mu_n", 0, 12), ALU.subtract, [bf("c0v"), bf("pvec")], [bf("c0v")])
    ts("dve", a0h, pv("a0f", 0, 8), 0.5, None, ALU.mult, None, [bf("pvec")], [bf("a0h")])
    ts("dve", kah, pv("k_a", 0, 4), 0.5, None, ALU.mult, None, [bf("pvec")], [bf("kah")])

    def rmsnorm_T(x_sb, xs16, junk, Aw, Awname, shc, xn="a_x", part=None):
        if part in (None, 1):
            memset("dve", ss[:, 0:8], 0.0, [bf("ss")])
            for t in range(8):
                act(junk, x_sb[:, t, :], AF.Square, [bf(xn + "%d" % t)], [bf("a_junk"), bf("ss")], accum=ss[:, t:t + 1])
            act(rstd[:, 0:8], ss[:, 0:8], AF.Sqrt, [bf("ss"), bf("epsr")], [bf("rstd")], bias=epsr[:, 0:1], scale=1.0 / 1024)
            recip(rstd[:, 0:8], rstd[:, 0:8], [bf("rstd")], [bf("rstd")])
            for t in range(8):
                if t % 2 == 0:
                    ts("dve", xs16[:, t, :], x_sb[:, t, :], rstd[:, t:t + 1], None, ALU.mult, None,
                       [bf(xn + "%d" % t), bf("rstd")], [bf("a_xs16_%d" % t)])
                else:
                    act(xs16[:, t, :], x_sb[:, t, :], AF.Copy, [bf(xn + "%d" % t), bf("rstd")], [bf("a_xs16_%d" % t)], scale=rstd[:, t:t + 1])
        if part in (None, 2):
            for kc in range(8):
                for half in range(2):
                    p, pb = getB()
                    for q in range(4):
                        t = half * 4 + q
                        tr(p[:, q * 128:(q + 1) * 128], xs16[:, t, kc * 128:(kc + 1) * 128], ident16,
                           [bf("a_xs16_%d" % t), bf("ident16")], [pb])
                    act(xnT[:, kc, half * 512:(half + 1) * 512], p[:, 0:512], AF.Identity, [pb, bf(Awname), bf("mod")],
                        [bf("xnT%d" % half)], bias=mod[:, shc + kc, half:half + 1], scale=Aw[:, kc, half:half + 1])

    wl1v, wl1b = loadw(w1cat_d.rearrange("(kc p) n -> p kc n", p=128),
                     lambda w: w[:, 0:3072].rearrange("p (kc n) -> p kc n", kc=8))
    w_in_v = w_in_d.rearrange("(kc p) n -> p kc n", p=128)
    wrkv = []
    for i in range(3):
        v, b_ = loadw(w_in_v[:, :, i * 512:(i + 1) * 512], lambda w: w.rearrange("p (kc n) -> p kc n", kc=8))
        wrkv.append((v, b_))

    S.dma("sp", adab, adab_d, writes=[bf("adab")])
    S.dma("sp", selb, selb_d, writes=[bf("selb")])
    act(scT, cT, AF.Silu, [bf("cT")], [bf("scT")])
    ada_v = ada_w_d.rearrange("(kc p) n -> p kc n", p=128)
    adaw = AR.alloc([128, 8, 1536])
    S.dma("sp", adaw[:, 0:4, :], ada_v[:, 0:4, :], writes=[bf("a_adaw0")])
    S.dma("act", adaw[:, 4:8, :], ada_v[:, 4:8, :], writes=[bf("a_adaw1")])
    modrow = AR.alloc([3, 1536])
    for nchunk in range(3):
        pr_, prb_ = getF()
        for kc in range(8):
            mm(pr_[0:3, :], scT[:, kc * 3:kc * 3 + 3], adaw[:, kc, nchunk * 512:(nchunk + 1) * 512], kc == 0, kc == 7,
               [bf("a_adaw%d" % (kc // 4)), bf("scT")], [prb_])
        cp("act", modrow[:, nchunk * 512:(nchunk + 1) * 512], pr_[0:3, :], [prb_], [bf("a_modrow")])
    modp, modpb = getF()
    for f in range(12):
        mm(modp[:, f * 3:f * 3 + 3], modrow[:, f * 128:(f + 1) * 128], cst[0:3, 0, 0:3], True, True, [bf("a_modrow"), bf("cst")], [modpb])
    modpart = AR.alloc([128, 12, 3])
    for j in range(3):
        tt("dve", modpart[:, :, j], modp[:, 0:36].rearrange("p (f j) -> p f j", j=3)[:, :, j], adab, ALU.add, [modpb, bf("adab")], [bf("a_modpart")])
    S.dma("pool", abounce_d, modpart.rearrange("p f j -> p (f j)"), reads=[bf("a_modpart")], writes=[bf("abounce")])
    S.coll(lambda en: en.collective_compute("AllGather", ALU.bypass, replica_groups=[[0, 1, 2, 3], [4, 5, 6, 7]],
                                            ins=[abounce_d.opt()], outs=[agath_d.opt()]),
           reads=[bf("abounce")], writes=[bf("agath")])
    modall = AR.alloc([128, 48, 3])
    S.dma("pool", modall.rearrange("p (r f) j -> p r (f j)", r=4), agath_d.rearrange("(r p) n -> p r n", p=128), reads=[bf("agath")], writes=[bf("a_modall")])
    rmsnorm_T(x_sb, xs16, junk, A1, "A1", 0, part=1)
    cp("dve", mod[:, :, 0], modall[:, :, 0], [bf("a_modall")], [bf("mod")])
    ts("dve", mod[:, :, 1], modall[:, :, 1], selb[:, 0:1], None, ALU.mult, None, [bf("a_modall"), bf("selb")], [bf("mod")])
    stt("dve", mod[:, :, 1], modall[:, :, 2], selb[:, 1:2], mod[:, :, 1], ALU.mult, ALU.add, [bf("a_modall"), bf("selb"), bf("mod")], [bf("mod")])
    for j in range(2):
        stt("dve", A1[:, :, j], mod[:, 8:16, j], 1.0, pv("n1g", 0, 8), ALU.add, ALU.mult, [bf("mod"), bf("pvec")], [bf("A1")])
        stt("dve", A2[:, :, j], mod[:, 32:40, j], 1.0, pv("n2g", 0, 8), ALU.add, ALU.mult, [bf("mod"), bf("pvec")], [bf("A2")])
    dump("mod", mod.rearrange("p f j -> p (f j)"), [bf("mod")])

    rmsnorm_T(x_sb, xs16, junk, A1, "A1", 0, part=2)
    memset("dve", ss[0:2, 8:9], 0.0, [bf("ssh")])
    act(junk[0:2, :], xh_sb, AF.Square, [bf("a_xh")], [bf("a_junk"), bf("ssh")], accum=ss[0:2, 8:9])
    act(rstd[0:2, 8:9], ss[0:2, 8:9], AF.Sqrt, [bf("ssh"), bf("epsr")], [bf("rstdh")], bias=epsr[0:2, 0:1], scale=1.0 / 1024)
    recip(rstd[0:2, 8:9], rstd[0:2, 8:9], [bf("rstdh")], [bf("rstdh")])
    ts("dve", xh16, xh_sb, rstd[0:2, 8:9], None, ALU.mult, None, [bf("a_xh"), bf("rstdh")], [bf("a_xh16")])
    p, pb = getB()
    for kc in range(8):
        tr(p[:, kc * 2:kc * 2 + 2], xh16[:, kc * 128:(kc + 1) * 128], ident16[0:2, 0:2], [bf("a_xh16"), bf("ident16")], [pb])
    for kc in range(8):
        act(xht[:, kc, :], p[:, kc * 2:kc * 2 + 2], AF.Identity, [pb, bf("A1"), bf("mod")], [bf("a_xht")],
            bias=mod[:, kc, 1:2], scale=A1[:, kc, 1:2])
    tt("dve", xnTh, xht, hm.rearrange("p (k j) -> p k j", j=2), ALU.mult, [bf("a_xht"), bf("hm")], [bf("xnTh")])
    XN = [bf("xnT0"), bf("xnT1")]
    dump("xnT", xnT.rearrange("p k n -> p (k n)"), XN)

    class _Stop(Exception):
        pass

    def chk(tag):
        if stop_after == tag:
            raise _Stop()

    def _rest():
        for mt, (dst, fn, nm) in enumerate(((hTd, AF.Tanh, "hTd"), (hTi, AF.Identity, "hTi"), (hTg, AF.Sigmoid, "hTg"))):
            for half in range(2):
                p, pb = getF()
                for kc in range(8):
                    mm(p, wl1v[:, kc, mt * 128:(mt + 1) * 128], xnT[:, kc, half * 512:(half + 1) * 512], kc == 0, kc == 7,
                       [wl1b, XN[half]], [pb])
                act(dst[:, half * 512:(half + 1) * 512], p, fn, [pb], [bf(nm)])
        dump("hTd", hTd, [bf("hTd")])
        chk("p3")

        new_phase()
        STO = dict(
            QT=AR.alloc([128, 4, 4, 2 * 128], BF16),
            PT=AR.alloc([128, 4, 4, 2 * 64], BF16),
            G=AR.alloc([128, 4, 4, 2 * 64]),
            Y0=AR.alloc([128, 4, 512]),
            Gam=AR.alloc([128, 4, 4, 2]),
            H0=AR.alloc([128, 4, 4, 2 * 64], BF16),
            bonus=AR.alloc([128, 4, 512], BF16),
            )
        rawA = [AR.alloc([128, 516])] * 2
        rawB = rawA
        rT16 = AR.alloc([128, 512], BF16); vT16 = AR.alloc([128, 512], BF16); kT = AR.alloc([128, 512])
        alias_ptr = AR.ptr
        rrk = AR.alloc([128, 512], BF16)
        kk = AR.alloc([128, 512])
        Vtok2 = [AR.alloc([128, 4, 128], BF16) for _ in range(2)]
        sg32 = AR.alloc([128, 4, 128])
        Ei = AR.alloc([128, 512]); Ex = AR.alloc([128, 512]); En = AR.alloc([128, 512])
        shtmp = Ex
        aT = AR.alloc([128, 512]); t1 = AR.alloc([128, 512]); bT = AR.alloc([128, 512])
        sq, rs = t1, bT
        kd = [AR.alloc([128, 512]) for _ in range(2)]
        AR16 = AR.alloc([128, 4, 256], BF16); BT16 = AR.alloc([128, 512], BF16); KT16 = AR.alloc([128, 512], BF16)
        AZ = AR.alloc([128, 4, 2, 128], BF16); Btok = AR.alloc([128, 4, 128], BF16); Ktok = AR.alloc([128, 4, 128], BF16)
        NGS = 4
        GB = []
        gb_ptr = {}
        for i_ in range(NGS):
            gb_ptr[i_] = AR.ptr
            if i_ == 3:
                gb3_ptr = AR.ptr
            if i_ < 2:
                mb_ = AR.alloc([128, 4, 2, 2, 128], BF16)
            else:
                mb_ = wring[0][:, (i_ - 2) * 2048:(i_ - 1) * 2048].rearrange("p (m x e c) -> p m x e c", m=4, x=2, e=2)
                bmb = bf("a_MB_%d" % i_)
                bmb.r = bmb.r + list(wringb[0].r) + ([wringb[0].w] if wringb[0].w is not None else [])
            g_ = dict(NL=AR.alloc([128, 2, 2, 128], BF16), ARB=AR.alloc([128, 2, 128], BF16), KA=AR.alloc([128, 2, 256], BF16),
                      MB=mb_,
                      RR=[AR.alloc([128, 2, 2, 128], BF16) for _ in range(2)], LN=[AR.alloc([128, 2, 2, 128], BF16) for _ in range(2)],
                      XX=AR.alloc([128, 2, 2, 128], BF16), WU=AR.alloc([128, 2, 128], BF16))
            GB.append(g_)
        AR16m = [AR.alloc([128, 4, 256], BF16) for _ in range(2)]
        HS = [dict(Hst=AR.alloc([128, 4, 64]), Hs16=AR.alloc([128, 4, 64], BF16), Htmp=AR.alloc([128, 4, 64]))]
        sp__ = AR.ptr
        AR.ptr = gb3_ptr
        for _ in range(3):
            HS.append(dict(Hst=AR.alloc([128, 4, 64]), Hs16=AR.alloc([128, 4, 64], BF16), Htmp=AR.alloc([128, 4, 64])))
        assert AR.ptr <= gb3_ptr + 2048
        AR.ptr = sp__
        GB3_NAMES = ["a_%s_3" % k_ for k_ in ("NL", "ARB", "KA", "RR0", "RR1", "LN0", "LN1", "XX", "WU")]
        GB01_NAMES = ["a_%s_%d" % (k_, i_) for i_ in range(2) for k_ in ("NL", "ARB", "KA", "MB", "RR0", "RR1", "LN0", "LN1", "XX", "WU")]
        OT = []
        sp2__ = AR.ptr
        for k_ in range(4):
            if k_ % 2 == 0:
                AR.ptr = gb_ptr[k_ // 2]
            y_ = AR.alloc([128, 512]); q_ = AR.alloc([128, 512]); n16_ = AR.alloc([128, 512], BF16); g_s = AR.alloc([128, 48])
            OT.append(dict(ytok=y_, ysq=q_, yn16=n16_, gst=g_s, ynT=q_.rearrange("p (c t) -> p c t", t=128)))
            assert AR.ptr <= gb_ptr[k_ // 2] + 3072
        AR.ptr = sp2__
        OT_NAMES = ["a_o%s_%d" % (k_, i_) for i_ in range(4) for k_ in ("ytok", "ysq", "yn16", "gst")]
        HS_NAMES = ["a_%s%d" % (k_, i_) for i_ in range(1, 4) for k_ in ("Hst", "Hs16", "Htmp")]

        def inherit(dst_names, src_names):
            refs = []
            for nm in src_names:
                b_ = bf(nm)
                if b_.w is not None:
                    refs.append(b_.w)
                refs.extend(b_.r)
            for nm in dst_names:
                b_ = bf(nm)
                b_.r = list(b_.r) + refs
        memset("dve", rawA[0], 0.0, [bf("a_rawA0")])

        free_banks = list(range(6))

        def take(nb):
            while len(free_banks) < nb:
                yield
            out = []
            for _ in range(nb):
                i = free_banks.pop(0)
                out.append((psF[i], psFb[i], i))
            return out

        def give(*idx):
            free_banks.extend(idx)

        freeB = [0, 1]

        def takeB():
            while not freeB:
                yield
            i = freeB.pop(0)
            return psB[i], psBb[i], i

        def group_gen(blk, ct, d, t, bs):
            sto = STO
            G_ = GB[bs]
            Vtok, VtB = Vtok2[ct % 2], bf("a_Vtok%d" % (ct % 2))
            n = lambda s_: bf("a_%s_%d" % (s_, bs))
            NL, ARB, KA, MB, RR, LN, XX, WU = (G_[k_] for k_ in ("NL", "ARB", "KA", "MB", "RR", "LN", "XX", "WU"))
            v2 = lambda q: q.rearrange("p (a b) -> p a b", b=128)
            v22 = lambda q: q.rearrange("p (x a b) -> p x a b", x=2, b=128)
            tcols = slice(t * 128, (t + 1) * 128)
            (pA, pAb, iA), (pK, pKb, iK), (pC, pCb, iC) = yield from take(3)
            for e in range(2):
                mm(pA[:, e * 256:(e + 1) * 256], BT16[:, tcols], AR16m[e][:, t, :], True, True, [bf("a_BT16"), bf("a_AR16m")], [pAb])
                mm(pC[:, e * 128:(e + 1) * 128], AR16m[e][:, t, 0:128], BT16[:, tcols], True, True, [bf("a_AR16m"), bf("a_BT16")], [pCb])
                mm(pK[:, e * 256:(e + 1) * 256], KT16[:, tcols], AR16m[e][:, t, :], True, True, [bf("a_KT16"), bf("a_AR16m")], [pKb])
            yield
            pA3 = pA.rearrange("p (a b) -> p a b", b=256)
            pK3 = pK.rearrange("p (a b) -> p a b", b=256)
            tt("dve", NL[:, 0], pA3[:, :, 0:128], msk[:, d, 0:2, 0:128], ALU.mult, [pAb, bf("msk")], [n("NL")])
            tt("dve", NL[:, 1], v2(pC[:, 0:256]), msk[:, d, 0:2, 256:384], ALU.mult, [pCb, bf("msk")], [n("NL")])
            tt("dve", KA, pK3, msk[:, d, 0:2, 0:256], ALU.mult, [pKb, bf("msk")], [n("KA")])
            tt("dve", ARB, pA3[:, :, 128:256], msk[:, d, 0:2, 128:256], ALU.mult, [pAb, bf("msk")], [n("ARB")])
            give(iA, iK, iC)
            yield
            tt("dve", MB.rearrange("p m x e c -> p m x (e c)"),
               NL.rearrange("p x e c -> p x (e c)").unsqueeze(1).to_broadcast([128, 4, 2, 256]),
               mk.rearrange("p m e c -> p m (e c)").unsqueeze(2).to_broadcast([128, 4, 2, 256]), ALU.mult, [n("NL"), bf("mk")], [n("MB")])
            tt("dve", RR[0], MB[:, 0], id4[:, 0:2].unsqueeze(1).to_broadcast([128, 2, 2, 128]), ALU.add, [n("MB"), bf("id4")], [n("RR0")])
            yield
            Nc, Lc, NLb = MB[:, 0, 0], MB[:, 0, 1], n("MB")
            pend = None
            for lev in range(4):
                nbk = (1 if lev < 3 else 0) + (1 if lev == 0 else 0) + (1 if pend is not None else 0)
                bks = list((yield from take(nbk)))
                if lev < 3:
                    p1, p1b, i1 = bks.pop(0)
                    for e in range(2):
                        mm(p1[:, e * 128:(e + 1) * 128], Nc[:, e, :], Lc[:, e, :], True, True, [NLb], [p1b])
                        mm(p1[:, 256 + e * 128:256 + (e + 1) * 128], Lc[:, e, :], Nc[:, e, :], True, True, [NLb], [p1b])
                if lev == 0:
                    pZ, pZb, iZ = bks.pop(0)
                    for e in range(2):
                        mm(pZ[:, e * 64:(e + 1) * 64], KA[:, e, 0:128], Vtok[:, t, e * 64:(e + 1) * 64], True, True, [n("KA"), VtB], [pZb])
                if pend is not None:
                    p2, p2b, i2 = bks.pop(0)
                    LNp, LNpb, c_, n_ = pend
                    for e in range(2):
                        mm(p2[:, e * 128:(e + 1) * 128], LNp[:, 0, e, :], RR[c_][:, 0, e, :], True, False, [LNpb, n("RR%d" % c_)], [p2b])
                        mm(p2[:, e * 128:(e + 1) * 128], ident16, RR[c_][:, 0, e, :], False, True, [bf("ident16"), n("RR%d" % c_)], [p2b])
                        mm(p2[:, 256 + e * 128:256 + (e + 1) * 128], LNp[:, 1, e, :], RR[c_][:, 1, e, :], True, False, [LNpb, n("RR%d" % c_)], [p2b])
                        mm(p2[:, 256 + e * 128:256 + (e + 1) * 128], ident16, RR[c_][:, 1, e, :], False, True, [bf("ident16"), n("RR%d" % c_)], [p2b])
                yield
                if pend is not None:
                    cp("act", RR[n_], v22(p2), [p2b], [n("RR%d" % n_)])
                    give(i2)
                    pend = None
                if lev < 3:
                    LNn, LNb = LN[lev % 2], n("LN%d" % (lev % 2))
                    cp("dve", LNn, v22(p1), [p1b], [LNb])
                    give(i1)
                    pend = (LNn, LNb, lev % 2, (lev + 1) % 2)
                    Nc, Lc, NLb = LNn[:, 1], LNn[:, 0], LNb
                if lev == 0:
                    cp("act", AZ[:, t, :, 64:128], pZ[:, 0:128].rearrange("p (e k) -> p e k", k=64), [pZb], [bf("a_AZ%d" % t)])
                    give(iZ)
                yield
            cur = 1
            for li in range(3):
                nx = 1 - cur
                D_, Dt_, Db_ = RR[cur][:, 0], RR[cur][:, 1], n("RR%d" % cur)
                O_, Ot_, Ob_ = MB[:, 1 + li, 0], MB[:, 1 + li, 1], n("MB")
                ((p1, p1b, i1),) = yield from take(1)
                for e in range(2):
                    mm(p1[:, e * 128:(e + 1) * 128], Ot_[:, e, :], D_[:, e, :], True, True, [Ob_, Db_], [p1b])
                    if li < 2:
                        mm(p1[:, 256 + e * 128:256 + (e + 1) * 128], O_[:, e, :], Dt_[:, e, :], True, True, [Ob_, Db_], [p1b])
                yield
                if li < 2:
                    cp("act", XX, v22(p1), [p1b], [n("XX")])
                else:
                    cp("act", XX[:, 0], v2(p1[:, 0:256]), [p1b], [n("XX")])
                give(i1)
                yield
                ((p2, p2b, i2),) = yield from take(1)
                for e in range(2):
                    mm(p2[:, e * 128:(e + 1) * 128], Dt_[:, e, :], XX[:, 0, e, :], True, False, [Db_, n("XX")], [p2b])
                    mm(p2[:, e * 128:(e + 1) * 128], ident16, D_[:, e, :], False, True, [bf("ident16"), Db_], [p2b])
                    if li < 2:
                        mm(p2[:, 256 + e * 128:256 + (e + 1) * 128], D_[:, e, :], XX[:, 1, e, :], True, False, [Db_, n("XX")], [p2b])
                        mm(p2[:, 256 + e * 128:256 + (e + 1) * 128], ident16, Dt_[:, e, :], False, True, [bf("ident16"), Db_], [p2b])
                yield
                if li < 2:
                    cp("act", RR[nx], v22(p2), [p2b], [n("RR%d" % nx)])
                else:
                    cp("act", RR[nx][:, 0], v2(p2[:, 0:256]), [p2b], [n("RR%d" % nx)])
                give(i2)
                yield
                cur = nx
            TT, TTb = RR[0][:, 0], n("RR0")
            ((pW, pWb, iW),) = yield from take(1)
            for e in range(2):
                mm(pW[:, e * 128:(e + 1) * 128], TT[:, e, :], AZ[:, t, e, :], True, True, [TTb, bf("a_AZ%d" % t), bf("a_AZ")], [pWb])
            yield
            cp("act", WU, v2(pW[:, 0:256]), [pWb], [n("WU")])
            give(iW)
            yield
            (pP, pPb, iP), (pF, pFb, iF) = yield from take(2)
            for e in range(2):
                es = slice(e * 64, e * 64 + 64)
                tpo = (0, 64) if e else None
                bk = Btok[:, t, e * 64:(e + 1) * 64]
                kkk = Ktok[:, t, e * 64:(e + 1) * 64]
                vv = Vtok[:, t, e * 64:(e + 1) * 64]
                mm(pP[es, 0:64], WU[:, e, 0:64], bk, True, True, [n("WU"), bf("a_Btok")], [pPb], tp=tpo)
                mm(pF[es, 0:64], bk, WU[:, e, 64:128], True, False, [n("WU"), bf("a_Btok")], [pFb], tp=tpo)
                mm(pF[es, 0:64], kkk, vv, False, True, [bf("a_Ktok"), VtB], [pFb], tp=tpo)
                mm(pF[es, 64:192], WU[:, e, 0:64], ARB[:, e, :], True, True, [n("WU"), n("ARB")], [pFb], tp=tpo)
                mm(pF[:, 192 + e * 64:192 + (e + 1) * 64], ARB[:, e, :], WU[:, e, 64:128], True, False, [n("WU"), n("ARB")], [pFb])
                mm(pF[:, 192 + e * 64:192 + (e + 1) * 64], KA[:, e, 128:256], vv, False, True, [n("KA"), VtB], [pFb])
            yield
            cp("act", sto["PT"][:, ct, t, d * 64:(d + 1) * 64], pP[:, 0:64], [pPb], [bf("a_PT")])
            ts("dve", sto["G"][:, ct, t, d * 64:(d + 1) * 64], pF[:, 0:64], sto["Gam"][:, ct, t, d:d + 1], None, ALU.mult, None, [pFb, bf("a_Gam")], [bf("a_G")])
            tt("dve", sto["QT"][:, ct, t, d * 128:(d + 1) * 128], pF[:, 64:192], AR16[:, t, 128:256], ALU.add, [pFb, bf("a_AR16")], [bf("a_QT")])
            ydst = sto["Y0"][:, t, ct * 128:(ct + 1) * 128]
            if d == 0:
                cp("dve", ydst, pF[:, 192:320], [pFb], [bf("a_Y0_%d" % t)])
            else:
                tt("dve", ydst, pF[:, 192:320], ydst, ALU.add, [pFb, bf("a_Y0_%d" % t)], [bf("a_Y0_%d" % t)])
            give(iP, iF)
            yield

        def take_now(nb):
            assert len(free_banks) >= nb, "PSUM bank pool exhausted outside a generator"
            out = []
            for _ in range(nb):
                i = free_banks.pop(0)
                out.append((psF[i], psFb[i], i))
            return out

        def run_gens(gens):
            alive = list(gens)
            while alive:
                for g_ in list(alive):
                    try:
                        next(g_)
                    except StopIteration:
                        alive.remove(g_)

        def stage_ct(blk, ct):
            c0 = 0 if blk == "A" else 512
            raw, rb = rawA[0], bf("a_rawA0")
            Vt, Vtb = Vtok2[ct % 2], bf("a_Vtok%d" % (ct % 2))
            for wi, (ft, dst, dn) in enumerate(((ct, rT16, "a_rT"), (4 + ct, kT, "a_kT"), (8 + ct, vT16, "a_vT"))):
                slab, slb = wrkv[ft // 4]
                fo = (ft % 4) * 128
                ((p, pb, ip),) = yield from take(1)
                for kc in range(8):
                    mm(p, slab[:, kc, fo:fo + 128], xnT[:, kc, c0:c0 + 512], kc == 0, kc == 7, [slb, XN[c0 // 512]], [pb])
                if blk == "A":
                    r3 = raw.rearrange("p (s c) -> p s c", c=258)
                    cp("act", r3[:, :, 1:257], p.rearrange("p (s c) -> p s c", c=256), [pb], [rb])
                    give(ip)
                    prev, main, nxt = r3[:, :, 0:256], r3[:, :, 1:257], r3[:, :, 2:258]
                    v3 = lambda a_: a_.rearrange("p (s c) -> p s c", c=256)
                else:
                    ((p2, p2b, ip2),) = yield from take(1)
                    for kc in range(8):
                        mm(p2[:, 0:2], slab[:, kc, fo:fo + 128], xnTh[:, kc, :], kc == 0, kc == 7, [slb, bf("xnTh")], [p2b])
                    cp("act", raw[:, 1:513], p, [pb], [rb])
                    cp("act", raw[:, 0:1], p2[:, 0:1], [p2b], [rb])
                    cp("act", raw[:, 513:514], p2[:, 1:2], [p2b], [rb])
                    give(ip, ip2)
                    prev, main, nxt = raw[:, 0:512], raw[:, 1:513], raw[:, 2:514]
                    v3 = lambda a_: a_
                yield
                act(v3(shtmp), main, AF.Identity, [rb, bf("c0v")], [bf("a_Ex")], scale=c0v[:, ft:ft + 1])
                yield
                stt("dve", v3(shtmp), prev, pv("mu_p", ft), v3(shtmp), ALU.mult, ALU.add, [rb, bf("pvec"), bf("a_Ex")], [bf("a_Ex")])
                stt("dve", v3(dst), nxt, pv("mu_n", ft), v3(shtmp), ALU.mult, ALU.add, [rb, bf("pvec"), bf("a_Ex")], [bf(dn)])
                yield
            act(sq, kT, AF.Square, [bf("a_kT"), bf("pvec")], [bf("a_t1")], scale=pv("k_k", ct))
            ((p, pb, ip),) = yield from take(1)
            mm(p, blk1, sq, True, True, [bf("cst"), bf("a_t1")], [pb])
            yield
            ts("dve", rs, p, 1e-24, None, ALU.max, None, [pb], [bf("a_bT")])
            give(ip)
            act(rs, rs, AF.Sqrt, [bf("a_bT")], [bf("a_bT")])
            yield
            recip(rs, rs, [bf("a_bT")], [bf("a_bT")])
            stt("dve", kk, kT, pv("k_k", ct), rs, ALU.mult, ALU.mult, [bf("a_kT"), bf("pvec"), bf("a_bT")], [bf("a_kk")])
            act(rrk, rT16, AF.Copy, [bf("a_rT"), bf("pvec")], [bf("a_rrk")], scale=pv("r_k", ct))
            p, pb, ib = yield from takeB()
            for t in range(4):
                tr(p[:, t * 128:(t + 1) * 128], vT16[:, t * 128:(t + 1) * 128], ident16, [bf("a_vT"), bf("ident16")], [pb])
            yield
            cp("act", Vt, p[:, 0:512].rearrange("p (a b) -> p a b", b=128), [pb], [Vtb])
            freeB.append(ib)
            yield

        def prep_first(blk, ct, d):
            c0 = 0 if blk == "A" else 512
            sto = STO
            ds = slice(d * 64, d * 64 + 64)
            tpd = (64, 0) if d else None
            ((p, pb, ip),) = yield from take(1)
            for t in range(4):
                mm(p[:, t * 128:(t + 1) * 128], hTd[ds, c0 + t * 128:c0 + (t + 1) * 128], w2dec[ds, ct * 128:(ct + 1) * 128],
                   True, False, [bf("hTd"), bf("w2dec")], [pb], tp=tpd)
                mm(p[:, t * 128:(t + 1) * 128], ones32[0:1, :], w0row[0:1, d * 512 + ct * 128:d * 512 + (ct + 1) * 128],
                   False, True, [bf("ones32"), bf("w0row")], [pb])
            yield
            act(sg32, p.rearrange("p (a b) -> p a b", b=128), AF.Tanh, [pb], [bf("a_sg32")], scale=0.5)
            give(ip)
            yield
            ts("dve", sg32, sg32, 0.5, 0.5, ALU.mult, ALU.add, [bf("a_sg32")], [bf("a_sg32")])
            yield
            (pI, pIb, iI), (pX, pXb, iX), (pa_, pab_, ia_) = yield from take(3)
            for t in range(4):
                mm(pI[:, t * 128:(t + 1) * 128], sg32[:, t, :], cst[:, 1 + 2 * d, :], True, True, [bf("a_sg32"), bf("cst")], [pIb])
                mm(pX[:, t * 128:(t + 1) * 128], sg32[:, t, :], cst[:, 2 + 2 * d, :], True, True, [bf("a_sg32"), bf("cst")], [pXb])
            mm(pa_, a2cat[ds, ct * 128:(ct + 1) * 128], hTi[ds, c0:c0 + 512], True, True, [bf("a2cat"), bf("hTi")], [pab_], tp=tpd)
            yield
            act(Ei, pI, AF.Exp, [pIb], [bf("a_Ei")])
            act(En, pI, AF.Exp, [pIb], [bf("a_En")], scale=-1.0)
            act(Ex, pX, AF.Exp, [pXb], [bf("a_Ex")])
            act(aT, pa_, AF.Tanh, [pab_, bf("a0h")], [bf("a_aT")], bias=a0h[:, d * 4 + ct:d * 4 + ct + 1], scale=0.5)
            give(iI, iX, ia_)
            yield
            lastc = 127 if d == 0 else 0
            cp("dve", sto["Gam"][:, ct, :, d], Ei.rearrange("p (t c) -> p t c", c=128)[:, :, lastc], [bf("a_Ei")], [bf("a_Gam")])
            ts("dve", t1, aT, -1.0, kah[:, ct:ct + 1], ALU.add, ALU.mult, [bf("a_aT"), bf("kah")], [bf("a_t1")])
            yield
            stt("dve", kd[d], t1, 1.0, kT, ALU.add, ALU.mult, [bf("a_t1"), bf("a_kT")], [bf("a_kd%d" % d)])
            stt("dve", bT, aT, 1.0, kk, ALU.add, ALU.mult, [bf("a_kk"), bf("a_aT")], [bf("a_bT")])
            yield

        def prep_second(blk, ct, d):
            v4 = lambda a_: a_.rearrange("p (t c) -> p t c", c=128)
            stt("dve", AR16[:, :, 0:128], v4(kk), -1.0, v4(Ex), ALU.mult, ALU.mult, [bf("a_kk"), bf("a_Ex")], [bf("a_AR16")])
            tt("dve", AR16[:, :, 128:256], v4(rT16), v4(Ei), ALU.mult, [bf("a_rT"), bf("a_Ei")], [bf("a_AR16")])
            stt("dve", BT16, bT, 0.5, En, ALU.mult, ALU.mult, [bf("a_bT"), bf("a_En")], [bf("a_BT16")])
            for e_ in range(2):
                act(AR16m[e_], AR16, AF.Copy, [bf("a_AR16"), bf("hsel")], [bf("a_AR16m")], scale=hsel[:, e_:e_ + 1])
            tt("dve", KT16, kd[d], En, ALU.mult, [bf("a_kd%d" % d), bf("a_En")], [bf("a_KT16")])

        def prep_late(blk, ct, d):
            p, pb, ib = yield from takeB()
            for t in range(4):
                tr(p[:, t * 128:(t + 1) * 128], AR16[:, t, 0:128], ident16, [bf("a_AR16"), bf("ident16")], [pb])
            yield
            for e_ in range(2):
                cp("act", AZ[:, :, e_, 0:64], p[:, 0:512].rearrange("p (t c) -> p t c", c=128)[:, :, e_ * 64:(e_ + 1) * 64], [pb], [bf("a_AZ")])
            freeB.append(ib)
            yield
            p, pb, ib = yield from takeB()
            for t in range(4):
                tr(p[:, t * 128:(t + 1) * 128], BT16[:, t * 128:(t + 1) * 128], ident16, [bf("a_BT16"), bf("ident16")], [pb])
            yield
            cp("dve", Btok, p[:, 0:512].rearrange("p (a b) -> p a b", b=128), [pb], [bf("a_Btok")])
            freeB.append(ib)
            yield
            p, pb, ib = yield from takeB()
            for t in range(4):
                tr(p[:, t * 128:(t + 1) * 128], KT16[:, t * 128:(t + 1) * 128], ident16, [bf("a_KT16"), bf("ident16")], [pb])
            yield
            cp("act", Ktok, p[:, 0:512].rearrange("p (a b) -> p a b", b=128), [pb], [bf("a_Ktok")])
            freeB.append(ib)
            yield

        def bonus_ct(ct):
            sto = STO
            tt("dve", t1, kd[0], kd[1], ALU.add, [bf("a_kd0"), bf("a_kd1")], [bf("a_t1")])
            tt("dve", t1, t1, rrk, ALU.mult, [bf("a_t1"), bf("a_rrk")], [bf("a_t1")])
            ((p, pb, ip),) = take_now(1)
            mm(p, blk1, t1, True, True, [bf("cst"), bf("a_t1")], [pb])
            tt("dve", sto["bonus"][:, ct, :], p, vT16, ALU.mult, [pb, bf("a_vT")], [bf("a_bonus")])
            give(ip)

        def first_gen(blk, ct, d):
            if d == 0:
                yield from stage_ct(blk, ct)
            yield from prep_first(blk, ct, d)

        def rwkv_block(blk):
            seq = [(ct, d) for ct in range(4) for d in range(2)]
            run_gens([first_gen(blk, 0, 0)])
            for i_, (ct, d) in enumerate(seq):
                prep_second(blk, ct, d)
                if d == 1:
                    bonus_ct(ct)
                gens = [prep_late(blk, ct, d)] + [group_gen(blk, ct, d, t_, j_) for j_, t_ in enumerate(range(4))]
                if i_ + 1 < len(seq):
                    gens.append(first_gen(blk, *seq[i_ + 1]))
                run_gens(gens)

        def recur_gen(blk, tiles, d, init, useG=True, saveH=True, hs=0, after=None):
            sto = STO
            Hst, Hs16, Htmp = HS[hs]["Hst"], HS[hs]["Hs16"], HS[hs]["Htmp"]
            HB, H16B, HTB = bf("a_Hst%d" % hs), bf("a_Hs16%d" % hs), bf("a_Htmp%d" % hs)
            if init is None:
                memset("dve", Hst, 0.0, [HB])
            else:
                cp("dve", Hst, init[0], [init[1]], [HB])
            for t in tiles:
                cp("act", Hs16, Hst, [HB], [H16B])
                if saveH:
                    cp("act", sto["H0"][:, :, t, d * 64:(d + 1) * 64], Hst, [HB], [bf("a_H0_%d_%d" % (t, d))])
                (p0, p0b, i0), (p1, p1b, i1) = yield from take(2)
                pe2 = [(p0, p0b), (p1, p1b)]
                for ct in range(4):
                    for e in range(2):
                        es = slice(e * 64, e * 64 + 64)
                        mm(pe2[e][0][es, ct * 64:(ct + 1) * 64], sto["PT"][es, ct, t, d * 64:(d + 1) * 64], Hs16[es, ct, :], True, True,
                           [bf("a_PT"), H16B], [pe2[e][1]], tp=(64, 64) if e else None)
                yield
                for e in range(2):
                    es = slice(e * 64, e * 64 + 64)
                    tt("dve", Htmp[es], pe2[e][0][es, 0:256].rearrange("p (a b) -> p a b", b=64), Hst[es], ALU.add, [pe2[e][1], HB], [HTB])
                give(i0, i1)
                yield
                for ct in range(4):
                    if useG:
                        stt("dve", Hst[:, ct, :], Htmp[:, ct, :], sto["Gam"][:, ct, t, d:d + 1], sto["G"][:, ct, t, d * 64:(d + 1) * 64],
                            ALU.mult, ALU.add, [HTB, bf("a_Gam"), bf("a_G")], [HB])
                    else:
                        ts("dve", Hst[:, ct, :], Htmp[:, ct, :], sto["Gam"][:, ct, t, d:d + 1], None, ALU.mult, None,
                           [HTB, bf("a_Gam")], [HB])
                yield
            if after is not None:
                yield from after(Hst, HB)

        def rwkv_out_gen(blk, t, k):
            sto = STO
            c0 = 0 if blk == "A" else 512
            T_ = OT[k]
            ytok, ysq, yn16, gst, ynT = T_["ytok"], T_["ysq"], T_["yn16"], T_["gst"], T_["ynT"]
            n = lambda s_: bf("a_o%s_%d" % (s_, k))
            (p0, p0b, i0), (p1, p1b, i1) = yield from take(2)
            py2 = [(p0, p0b), (p1, p1b)]
            for h in range(8):
                ct, e = h // 2, h % 2
                es = slice(e * 64, e * 64 + 64)
                for d in range(2):
                    mm(py2[e][0][:, ct * 64:(ct + 1) * 64], sto["QT"][es, ct, t, d * 128:(d + 1) * 128], sto["H0"][es, ct, t, d * 64:(d + 1) * 64],
                       d == 0, d == 1, [bf("a_QT"), bf("a_H0_%d_%d" % (t, d))], [py2[e][1]], tp=(64, 0) if e else None)
            yield
            yt4 = ytok.rearrange("p (c e v) -> p c e v", e=2, v=64)
            y04 = sto["Y0"][:, t, :].rearrange("p (c e v) -> p c e v", e=2, v=64)
            for e in range(2):
                tt("dve", yt4[:, :, e, :], py2[e][0][:, 0:256].rearrange("p (c v) -> p c v", v=64), y04[:, :, e, :], ALU.add,
                   [py2[e][1], bf("a_Y0_%d" % t)], [n("ytok")])
            give(i0, i1)
            yield
            y3 = ytok.rearrange("p (h v) -> p h v", v=64)
            S.op("dve", lambda e_, y3=y3, gst=gst: e_.reduce_sum(gst[:, 0:8], y3, AX.X), reads=[n("ytok")], writes=[n("gst")])
            act(ysq, ytok, AF.Square, [n("ytok")], [n("ysq")])
            yield
            S.op("dve", lambda e_, ysq=ysq, gst=gst: e_.reduce_sum(gst[:, 8:16], ysq.rearrange("p (h v) -> p h v", v=64), AX.X), reads=[n("ysq")], writes=[n("gst")])
            ts("dve", gst[:, 16:24], gst[:, 0:8], 1.0 / 64, None, ALU.mult, None, [n("gst")], [n("gst")])
            tt("dve", gst[:, 24:32], gst[:, 16:24], gst[:, 16:24], ALU.mult, [n("gst")], [n("gst")])
            stt("dve", gst[:, 32:40], gst[:, 8:16], 1.0 / 64, gst[:, 24:32], ALU.mult, ALU.subtract, [n("gst")], [n("gst")])
            yield
            act(gst[:, 40:48], gst[:, 32:40], AF.Sqrt, [n("gst"), bf("epsr")], [n("gst")], bias=epsr[:, 1:2])
            yield
            recip(gst[:, 40:48], gst[:, 40:48], [n("gst")], [n("gst")])
            mb = gst[:, 16:24].unsqueeze(2).to_broadcast([128, 8, 64])
            rb_ = gst[:, 40:48].unsqueeze(2).to_broadcast([128, 8, 64])
            tt("dve", y3, y3, mb, ALU.subtract, [n("ytok"), n("gst")], [n("ytok")])
            tt("dve", yn16.rearrange("p (h v) -> p h v", v=64), y3, rb_, ALU.mult, [n("ytok"), n("gst")], [n("yn16")])
            yield
            pT, pTb, iT = yield from takeB()
            ((pg_, pgb_, ig),) = yield from take(1)
            for ct in range(4):
                tr(pT[:, ct * 128:(ct + 1) * 128], yn16[:, ct * 128:(ct + 1) * 128], ident16, [n("yn16"), bf("ident16")], [pTb])
            for ct in range(4):
                mm(pg_[:, ct * 128:(ct + 1) * 128], g2[:, ct * 128:(ct + 1) * 128], hTg[:, c0 + t * 128:c0 + (t + 1) * 128], True, True,
                   [bf("g2"), bf("hTg")], [pgb_])
            yield
            for ct in range(4):
                act(ynT[:, ct, :], pT[:, ct * 128:(ct + 1) * 128], AF.Identity, [pTb, bf("pvec")], [n("ysq")],
                    bias=pv("lnx_b", ct), scale=pv("lnx_g", ct))
            freeB.append(iT)
            yield
            tt("dve", ynT, ynT, sto["bonus"][:, :, t * 128:(t + 1) * 128], ALU.add, [n("ysq"), bf("a_bonus")], [n("ysq")])
            tt("dve", outT16[:, :, c0 + t * 128:c0 + (t + 1) * 128], ynT, pg_.rearrange("p (c t) -> p c t", t=128), ALU.mult,
               [n("ysq"), pgb_], [bf("outT16")])
            give(ig)
            yield

        def rwkv_out(blk, tiles):
            inherit(OT_NAMES, GB01_NAMES)
            run_gens([rwkv_out_gen(blk, t, k) for k, t in enumerate(tiles)])

        rwkv_block("A")
        chk("blockA")
        dump("PT", STO["PT"].rearrange("p a b c -> p (a b c)"), [bf("a_PT")]); dump("G", STO["G"].rearrange("p a b c -> p (a b c)"), [bf("a_G")])
        dump("QT", STO["QT"].rearrange("p a b c -> p (a b c)"), [bf("a_QT")]); dump("Y0", STO["Y0"].rearrange("p a b -> p (a b)"), [bf("a_Y0")]); dump("Gam", STO["Gam"].rearrange("p a b c -> p (a b c)"), [bf("a_Gam")])
        stv = st_d.rearrange("s d p c v -> s d p (c v)")
        inherit(HS_NAMES, GB3_NAMES)

        def out_state(seg, d):
            def f(Hst_, HB_):
                S.dma("sp", stv[seg, d], Hst_.rearrange("p c v -> p (c v)"), reads=[HB_], is_output=True)
                return
                yield
            return f
        run_gens([recur_gen("A", ([2 * seg, 2 * seg + 1] if d == 0 else [2 * seg + 1, 2 * seg]), d, None, hs=seg * 2 + d, after=out_state(seg, d))
                  for seg in range(2) for d in range(2)])
        chk("recurA")
        rwkv_out("A", range(4))
        chk("outA")
        inherit(GB3_NAMES, HS_NAMES)
        inherit(GB01_NAMES, OT_NAMES)
        rwkv_block("B")
        inherit(HS_NAMES, GB3_NAMES)
        wa, wab = loadw(w_in_v[:, :, 1536:2048], lambda w: w.rearrange("p (kc n) -> p kc n", kc=8), slot=1)
        wg, wgb = loadw(w_in_v[:, :, 2048:2560], lambda w: w.rearrange("p (kc n) -> p kc n", kc=8), slot=2)
        save_ptr = AR.ptr
        AR.ptr = alias_ptr
        XS = AR.alloc([128, 2, 2, 4, 64])
        G4 = AR.alloc([128, 4, 1024])
        ctmp = AR.alloc([128, 4, 64])
        assert AR.ptr <= alias_ptr + 5888
        AR.ptr = save_ptr
        retired = [bf(n) for n in ("a_rrk", "a_kk", "a_Vtok0", "a_Vtok1", "a_sg32", "a_Ei", "a_Ex", "a_En", "a_aT", "a_t1", "a_bT", "a_kd0", "a_kd1")]
        def after_N(d):
            def f(Hst_, HB_):
                cp("dve", XS[:, d, 1], Hst_, [HB_], [bf("a_XS")] + retired)
                return
                yield
            return f

        def after_M(d):
            def f(Hst_, HB_):
                (p0, p0b, i0), (p1, p1b, i1) = yield from take(2)
                pe2 = [(p0, p0b), (p1, p1b)]
                for ct in range(4):
                    for e in range(2):
                        es = slice(e * 64, e * 64 + 64)
                        mm(pe2[e][0][es, ct * 64:(ct + 1) * 64], Hst_[es, ct, :], cst[es, 0, e * 64:(e + 1) * 64], True, True,
                           [HB_, bf("cst")], [pe2[e][1]], tp=(64, 64) if e else None)
                yield
                for e in range(2):
                    es = slice(e * 64, e * 64 + 64)
                    cp("dve", XS[es, d, 0], pe2[e][0][es, 0:256].rearrange("p (a b) -> p a b", b=64), [pe2[e][1]], [bf("a_XS")] + retired)
                give(i0, i1)
            return f
        gl = []
        for d in range(2):
            tiles = [0, 1, 2, 3] if d == 0 else [3, 2, 1, 0]
            gl.append(recur_gen("B", tiles, d, None, useG=True, saveH=False, hs=2 * d, after=after_N(d)))
            gl.append(recur_gen("B", tiles, d, (idh, bf("idh")), useG=False, saveH=False, hs=2 * d + 1, after=after_M(d)))
        run_gens(gl)
        S.dma("pool", bounce_d, XS.rearrange("p a b c d -> p (a b c d)"), reads=[bf("a_XS")], writes=[bf("bounce")])
        S.coll(lambda en: en.collective_compute("AllGather", ALU.bypass, replica_groups=[[0, 1, 2, 3], [4, 5, 6, 7]],
                                                ins=[bounce_d.opt()], outs=[gath_d.opt()]),
               reads=[bf("bounce")], writes=[bf("gath")])
        S.dma("pool", G4, gath_d.rearrange("(r p) n -> p r n", p=128), reads=[bf("gath")], writes=[bf("a_G4")])
        G4v = G4.rearrange("p r (d m c v) -> p r d m c v", d=2, m=2, c=4)
        ctmps = [ctmp, HS[3]["Htmp"]]

        def compose_gen(d):
            HB = bf("Hin%d" % d)
            ct_, ctb_ = ctmps[d], bf("a_ctmp%d" % d)
            order = [0, 1, 2] if d == 0 else [3, 2, 1]
            for j in order:
                (p0, p0b, i0), (p1, p1b, i1) = yield from take(2)
                pe2 = [(p0, p0b), (p1, p1b)]
                for ct in range(4):
                    for e in range(2):
                        es = slice(e * 64, e * 64 + 64)
                        mm(pe2[e][0][es, ct * 64:(ct + 1) * 64], G4v[es, j, d, 0, ct, :], Hin[es, d, ct, :], True, True,
                           [bf("a_G4"), HB], [pe2[e][1]], tp=(64, 64) if e else None)
                yield
                for e in range(2):
                    es = slice(e * 64, e * 64 + 64)
                    tt("dve", ct_[es], pe2[e][0][es, 0:256].rearrange("p (a b) -> p a b", b=64), G4v[es, j, d, 1], ALU.add,
                       [pe2[e][1], bf("a_G4")], [ctb_])
                give(i0, i1)
                yield
                tt("dve", ct_, ct_, Hin[:, d], ALU.subtract, [ctb_, HB], [ctb_])
                stt("dve", Hin[:, d], ct_, selv[:, d * 4 + j:d * 4 + j + 1], Hin[:, d], ALU.mult, ALU.add, [ctb_, bf("selv"), HB], [HB])
                yield
        bf("a_ctmp1").r = list(bf("a_ctmp1").r) + list(bf("a_Htmp3").r) + ([bf("a_Htmp3").w] if bf("a_Htmp3").w is not None else [])
        run_gens([compose_gen(0), compose_gen(1)])
        bf("a_Htmp3").r = list(bf("a_Htmp3").r) + list(bf("a_ctmp1").r) + ([bf("a_ctmp1").w] if bf("a_ctmp1").w is not None else [])
        run_gens([recur_gen("B", ([0, 1, 2, 3] if d == 0 else [3, 2, 1, 0]), d, (Hin[:, d], bf("Hin%d" % d)), hs=d) for d in range(2)])
        rwkv_out("B", range(4))
        dump("outT", outT16.rearrange("p c n -> p (c n)"), [bf("outT16")])
        chk("rwkv")

        new_phase()
        x_sb = AR.alloc([128, 8, 1024])
        for t in range(8):
            S.dma("sp", x_sb[:, t, :], xm[t * 128:(t + 1) * 128, :], writes=[bf("xt%d" % t)])
        cv_ptr = AR.ptr
        cv = AR.alloc([128, 4, 1024])
        upA = [AR.alloc([128, 2, 286], BF16) for _ in range(2)]
        upB = [AR.alloc([128, 8, 94], BF16) for _ in range(2)]
        dgw_ptr = AR.ptr
        dgw = [AR.alloc([128, 31, 128], BF16) for _ in range(2)]
        ucT = AR.alloc([128, 4, 1024], BF16)
        mergedT = AR.alloc([128, 8, 1024], BF16)
        tmpa = [AR.alloc([128, 512]) for _ in range(2)]
        tmpb = [AR.alloc([128, 512]) for _ in range(2)]
        lnm = AR.alloc([128, 512]); lnr = AR.alloc([128, 512])
        g1rep = [AR.alloc([128, 1024]) for _ in range(2)]
        dg = AR.alloc([128, 128])
        for i in range(2):
            memset("dve", upA[i], 0.0, [bf("a_upA%d" % i)])
            memset("dve", upB[i], 0.0, [bf("a_upB%d" % i)])
        def glu_proj(ct, half):
            (pa, pab), (pg, pgb) = getF(), getF()
            for kc in range(8):
                mm(pa, wa[:, kc, ct * 128:(ct + 1) * 128], xnT[:, kc, half * 512:(half + 1) * 512], kc == 0, kc == 7, [wab, XN[half]], [pab])
            for kc in range(8):
                mm(pg, wg[:, kc, ct * 128:(ct + 1) * 128], xnT[:, kc, half * 512:(half + 1) * 512], kc == 0, kc == 7, [wgb, XN[half]], [pgb])
            sgt = tmpa[half]
            act(sgt, pg, AF.Sigmoid, [pgb], [bf("a_tmpa%d" % half)])
            if half == 0:
                up, upn, L = upA[ct % 2], "a_upA%d" % (ct % 2), 256
            else:
                up, upn, L = upB[ct % 2], "a_upB%d" % (ct % 2), 64
            v3 = lambda a_, L=L: a_.rearrange("p (r c) -> p r c", c=L)
            tt("dve", up[:, :, 15:15 + L], v3(pa), v3(sgt), ALU.mult, [pab, bf("a_tmpa%d" % half)], [bf(upn)])
            if half == 0:
                dw, dwb = dgw[ct % 2], bf("a_dgw%d" % (ct % 2))
                for j in range(31):
                    if j % 2:
                        act(dw[:, j, :], ident16, AF.Copy, [bf("ident16"), bf("pvec")], [dwb], scale=pv("conv_w", j * 4 + ct))
                    else:
                        ts("dve", dw[:, j, :], ident16, pv("conv_w", j * 4 + ct), None, ALU.mult, None, [bf("ident16"), bf("pvec")], [dwb])

        def conv_mm(ct, half):
            if half == 0:
                up, upn, L = upA[ct % 2], "a_upA%d" % (ct % 2), 256
            else:
                up, upn, L = upB[ct % 2], "a_upB%d" % (ct % 2), 64
            dw, dwb = dgw[ct % 2], bf("a_dgw%d" % (ct % 2))
            pcv, pcvb = getF()
            for j in range(31):
                mm(pcv, dw[:, j, :], up[:, :, j:j + L], j == 0, j == 30, [dwb, bf(upn)], [pcvb])
            act(cv[:, ct, half * 512:(half + 1) * 512], pcv, AF.Identity, [pcvb, bf("pvec")], [bf("a_cv%d_%d" % (ct, half))], bias=pv("conv_b", ct))

        seq_c = [(ct, half) for ct in range(4) for half in range(2)]
        glu_proj(*seq_c[0])
        for i_, ch_ in enumerate(seq_c):
            if i_ + 1 < len(seq_c):
                glu_proj(*seq_c[i_ + 1])
            conv_mm(*ch_)
        for half in range(2):
            hs = slice(half * 512, (half + 1) * 512)
            (pm, pmb), (pq, pqb) = getF(), getF()
            for ct in range(4):
                mm(pm, cst[:, 6, :], cv[:, ct, hs], ct == 0, ct == 3, [bf("cst"), bf("a_cv%d_%d" % (ct, half))], [pmb])
            for ct in range(4):
                sqt = tmpb[ct % 2]
                act(sqt, cv[:, ct, hs], AF.Square, [bf("a_cv%d_%d" % (ct, half))], [bf("a_tmpb%d" % (ct % 2))])
                mm(pq, cst[:, 6, :], sqt, ct == 0, ct == 3, [bf("cst"), bf("a_tmpb%d" % (ct % 2))], [pqb])
            cp("act", lnm, pm, [pmb], [bf("a_lnm")])
            tt("dve", lnr, lnm, lnm, ALU.mult, [bf("a_lnm")], [bf("a_lnr")])
            tt("dve", lnr, pq, lnr, ALU.subtract, [pqb, bf("a_lnr")], [bf("a_lnr")])
            act(lnr, lnr, AF.Sqrt, [bf("a_lnr"), bf("epsr")], [bf("a_lnr")], bias=epsr[:, 2:3])
            recip(lnr, lnr, [bf("a_lnr")], [bf("a_lnr")])
            for ct in range(4):
                tq = tmpb[ct % 2]; tqb = bf("a_tmpb%d" % (ct % 2))
                tt("dve", tq, cv[:, ct, hs], lnm, ALU.subtract, [bf("a_cv%d_%d" % (ct, half)), bf("a_lnm")], [tqb])
                tt("dve", tq, tq, lnr, ALU.mult, [tqb, bf("a_lnr")], [tqb])
                act(ucT[:, ct, hs], tq, AF.Silu, [tqb, bf("pvec")], [bf("a_ucT")], bias=pv("cln_b", ct), scale=pv("cln_g", ct))
        dump("ucT", ucT.rearrange("p c n -> p (c n)"), [bf("a_ucT")])
        wr, wrb = loadw(wor_d.rearrange("(kc p) n -> p kc n", p=128), lambda w: w.rearrange("p (kc n) -> p kc n", kc=4), slot=3)
        wc, wcb = loadw(woc_d.rearrange("(kc p) n -> p kc n", p=128), lambda w: w.rearrange("p (kc n) -> p kc n", kc=4), slot=0)
        for g in range(2):
            wgr, wgrb = loadw(w_in_v[:, :, 2560 + g * 512:2560 + (g + 1) * 512], lambda w: w.rearrange("p (kc n) -> p kc n", kc=8), slot=1)
            wgc, wgcb = loadw(w_in_v[:, :, 3584 + g * 512:3584 + (g + 1) * 512], lambda w: w.rearrange("p (kc n) -> p kc n", kc=8), slot=2)
            for f4 in range(4):
                fo = g * 4 + f4
                for half in range(2):
                    hs = slice(half * 512, (half + 1) * 512)
                    (pr, prb), (pc, pcb), (pgr, pgrb), (pgc, pgcb) = getF(), getF(), getF(), getF()
                    for kc in range(4):
                        mm(pr, wr[:, kc, fo * 128:(fo + 1) * 128], outT16[:, kc, hs], kc == 0, kc == 3, [wrb, bf("outT16")], [prb])
                    for kc in range(4):
                        mm(pc, wc[:, kc, fo * 128:(fo + 1) * 128], ucT[:, kc, hs], kc == 0, kc == 3, [wcb, bf("a_ucT")], [pcb])
                    for kc in range(8):
                        mm(pgr, wgr[:, kc, f4 * 128:(f4 + 1) * 128], xnT[:, kc, hs], kc == 0, kc == 7, [wgrb, XN[half]], [pgrb])
                    for kc in range(8):
                        mm(pgc, wgc[:, kc, f4 * 128:(f4 + 1) * 128], xnT[:, kc, hs], kc == 0, kc == 7, [wgcb, XN[half]], [pgcb])
                    ta, tab, tb_, tbb = tmpa[half], bf("a_tmpa%d" % half), tmpb[half], bf("a_tmpb%d" % half)
                    act(ta, pgr, AF.Sigmoid, [pgrb], [tab])
                    act(tb_, pgc, AF.Sigmoid, [pgcb], [tbb])
                    tt("dve", ta, pr, ta, ALU.mult, [prb, tab], [tab])
                    tt("dve", tb_, pc, tb_, ALU.mult, [pcb, tbb], [tbb])
                    tt("dve", mergedT[:, fo, hs], ta, tb_, ALU.add, [tab, tbb], [bf("a_mergedT")])

        def bcast_rows(dst_list, col0, tag):
            for j in range(2):
                for hh in range(2):
                    p, pb = getF()
                    for k4 in range(4):
                        kc = hh * 4 + k4
                        ts("dve", dg, ident32, mod[:, col0 + kc, j:j + 1], None, ALU.mult, None, [bf("cst"), bf("mod")], [bf("a_dg")])
                        mm(p[:, k4 * 128:(k4 + 1) * 128], cst[:, 7, :], dg, True, True, [bf("cst"), bf("a_dg")], [pb])
                    cp("act", dst_list[j][:, hh * 512:(hh + 1) * 512], p, [pb], [bf("a_%s%d" % (tag, j))])

        bcast_rows(g1rep, 16, "g1rep")
        wo_v = wo_d.rearrange("(kc p) n -> p kc n", p=128)
        sp3_ = AR.ptr
        AR.ptr = cv_ptr
        xs16c = AR.alloc([128, 8, 1024], BF16)
        AR.ptr = dgw_ptr
        junkc = AR.alloc([128, 1024])
        AR.ptr = sp3_
        inherit(["a_xs16c_%d" % t_ for t_ in range(8)] + ["a_junkc"],
                ["a_cv%d_%d" % (c_, h_) for c_ in range(4) for h_ in range(2)] + ["a_dgw0", "a_dgw1"])
        wos = [loadw(wo_v[:, :, nh * 512:(nh + 1) * 512], lambda w: w.rearrange("p (kc n) -> p kc n", kc=8), slot=3 * nh) for nh in range(2)]
        for t in range(8):
            j = 0 if t < 4 else 1
            for nh in range(2):
                ns = slice(nh * 512, (nh + 1) * 512)
                wo, wob = wos[nh]
                p, pb = getF()
                for kc in range(8):
                    mm(p, mergedT[:, kc, t * 128:(t + 1) * 128], wo[:, kc, :], kc == 0, kc == 7, [bf("a_mergedT"), wob], [pb])
                ta, tab = tmpa[nh], bf("a_tmpa%d" % nh)
                tt("dve", ta, p, g1rep[j][:, ns], ALU.mult, [pb, bf("a_g1rep%d" % j)], [tab])
                tt("dve", x_sb[:, t, ns], ta, x_sb[:, t, ns], ALU.add, [tab, bf("xt%d" % t)], [bf("xt%d" % t)])
            sb_ = bf("a_ssn%d" % t)
            memset("dve", ss[:, t:t + 1], 0.0, [sb_])
            act(junkc, x_sb[:, t, :], AF.Square, [bf("xt%d" % t)], [bf("a_junkc"), sb_], accum=ss[:, t:t + 1])
            act(rstd[:, t:t + 1], ss[:, t:t + 1], AF.Sqrt, [sb_, bf("epsr")], [sb_], bias=epsr[:, 0:1], scale=1.0 / 1024)
            recip(rstd[:, t:t + 1], rstd[:, t:t + 1], [sb_], [sb_])
            if t % 2 == 0:
                ts("dve", xs16c[:, t, :], x_sb[:, t, :], rstd[:, t:t + 1], None, ALU.mult, None, [bf("xt%d" % t), sb_], [bf("a_xs16c_%d" % t)])
            else:
                act(xs16c[:, t, :], x_sb[:, t, :], AF.Copy, [bf("xt%d" % t), sb_], [bf("a_xs16c_%d" % t)], scale=rstd[:, t:t + 1])
            if t % 4 == 3:
                half = t // 4
                for kc in range(8):
                    p, pb = getB()
                    for q in range(4):
                        t_ = half * 4 + q
                        tr(p[:, q * 128:(q + 1) * 128], xs16c[:, t_, kc * 128:(kc + 1) * 128], ident16, [bf("a_xs16c_%d" % t_), bf("ident16")], [pb])
                    act(xnT[:, kc, half * 512:(half + 1) * 512], p[:, 0:512], AF.Identity, [pb, bf("A2"), bf("mod")],
                        [bf("xnT%d" % half)], bias=mod[:, 24 + kc, half:half + 1], scale=A2[:, kc, half:half + 1])
        chk("phaseC")

        new_phase()
        x_sb = AR.alloc([128, 8, 1024])
        h16T = AR.alloc([128, 32, 1024], BF16)
        g2rep = [AR.alloc([128, 1024]) for _ in range(2)]
        fgrep = AR.alloc([128, 1024])
        rtmp = [AR.alloc([128, 512]) for _ in range(2)]
        dg = AR.alloc([128, 128])
        ytile = [AR.alloc([128, 1024]) for _ in range(2)]
        S.dma("sp", fgrep, fgrep_d, writes=[bf("a_fgrep")])
        bcast_rows(g2rep, 40, "g2rep")
        w1_v = w1_d.rearrange("(kc p) n -> p kc n", p=128)
        k_ = 0
        for s_ in range(8):
            w1s, w1b = loadw(w1_v[:, :, s_ * 512:(s_ + 1) * 512], lambda w: w.rearrange("p (kc n) -> p kc n", kc=8))
            for m in range(4):
                ff = s_ * 4 + m
                for half in range(2):
                    hs = slice(half * 512, (half + 1) * 512)
                    p, pb = getF()
                    for kc in range(8):
                        mm(p, w1s[:, kc, m * 128:(m + 1) * 128], xnT[:, kc, hs], kc == 0, kc == 7, [w1b, XN[0], XN[1]], [pb])
                    rt, rtb = rtmp[k_ % 2], bf("a_rtmp%d" % (k_ % 2))
                    act(rt, p, AF.Relu, [pb], [rtb])
                    tt("dve", h16T[:, ff, hs], rt, rt, ALU.mult, [rtb], [bf("a_h16T%d" % ff)])
                    k_ += 1
        w2_v = w2_d.rearrange("(fc p) n -> p fc n", p=128)
        for nh in range(2):
            ns = slice(nh * 512, (nh + 1) * 512)
            slabs = [loadw(w2_v[:, 8 * q_:8 * q_ + 8, ns], lambda w: w.rearrange("p (fc n) -> p fc n", fc=8), slot=q_) for q_ in range(4)]
            for t in range(8):
                j = 0 if t < 4 else 1
                p, pb = getF()
                for ff in range(32):
                    w2s, w2b = slabs[ff // 8]
                    mm(p, h16T[:, ff, t * 128:(t + 1) * 128], w2s[:, ff % 8, :], ff == 0, ff == 31, [bf("a_h16T%d" % ff), w2b], [pb])
                rt, rtb = rtmp[k_ % 2], bf("a_rtmp%d" % (k_ % 2))
                tt("dve", rt, p, g2rep[j][:, ns], ALU.mult, [pb, bf("a_g2rep%d" % j)], [rtb])
                tt("dve", x_sb[:, t, ns], rt, x_sb[:, t, ns], ALU.add, [rtb, bf("xt%d" % t)], [bf("xt%d" % t)])
                k_ += 1
                if nh == 1:
                    yt, ytb = ytile[t % 2], bf("a_ytile%d" % (t % 2))
                    sb_ = bf("a_ssf%d" % t)
                    memset("dve", ss[:, 8 + t:9 + t], 0.0, [sb_])
                    act(yt, x_sb[:, t, :], AF.Square, [bf("xt%d" % t)], [ytb, sb_], accum=ss[:, 8 + t:9 + t])
                    act(rstd[:, 8 + t:9 + t], ss[:, 8 + t:9 + t], AF.Sqrt, [sb_, bf("epsr")], [sb_], bias=epsr[:, 0:1], scale=1.0 / 1024)
                    recip(rstd[:, 8 + t:9 + t], rstd[:, 8 + t:9 + t], [sb_], [sb_])
                    stt("dve", yt, x_sb[:, t, :], rstd[:, 8 + t:9 + t], fgrep, ALU.mult, ALU.mult, [bf("xt%d" % t), sb_, bf("a_fgrep")], [ytb])
                    S.dma("sp", y_d[t * 128:(t + 1) * 128, :], yt, reads=[ytb], is_output=True)


    try:
        _rest()
    except _Stop:
        pass
    S.emit()
    st.close()
    return nc


def prep_inputs(inp):
    f = lambda k: np.asarray(inp[k], np.float32)
    xp, xs = f("x_prompt"), f("x_sample")
    pv = np.zeros((128, NPV), np.float32)

    def put(name, arr):
        a = _fm(arr)
        pv[:, PV_OFF[name]:PV_OFF[name] + a.shape[1]] = a
    put("ada_b", f("ada_b")[0]); put("n1g", f("norm1_g")[0]); put("n2g", f("norm2_g")[0])
    put("mu_p", f("mu_prev")[0]); put("mu_n", f("mu_next")[0])
    put("a0f", f("iclr_a0")[0, 0]); put("a0b", f("iclr_a0")[0, 1])
    put("k_k", f("k_k")[0]); put("k_a", f("k_a")[0]); put("r_k", f("r_k")[0].reshape(-1))
    put("lnx_g", f("lnx_g")[0]); put("lnx_b", f("lnx_b")[0]); put("conv_b", f("conv_b")[0])
    put("cln_g", f("conv_ln_g")[0]); put("cln_b", f("conv_ln_b")[0])
    cw = f("conv_w")[0]
    cwp = np.concatenate([_fm(cw[j]) for j in range(31)], axis=1)
    pv[:, PV_OFF["conv_w"]:PV_OFF["conv_w"] + 124] = cwp
    shared = dict(
        pvec=pv,
        w0row=np.ascontiguousarray(f("decay_w0")[0].reshape(1, 1024)),
        fgrep=np.ascontiguousarray(np.broadcast_to(f("final_g")[None, :], (128, 1024))),
        ident=np.eye(128, dtype=np.float32),
        w1cat=np.ascontiguousarray(np.concatenate([f("decay_w1")[0, 0], f("decay_w1")[0, 1], f("iclr_a1")[0, 0],
                                                   f("iclr_a1")[0, 1], f("gate_g1")[0]], axis=1)),
        w2dec=np.ascontiguousarray(np.concatenate([f("decay_w2")[0, 0], f("decay_w2")[0, 1]], axis=0)),
        a2cat=np.ascontiguousarray(np.concatenate([f("iclr_a2")[0, 0], f("iclr_a2")[0, 1]], axis=0)),
        g2=f("gate_g2")[0],
        w_in=f("w_in")[0],
        w_out_rwkv=f("w_out_rwkv")[0], w_out_conv=f("w_out_conv")[0], w_o=f("w_o")[0],
        mlp_w1=f("mlp_w1")[0], mlp_w2=f("mlp_w2")[0],
    )
    cst, msk, id4, mk = _consts()
    shared.update(cst=cst, msk=msk, id4=id4, mk=mk)
    shared.pop("ident")
    in_maps = []
    for c in range(NCORES):
        b, q = c // 4, c % 4
        xmc = np.concatenate([xp[2 * c], xp[2 * c + 1], xs[b, q * 512:(q + 1) * 512]], axis=0)
        xhc = np.zeros((2, 1024), np.float32)
        hmk = np.zeros((128, 8, 2), np.float32)
        if q > 0:
            xhc[0] = xs[b, q * 512 - 1]; hmk[:, :, 0] = 1.0
        if q < 3:
            xhc[1] = xs[b, (q + 1) * 512]; hmk[:, :, 1] = 1.0
        cond = np.stack([f("c_ctx"), f("c")[0], f("c")[1]], axis=1)
        cT = np.ascontiguousarray(cond.reshape(8, 128, 3).transpose(1, 0, 2).reshape(128, 24))
        m_ada = dict(ada_w=np.ascontiguousarray(f("ada_w")[0][:, q * 1536:(q + 1) * 1536]),
                     adab=_fm(f("ada_b")[0][q * 1536:(q + 1) * 1536]),
                     selb=np.ascontiguousarray(np.broadcast_to(np.array([1.0 - b, float(b)], np.float32)[None, :], (128, 2))))
        s0T = np.stack([np.ascontiguousarray(
            f(nm)[b, 0].transpose(0, 2, 1).reshape(4, 2, 64, 64).transpose(1, 2, 0, 3).reshape(128, 4, 64))
            for nm in ("state_fwd", "state_bwd")], axis=0)
        m = dict(shared)
        m.update(m_ada)
        m["s0T"] = s0T
        sel = np.zeros((128, 8), np.float32)
        for j in range(4):
            sel[:, j] = 1.0 if j < q else 0.0
            sel[:, 4 + j] = 1.0 if j > q else 0.0
        m["sel"] = sel
        m["idh"] = np.ascontiguousarray(np.tile(np.eye(64, dtype=np.float32)[:, None, :], (2, 4, 1)).reshape(128, 256))
        m.update(xm=np.ascontiguousarray(xmc), xh=xhc, hmask=hmk.reshape(128, 16), condT=cT)
        in_maps.append(m)
    return in_maps


def kernel(**inputs):
    in_maps = prep_inputs(inputs)
    nc = build()
    res = run_bass_kernel_spmd(nc, in_maps, core_ids=list(range(NCORES)))
    y_prompt = np.zeros((16, 256, 1024), np.float32)
    y_sample = np.zeros((2, 2048, 1024), np.float32)
    nsf = np.zeros((16, 1, 8, 64, 64), np.float32)
    nsb = np.zeros((16, 1, 8, 64, 64), np.float32)
    for c, r in enumerate(res.results):
        b, q = c // 4, c % 4
        y = np.asarray(r["y"], np.float32)
        y_prompt[2 * c] = y[0:256]
        y_prompt[2 * c + 1] = y[256:512]
        y_sample[b, q * 512:(q + 1) * 512] = y[512:1024]
        stt_ = np.asarray(r["st"], np.float32).reshape(2, 2, 2, 64, 4, 64).transpose(0, 1, 4, 2, 5, 3).reshape(2, 2, 8, 64, 64)
        nsf[2 * c:2 * c + 2, 0] = stt_[:, 0]
        nsb[2 * c:2 * c + 2, 0] = stt_[:, 1]
    return (y_prompt, y_sample, nsf, nsb)
```

```python
import contextlib
import os
import numpy as np
import concourse.bass as bass
import concourse.mybir as mybir
from concourse.bass_utils import run_bass_kernel_spmd

F32 = mybir.dt.float32
BF16 = mybir.dt.bfloat16
AF = mybir.ActivationFunctionType
ALU = mybir.AluOpType
AX = mybir.AxisListType

SAME_ENGINE_SYNC = True
N_DMA_SEMS = 6
NCORES = 8
EM05 = float(np.exp(-0.5))


class Buf:
    __slots__ = ("name", "w", "r", "parts")

    def __init__(self, name=""):
        self.name = name
        self.w = None
        self.r = []
        self.parts = None


def _flat(bufs):
    out = []
    for b in bufs:
        if b.parts:
            out.extend(b.parts)
        else:
            out.append(b)
    return out


class Sched:
    ENGS = ("pe", "act", "dve", "pool", "sp")

    def __init__(self, nc):
        self.nc = nc
        self.ops = {e: [] for e in self.ENGS}
        self.dma_rr = {e: 0 for e in self.ENGS}
        self.dma_hist = {e: [[] for _ in range(N_DMA_SEMS)] for e in self.ENGS}
        self.out_dmas = []

    def _deps(self, reads, writes):
        reads, writes = _flat(reads), _flat(writes)
        deps = []
        for b in reads:
            if b.w is not None:
                deps.append(b.w)
        for b in writes:
            if b.w is not None:
                deps.append(b.w)
            deps.extend(b.r)
        return deps

    def _commit(self, ref, reads, writes):
        reads, writes = _flat(reads), _flat(writes)
        for b in reads:
            b.r.append(ref)
        for b in writes:
            b.w = ref
            b.r = []

    def op(self, eng, fn, reads=(), writes=()):
        deps = self._deps(reads, writes)
        idx = len(self.ops[eng])
        self.ops[eng].append(dict(kind="op", fn=fn, deps=deps, sig=False, cnt=None))
        ref = ("op", eng, idx)
        self._commit(ref, reads, writes)
        return ref

    def dma(self, eng, out, in_, reads=(), writes=(), is_output=False):
        deps = self._deps(reads, writes)
        k = self.dma_rr[eng]
        self.dma_rr[eng] = (k + 1) % N_DMA_SEMS
        hist = self.dma_hist[eng][k]
        if hist:
            deps.append(hist[-1])
        val = 16 * (len(hist) + 1)
        ref = ("dma", eng, k, val)
        hist.append(ref)
        self.ops[eng].append(dict(kind="dma", out=out, in_=in_, deps=deps, semk=k, val=val))
        self._commit(ref, reads, writes)
        if is_output:
            self.out_dmas.append(ref)
        return ref

    def coll(self, fn, reads=(), writes=()):
        deps = self._deps(reads, writes)
        self.n_coll = getattr(self, "n_coll", 0) + 1
        ref = ("dma", "pool", N_DMA_SEMS, self.n_coll)
        self.ops["pool"].append(dict(kind="coll", fn=fn, deps=deps))
        self._commit(ref, reads, writes)
        return ref

    def emit(self):
        nc = self.nc

        def skip_same(d, e):
            return d[1] == e and (not SAME_ENGINE_SYNC or e == "pe")

        for e in self.ENGS:
            for o in self.ops[e]:
                last = {}
                for d in o["deps"]:
                    if d[0] == "op" and not skip_same(d, e):
                        if d[2] > last.get(d[1], -1):
                            last[d[1]] = d[2]
                o["last"] = last
                for pe_, idx_ in last.items():
                    self.ops[pe_][idx_]["sig"] = True
        for e in self.ENGS:
            c = 0
            for o in self.ops[e]:
                if o["kind"] == "op" and o["sig"]:
                    c += 1
                    o["cnt"] = c
        with contextlib.ExitStack() as st:
            esem = {e: st.enter_context(nc.semaphore("s_" + e)) for e in self.ENGS}
            dsem = {e: [st.enter_context(nc.semaphore("d_%s%d" % (e, k))) for k in range(N_DMA_SEMS + 1)]
                    for e in ("sp", "act", "pool")}
            block = st.enter_context(nc.Block())
            sched = self

            def run(e, eng):
                waited = {}
                for o in sched.ops[e]:
                    need = {}
                    for pe_, idx_ in o["last"].items():
                        need[("op", pe_)] = sched.ops[pe_][idx_]["cnt"]
                    for d in o["deps"]:
                        if d[0] != "op":
                            key = ("dma", d[1], d[2])
                            if d[3] > need.get(key, 0):
                                need[key] = d[3]
                    for key, v in need.items():
                        if waited.get(key, 0) >= v:
                            continue
                        waited[key] = v
                        s = esem[key[1]] if key[0] == "op" else dsem[key[1]][key[2]]
                        eng.wait_ge(s, v)
                    if o["kind"] == "op":
                        ins = o["fn"](eng)
                        if o["sig"]:
                            ins.then_inc(esem[e], 1)
                    elif o["kind"] == "coll":
                        o["fn"](eng).then_inc(dsem["pool"][N_DMA_SEMS])
                    else:
                        eng.dma_start(out=o["out"], in_=o["in_"]).then_inc(dsem[e][o["semk"]], 16)
                if e == "sp":
                    for ref in sched.out_dmas:
                        eng.wait_ge(dsem[ref[1]][ref[2]], ref[3])

            block.tensor(lambda eng: run("pe", eng))
            block.scalar(lambda eng: run("act", eng))
            block.vector(lambda eng: run("dve", eng))
            block.gpsimd(lambda eng: run("pool", eng))
            block.sync(lambda eng: run("sp", eng))


PV_FIELDS = [("ada_b", 48), ("n1g", 8), ("n2g", 8), ("mu_p", 12), ("mu_n", 12), ("a0f", 4), ("a0b", 4),
             ("k_k", 4), ("k_a", 4), ("r_k", 4), ("lnx_g", 4), ("lnx_b", 4), ("conv_b", 4), ("cln_g", 4),
             ("cln_b", 4), ("conv_w", 124)]
PV_OFF = {}
_o = 0
for _n, _c in PV_FIELDS:
    PV_OFF[_n] = _o
    _o += _c
NPV = _o


def _fm(v):
    v = np.asarray(v, np.float32).reshape(-1)
    return np.ascontiguousarray(v.reshape(-1, 128).T)


def _consts():
    idx = np.arange(128)
    s, t = idx[:, None], idx[None, :]
    cst = np.zeros((128, 8, 128), np.float32)
    cst[:, 0] = np.eye(128)
    cst[:, 1] = -EM05 * (s <= t)
    cst[:, 2] = -EM05 * (s < t)
    cst[:, 3] = -EM05 * (s >= t)
    cst[:, 4] = -EM05 * (s > t)
    cst[:, 5] = ((s // 64) == (t // 64))
    cst[:, 6] = 1.0 / 512
    cst[:, 7] = 1.0
    msk = np.zeros((128, 2, 2, 384), np.float32)
    for d in range(2):
        strict = (s < t) if d == 0 else (s > t)
        incl = (s <= t) if d == 0 else (s >= t)
        msk[:, d, :, 0:128] = strict[:, None, :]
        msk[:, d, :, 128:256] = incl[:, None, :]
        msk[:, d, :, 256:384] = strict.T[:, None, :]
    id4 = np.zeros((128, 2, 128), np.float32)
    id4[:] = np.eye(128)[:, None, :]
    mk = np.zeros((128, 4, 2, 128), np.float32)
    mk[:, 0] = (s // 16 == t // 16)[:, None, :]
    for li, b in enumerate((16, 32, 64)):
        mk[:, 1 + li] = ((s // (2 * b) == t // (2 * b)) & (s // b != t // b))[:, None, :]
    return cst.reshape(128, 1024), msk.reshape(128, 1536), id4.reshape(128, 256), mk.reshape(128, 1024)


class Arena:
    def __init__(self, t, words):
        self.t, self.words, self.ptr = t, words, 0

    def alloc(self, shape, dt=F32):
        free = int(np.prod(shape[1:]))
        words = free if dt == F32 else (free + 1) // 2
        assert self.ptr + words <= self.words, ("arena overflow", self.ptr, words, self.words)
        ap = self.t[0:shape[0], self.ptr:self.ptr + words]
        self.ptr += words
        if dt == BF16:
            ap = ap.bitcast(BF16)
        if len(shape) == 3:
            ap = ap.rearrange("p (a b) -> p a b", b=shape[2])
        elif len(shape) == 4:
            ap = ap.rearrange("p (a b c) -> p a b c", b=shape[2], c=shape[3])
        elif len(shape) == 5:
            ap = ap.rearrange("p (a b c d) -> p a b c d", b=shape[2], c=shape[3], d=shape[4])
        return ap


def build(dbg=(), stop_after=None):
    nc = bass.Bass("TRN2", target_bir_lowering=False)
    S = Sched(nc)
    st = contextlib.ExitStack()

    def din(name, shape, dt=F32):
        return nc.dram_tensor(name, list(shape), dt, kind="ExternalInput").ap()

    def dout(name, shape):
        return nc.dram_tensor(name, list(shape), F32, kind="ExternalOutput").ap()

    def sb(name, shape, dt=F32):
        t = st.enter_context(nc.sbuf_tensor(name, list(shape), dt))
        return t[:]

    xm = din("xm", [1024, 1024])
    xh = din("xh", [2, 1024])
    hmask = din("hmask", [128, 16])
    condT = din("condT", [128, 24])
    pvec_d = din("pvec", [128, NPV])
    w0row_d = din("w0row", [1, 1024])
    fgrep_d = din("fgrep", [128, 1024])
    cst_d = din("cst", [128, 1024])
    msk_d = din("msk", [128, 1536])
    id4_d = din("id4", [128, 256])
    mk_d = din("mk", [128, 1024])
    s0T_d = din("s0T", [2, 128, 4, 64])
    sel_d = din("sel", [128, 8])
    idh_d = din("idh", [128, 256])
    bounce_d = nc.dram_tensor("bounce", [128, 1024], F32).ap()
    gath_d = nc.dram_tensor("gath", [512, 1024], F32).ap()
    w1cat_d = din("w1cat", [1024, 384])
    w2dec_d = din("w2dec", [128, 512])
    a2cat_d = din("a2cat", [128, 512])
    g2_d = din("g2", [128, 512])
    ada_w_d = din("ada_w", [1024, 1536])
    adab_d = din("adab", [128, 12])
    selb_d = din("selb", [128, 2])
    abounce_d = nc.dram_tensor("abounce", [128, 36], F32).ap()
    agath_d = nc.dram_tensor("agath", [512, 36], F32).ap()
    w_in_d = din("w_in", [1024, 4608])
    wor_d = din("w_out_rwkv", [512, 1024])
    woc_d = din("w_out_conv", [512, 1024])
    wo_d = din("w_o", [1024, 1024])
    w1_d = din("mlp_w1", [1024, 4096])
    w2_d = din("mlp_w2", [4096, 1024])
    y_d = dout("y", [1024, 1024])
    st_d = dout("st", [2, 2, 128, 4, 64])
    dbg_d = {n: dout("dbg_" + n, shp) for n, shp in dbg}

    xnT = sb("xnT", [128, 8, 1024], BF16)
    xnTh = sb("xnTh", [128, 8, 2], BF16)
    pvec = sb("pvec_sb", [128, NPV])
    hm = sb("hm", [128, 16])
    cT = sb("cT", [128, 24]); scT = sb("scT", [128, 24])
    adab = sb("adab_sb", [128, 12]); selb = sb("selb_sb", [128, 2])
    mod = sb("mod", [128, 48, 2])
    A1 = sb("A1", [128, 8, 2]); A2 = sb("A2", [128, 8, 2])
    cst = sb("cst_sb", [128, 8, 128])
    ident32 = cst[:, 0, :]
    blk1 = cst[:, 5, :]
    ident16 = sb("ident16", [128, 128], BF16)
    id4 = sb("id4_sb", [128, 2, 128], BF16)
    msk = sb("msk_sb", [128, 2, 2, 384], BF16)
    mk = sb("mk_sb", [128, 4, 2, 128], BF16)
    w0row = sb("w0row_sb", [1, 1024])
    ones32 = sb("ones32", [1, 128])
    ss = sb("ss", [128, 16]); rstd = sb("rstd", [128, 16])
    epsr = sb("epsr", [128, 4])
    c0v = sb("c0v", [128, 12])
    a0h = sb("a0h", [128, 8]); kah = sb("kah", [128, 4])
    wring = [sb("wring%d" % i, [128, 4096], BF16) for i in range(4)]
    wringb = [Buf("wring%d" % i) for i in range(4)]
    w2dec = sb("w2dec_sb", [128, 512], BF16); a2cat = sb("a2cat_sb", [128, 512], BF16); g2 = sb("g2w", [128, 512], BF16)
    hTd = sb("hTd", [128, 1024], BF16); hTi = sb("hTi", [128, 1024], BF16); hTg = sb("hTg", [128, 1024], BF16)
    outT16 = sb("outT16", [128, 4, 1024], BF16)
    selv = sb("selv", [128, 8])
    hsel = sb("hsel", [128, 2])
    idh = sb("idh_sb", [128, 4, 64])
    Hin = sb("Hin", [128, 2, 4, 64])
    ARENA_WORDS = 31400
    arena_t = sb("arena", [128, ARENA_WORDS])
    AR = Arena(arena_t, ARENA_WORDS)

    psF = [st.enter_context(nc.psum_tensor("psF%d" % i, [128, 512], F32))[:] for i in range(6)]
    psB = [st.enter_context(nc.psum_tensor("psB%d" % i, [128, 512], BF16))[:] for i in range(2)]
    psFb = [Buf("psF%d" % i) for i in range(6)]
    psBb = [Buf("psB%d" % i) for i in range(2)]
    rr = {"F": 0, "B": 0, "W": 0}

    psHb = [Buf("psH%d" % i) for i in range(12)]
    for i in range(6):
        psFb[i].parts = [psHb[i], psHb[i + 6]]
    rr["H"] = 0

    def getF():
        i = rr["F"]; rr["F"] = (i + 1) % 6
        return psF[i], psFb[i]

    def getH():
        k = rr["H"]; rr["H"] = (k + 1) % 12
        return psF[k % 6][:, (k // 6) * 256:(k // 6) * 256 + 256], psHb[k]

    def getB():
        i = rr["B"]; rr["B"] = (i + 1) % 2
        return psB[i], psBb[i]

    def pv(name, i=0, n=1):
        o = PV_OFF[name] + i
        return pvec[:, o:o + n]

    def mm(out, lhsT, rhs, start, stop, r, w, tp=None):
        kw = {} if tp is None else dict(tile_position=tp)
        return S.op("pe", lambda e: e.matmul(out, lhsT, rhs, start=start, stop=stop, **kw), reads=r, writes=w)

    def tr(out, in_, idn, r, w):
        return S.op("pe", lambda e: e.transpose(out, in_, idn), reads=r, writes=w)

    def act(out, in_, func, r, w, bias=None, scale=None, accum=None):
        kw = {}
        if bias is not None: kw["bias"] = bias
        if scale is not None: kw["scale"] = scale
        if accum is not None: kw["accum_out"] = accum
        return S.op("act", lambda e: e.activation(out, in_, func, **kw), reads=r, writes=w)

    def tt(eng, out, a, b, op, r, w):
        return S.op(eng, lambda e: e.tensor_tensor(out, a, b, op), reads=r, writes=w)

    def ts(eng, out, a, s1, s2, op0, op1, r, w):
        if op1 is None:
            return S.op(eng, lambda e: e.tensor_scalar(out, a, s1, None, op0), reads=r, writes=w)
        return S.op(eng, lambda e: e.tensor_scalar(out, a, s1, s2, op0, op1), reads=r, writes=w)

    def stt(eng, out, a, s, b, op0, op1, r, w):
        return S.op(eng, lambda e: e.scalar_tensor_tensor(out, a, s, b, op0, op1), reads=r, writes=w)

    def cp(eng, out, in_, r, w):
        if eng == "act":
            return S.op("act", lambda e: e.copy(out, in_), reads=r, writes=w)
        return S.op(eng, lambda e: e.tensor_copy(out, in_), reads=r, writes=w)

    def recip(out, in_, r, w):
        return S.op("dve", lambda e: e.reciprocal(out, in_), reads=r, writes=w)

    def memset(eng, ap, val, w):
        return S.op(eng, lambda e: e.memset(ap, val), writes=w)

    B = {}
    carry = {"refs": []}

    def bf(name):
        if name not in B:
            b = Buf(name)
            b.r = list(carry["refs"])
            B[name] = b
        return B[name]

    def new_phase():
        refs = set()
        for b in list(B.values()) + psHb + psBb + wringb:
            if b.w is not None:
                refs.add(b.w)
            refs.update(b.r)
        best = {}
        for rf in refs:
            key = rf[:2] if rf[0] == "op" else rf[:3]
            val = rf[2] if rf[0] == "op" else rf[3]
            if key not in best or val > (best[key][2] if rf[0] == "op" else best[key][3]):
                best[key] = rf
        carry["refs"] = list(best.values())
        for wb_ in wringb:
            wb_.r = list(wb_.r) + list(carry["refs"])
        AR.ptr = 0
        for k in [k for k in B if k.startswith("a_")]:
            del B[k]

    def dump(name, ap, r):
        if name in dbg_d:
            S.dma("pool", dbg_d[name], ap, reads=r, is_output=True)

    def loadw(src_ap, view_fn, slot=None):
        if slot is None:
            i = rr["W"]; rr["W"] = (i + 1) % 4
        else:
            i = slot
        v = view_fn(wring[i])
        S.dma("pool", v, src_ap, writes=[wringb[i]])
        return v, wringb[i]

    S.dma("sp", pvec, pvec_d, writes=[bf("pvec")])
    S.dma("sp", cT, condT, writes=[bf("cT")])
    S.dma("sp", hm, hmask, writes=[bf("hm")])
    S.dma("sp", cst.rearrange("p a b -> p (a b)"), cst_d, writes=[bf("cst")])
    S.dma("sp", w0row, w0row_d, writes=[bf("w0row")])
    S.dma("sp", Hin[:, 0], s0T_d[0], writes=[bf("Hin0")])
    S.dma("sp", Hin[:, 1], s0T_d[1], writes=[bf("Hin1")])
    S.dma("sp", selv, sel_d, writes=[bf("selv")])
    S.dma("sp", idh.rearrange("p a b -> p (a b)"), idh_d, writes=[bf("idh")])
    S.dma("pool", ident16, cst_d[:, 0:128], writes=[bf("ident16")])
    S.dma("pool", id4.rearrange("p a b -> p (a b)"), id4_d, writes=[bf("id4")])
    S.dma("pool", msk.rearrange("p a b c -> p (a b c)"), msk_d, writes=[bf("msk")])
    S.dma("pool", mk.rearrange("p a b c -> p (a b c)"), mk_d, writes=[bf("mk")])
    S.dma("pool", w2dec, w2dec_d, writes=[bf("w2dec")])
    S.dma("pool", a2cat, a2cat_d, writes=[bf("a2cat")])
    S.dma("pool", g2, g2_d, writes=[bf("g2")])
    x_sb = AR.alloc([128, 8, 1024])
    xs16 = AR.alloc([128, 8, 1024], BF16)
    junk = AR.alloc([128, 1024])
    xh_sb = AR.alloc([2, 1024]); xh16 = AR.alloc([2, 1024], BF16); xht = AR.alloc([128, 8, 2])
    S.dma("sp", xh_sb, xh, writes=[bf("a_xh")])
    for t in range(8):
        S.dma("sp", x_sb[:, t, :], xm[t * 128:(t + 1) * 128, :], writes=[bf("a_x%d" % t)])
    memset("dve", epsr[:, 0:1], 1e-6, [bf("epsr")])
    memset("dve", epsr[:, 1:2], 64e-5, [bf("epsr")])
    memset("dve", epsr[:, 2:3], 1e-5, [bf("epsr")])
    memset("dve", ss, 0.0, [bf("ss")])
    memset("dve", ones32, 1.0, [bf("ones32")])
    memset("dve", hsel, 0.0, [bf("hsel")])
    memset("dve", hsel[0:64, 0:1], 1.0, [bf("hsel")])
    memset("dve", hsel[64:128, 1:2], 1.0, [bf("hsel")])
    ts("dve", c0v, pv("mu_p", 0, 12), -1.0, 1.0, ALU.mult, ALU.add, [bf("pvec")], [bf("c0v")])
    tt("dve", c0v, c0v, pv("mu_n", 0, 12), ALU.subtract, [bf("c0v"), bf("pvec")], [bf("c0v")])
    ts("dve", a0h, pv("a0f", 0, 8), 0.5, None, ALU.mult, None, [bf("pvec")], [bf("a0h")])
    ts("dve", kah, pv("k_a", 0, 4), 0.5, None, ALU.mult, None, [bf("pvec")], [bf("kah")])

    def rmsnorm_T(x_sb, xs16, junk, Aw, Awname, shc, xn="a_x", part=None):
        if part in (None, 1):
            memset("dve", ss[:, 0:8], 0.0, [bf("ss")])
            for t in range(8):
                act(junk, x_sb[:, t, :], AF.Square, [bf(xn + "%d" % t)], [bf("a_junk"), bf("ss")], accum=ss[:, t:t + 1])
            act(rstd[:, 0:8], ss[:, 0:8], AF.Sqrt, [bf("ss"), bf("epsr")], [bf("rstd")], bias=epsr[:, 0:1], scale=1.0 / 1024)
            recip(rstd[:, 0:8], rstd[:, 0:8], [bf("rstd")], [bf("rstd")])
            for t in range(8):
                if t % 2 == 0:
                    ts("dve", xs16[:, t, :], x_sb[:, t, :], rstd[:, t:t + 1], None, ALU.mult, None,
                       [bf(xn + "%d" % t), bf("rstd")], [bf("a_xs16_%d" % t)])
                else:
                    act(xs16[:, t, :], x_sb[:, t, :], AF.Copy, [bf(xn + "%d" % t), bf("rstd")], [bf("a_xs16_%d" % t)], scale=rstd[:, t:t + 1])
        if part in (None, 2):
            for kc in range(8):
                for half in range(2):
                    p, pb = getB()
                    for q in range(4):
                        t = half * 4 + q
                        tr(p[:, q * 128:(q + 1) * 128], xs16[:, t, kc * 128:(kc + 1) * 128], ident16,
                           [bf("a_xs16_%d" % t), bf("ident16")], [pb])
                    act(xnT[:, kc, half * 512:(half + 1) * 512], p[:, 0:512], AF.Identity, [pb, bf(Awname), bf("mod")],
                        [bf("xnT%d" % half)], bias=mod[:, shc + kc, half:half + 1], scale=Aw[:, kc, half:half + 1])

    wl1v, wl1b = loadw(w1cat_d.rearrange("(kc p) n -> p kc n", p=128),
                     lambda w: w[:, 0:3072].rearrange("p (kc n) -> p kc n", kc=8))
    w_in_v = w_in_d.rearrange("(kc p) n -> p kc n", p=128)
    wrkv = []
    for i in range(3):
        v, b_ = loadw(w_in_v[:, :, i * 512:(i + 1) * 512], lambda w: w.rearrange("p (kc n) -> p kc n", kc=8))
        wrkv.append((v, b_))

    S.dma("sp", adab, adab_d, writes=[bf("adab")])
    S.dma("sp", selb, selb_d, writes=[bf("selb")])
    act(scT, cT, AF.Silu, [bf("cT")], [bf("scT")])
    ada_v = ada_w_d.rearrange("(kc p) n -> p kc n", p=128)
    adaw = AR.alloc([128, 8, 1536])
    S.dma("sp", adaw[:, 0:4, :], ada_v[:, 0:4, :], writes=[bf("a_adaw0")])
    S.dma("act", adaw[:, 4:8, :], ada_v[:, 4:8, :], writes=[bf("a_adaw1")])
    modrow = AR.alloc([3, 1536])
    for nchunk in range(3):
        pr_, prb_ = getF()
        for kc in range(8):
            mm(pr_[0:3, :], scT[:, kc * 3:kc * 3 + 3], adaw[:, kc, nchunk * 512:(nchunk + 1) * 512], kc == 0, kc == 7,
               [bf("a_adaw%d" % (kc // 4)), bf("scT")], [prb_])
        cp("act", modrow[:, nchunk * 512:(nchunk + 1) * 512], pr_[0:3, :], [prb_], [bf("a_modrow")])
    modp, modpb = getF()
    for f in range(12):
        mm(modp[:, f * 3:f * 3 + 3], modrow[:, f * 128:(f + 1) * 128], cst[0:3, 0, 0:3], True, True, [bf("a_modrow"), bf("cst")], [modpb])
    modpart = AR.alloc([128, 12, 3])
    for j in range(3):
        tt("dve", modpart[:, :, j], modp[:, 0:36].rearrange("p (f j) -> p f j", j=3)[:, :, j], adab, ALU.add, [modpb, bf("adab")], [bf("a_modpart")])
    S.dma("pool", abounce_d, modpart.rearrange("p f j -> p (f j)"), reads=[bf("a_modpart")], writes=[bf("abounce")])
    S.coll(lambda en: en.collective_compute("AllGather", ALU.bypass, replica_groups=[[0, 1, 2, 3], [4, 5, 6, 7]],
                                            ins=[abounce_d.opt()], outs=[agath_d.opt()]),
           reads=[bf("abounce")], writes=[bf("agath")])
    modall = AR.alloc([128, 48, 3])
    S.dma("pool", modall.rearrange("p (r f) j -> p r (f j)", r=4), agath_d.rearrange("(r p) n -> p r n", p=128), reads=[bf("agath")], writes=[bf("a_modall")])
    rmsnorm_T(x_sb, xs16, junk, A1, "A1", 0, part=1)
    cp("dve", mod[:, :, 0], modall[:, :, 0], [bf("a_modall")], [bf("mod")])
    ts("dve", mod[:, :, 1], modall[:, :, 1], selb[:, 0:1], None, ALU.mult, None, [bf("a_modall"), bf("selb")], [bf("mod")])
    stt("dve", mod[:, :, 1], modall[:, :, 2], selb[:, 1:2], mod[:, :, 1], ALU.mult, ALU.add, [bf("a_modall"), bf("selb"), bf("mod")], [bf("mod")])
    for j in range(2):
        stt("dve", A1[:, :, j], mod[:, 8:16, j], 1.0, pv("n1g", 0, 8), ALU.add, ALU.mult, [bf("mod"), bf("pvec")], [bf("A1")])
        stt("dve", A2[:, :, j], mod[:, 32:40, j], 1.0, pv("n2g", 0, 8), ALU.add, ALU.mult, [bf("mod"), bf("pvec")], [bf("A2")])
    dump("mod", mod.rearrange("p f j -> p (f j)"), [bf("mod")])

    rmsnorm_T(x_sb, xs16, junk, A1, "A1", 0, part=2)
    memset("dve", ss[0:2, 8:9], 0.0, [bf("ssh")])
    act(junk[0:2, :], xh_sb, AF.Square, [bf("a_xh")], [bf("a_junk"), bf("ssh")], accum=ss[0:2, 8:9])
    act(rstd[0:2, 8:9], ss[0:2, 8:9], AF.Sqrt, [bf("ssh"), bf("epsr")], [bf("rstdh")], bias=epsr[0:2, 0:1], scale=1.0 / 1024)
    recip(rstd[0:2, 8:9], rstd[0:2, 8:9], [bf("rstdh")], [bf("rstdh")])
    ts("dve", xh16, xh_sb, rstd[0:2, 8:9], None, ALU.mult, None, [bf("a_xh"), bf("rstdh")], [bf("a_xh16")])
    p, pb = getB()
    for kc in range(8):
        tr(p[:, kc * 2:kc * 2 + 2], xh16[:, kc * 128:(kc + 1) * 128], ident16[0:2, 0:2], [bf("a_xh16"), bf("ident16")], [pb])
    for kc in range(8):
        act(xht[:, kc, :], p[:, kc * 2:kc * 2 + 2], AF.Identity, [pb, bf("A1"), bf("mod")], [bf("a_xht")],
            bias=mod[:, kc, 1:2], scale=A1[:, kc, 1:2])
    tt("dve", xnTh, xht, hm.rearrange("p (k j) -> p k j", j=2), ALU.mult, [bf("a_xht"), bf("hm")], [bf("xnTh")])
    XN = [bf("xnT0"), bf("xnT1")]
    dump("xnT", xnT.rearrange("p k n -> p (k n)"), XN)

    class _Stop(Exception):
        pass

    def chk(tag):
        if stop_after == tag:
            raise _Stop()

    def _rest():
        for mt, (dst, fn, nm) in enumerate(((hTd, AF.Tanh, "hTd"), (hTi, AF.Identity, "hTi"), (hTg, AF.Sigmoid, "hTg"))):
            for half in range(2):
                p, pb = getF()
                for kc in range(8):
                    mm(p, wl1v[:, kc, mt * 128:(mt + 1) * 128], xnT[:, kc, half * 512:(half + 1) * 512], kc == 0, kc == 7,
                       [wl1b, XN[half]], [pb])
                act(dst[:, half * 512:(half + 1) * 512], p, fn, [pb], [bf(nm)])
        dump("hTd", hTd, [bf("hTd")])
        chk("p3")

        new_phase()
        STO = dict(
            QT=AR.alloc([128, 4, 4, 2 * 128], BF16),
            PT=AR.alloc([128, 4, 4, 2 * 64], BF16),
            G=AR.alloc([128, 4, 4, 2 * 64]),
            Y0=AR.alloc([128, 4, 512]),
            Gam=AR.alloc([128, 4, 4, 2]),
            H0=AR.alloc([128, 4, 4, 2 * 64], BF16),
            bonus=AR.alloc([128, 4, 512], BF16),
            )
        rawA = [AR.alloc([128, 516])] * 2
        rawB = rawA
        rT16 = AR.alloc([128, 512], BF16); vT16 = AR.alloc([128, 512], BF16); kT = AR.alloc([128, 512])
        alias_ptr = AR.ptr
        rrk = AR.alloc([128, 512], BF16)
        kk = AR.alloc([128, 512])
        Vtok2 = [AR.alloc([128, 4, 128], BF16) for _ in range(2)]
        sg32 = AR.alloc([128, 4, 128])
        Ei = AR.alloc([128, 512]); Ex = AR.alloc([128, 512]); En = AR.alloc([128, 512])
        shtmp = Ex
        aT = AR.alloc([128, 512]); t1 = AR.alloc([128, 512]); bT = AR.alloc([128, 512])
        sq, rs = t1, bT
        kd = [AR.alloc([128, 512]) for _ in range(2)]
        AR16 = AR.alloc([128, 4, 256], BF16); BT16 = AR.alloc([128, 512], BF16); KT16 = AR.alloc([128, 512], BF16)
        AZ = AR.alloc([128, 4, 2, 128], BF16); Btok = AR.alloc([128, 4, 128], BF16); Ktok = AR.alloc([128, 4, 128], BF16)
        NGS = 4
        GB = []
        gb_ptr = {}
        for i_ in range(NGS):
            gb_ptr[i_] = AR.ptr
            if i_ == 3:
                gb3_ptr = AR.ptr
            if i_ < 2:
                mb_ = AR.alloc([128, 4, 2, 2, 128], BF16)
            else:
                mb_ = wring[0][:, (i_ - 2) * 2048:(i_ - 1) * 2048].rearrange("p (m x e c) -> p m x e c", m=4, x=2, e=2)
                bmb = bf("a_MB_%d" % i_)
                bmb.r = bmb.r + list(wringb[0].r) + ([wringb[0].w] if wringb[0].w is not None else [])
            g_ = dict(NL=AR.alloc([128, 2, 2, 128], BF16), ARB=AR.alloc([128, 2, 128], BF16), KA=AR.alloc([128, 2, 256], BF16),
                      MB=mb_,
                      RR=[AR.alloc([128, 2, 2, 128], BF16) for _ in range(2)], LN=[AR.alloc([128, 2, 2, 128], BF16) for _ in range(2)],
                      XX=AR.alloc([128, 2, 2, 128], BF16), WU=AR.alloc([128, 2, 128], BF16))
            GB.append(g_)
        AR16m = [AR.alloc([128, 4, 256], BF16) for _ in range(2)]
        HS = [dict(Hst=AR.alloc([128, 4, 64]), Hs16=AR.alloc([128, 4, 64], BF16), Htmp=AR.alloc([128, 4, 64]))]
        sp__ = AR.ptr
        AR.ptr = gb3_ptr
        for _ in range(3):
            HS.append(dict(Hst=AR.alloc([128, 4, 64]), Hs16=AR.alloc([128, 4, 64], BF16), Htmp=AR.alloc([128, 4, 64])))
        assert AR.ptr <= gb3_ptr + 2048
        AR.ptr = sp__
        GB3_NAMES = ["a_%s_3" % k_ for k_ in ("NL", "ARB", "KA", "RR0", "RR1", "LN0", "LN1", "XX", "WU")]
        GB01_NAMES = ["a_%s_%d" % (k_, i_) for i_ in range(2) for k_ in ("NL", "ARB", "KA", "MB", "RR0", "RR1", "LN0", "LN1", "XX", "WU")]
        OT = []
        sp2__ = AR.ptr
        for k_ in range(4):
            if k_ % 2 == 0:
                AR.ptr = gb_ptr[k_ // 2]
            y_ = AR.alloc([128, 512]); q_ = AR.alloc([128, 512]); n16_ = AR.alloc([128, 512], BF16); g_s = AR.alloc([128, 48])
            OT.append(dict(ytok=y_, ysq=q_, yn16=n16_, gst=g_s, ynT=q_.rearrange("p (c t) -> p c t", t=128)))
            assert AR.ptr <= gb_ptr[k_ // 2] + 3072
        AR.ptr = sp2__
        OT_NAMES = ["a_o%s_%d" % (k_, i_) for i_ in range(4) for k_ in ("ytok", "ysq", "yn16", "gst")]
        HS_NAMES = ["a_%s%d" % (k_, i_) for i_ in range(1, 4) for k_ in ("Hst", "Hs16", "Htmp")]

        def inherit(dst_names, src_names):
            refs = []
            for nm in src_names:
                b_ = bf(nm)
                if b_.w is not None:
                    refs.append(b_.w)
                refs.extend(b_.r)
            for nm in dst_names:
                b_ = bf(nm)
                b_.r = list(b_.r) + refs
        memset("dve", rawA[0], 0.0, [bf("a_rawA0")])

        free_banks = list(range(6))

        def take(nb):
            while len(free_banks) < nb:
                yield
            out = []
            for _ in range(nb):
                i = free_banks.pop(0)
                out.append((psF[i], psFb[i], i))
            return out

        def give(*idx):
            free_banks.extend(idx)

        freeB = [0, 1]

        def takeB():
            while not freeB:
                yield
            i = freeB.pop(0)
            return psB[i], psBb[i], i

        def group_gen(blk, ct, d, t, bs):
            sto = STO
            G_ = GB[bs]
            Vtok, VtB = Vtok2[ct % 2], bf("a_Vtok%d" % (ct % 2))
            n = lambda s_: bf("a_%s_%d" % (s_, bs))
            NL, ARB, KA, MB, RR, LN, XX, WU = (G_[k_] for k_ in ("NL", "ARB", "KA", "MB", "RR", "LN", "XX", "WU"))
            v2 = lambda q: q.rearrange("p (a b) -> p a b", b=128)
            v22 = lambda q: q.rearrange("p (x a b) -> p x a b", x=2, b=128)
            tcols = slice(t * 128, (t + 1) * 128)
            (pA, pAb, iA), (pC, pCb, iC) = yield from take(2)
            for e in range(2):
                mm(pA[:, e * 256:(e + 1) * 256], BT16[:, tcols], AR16m[e][:, t, :], True, True, [bf("a_BT16"), bf("a_AR16m")], [pAb])
                mm(pC[:, e * 128:(e + 1) * 128], AR16m[e][:, t, 0:128], BT16[:, tcols], True, True, [bf("a_AR16m"), bf("a_BT16")], [pCb])
            yield
            pA3 = pA.rearrange("p (a b) -> p a b", b=256)
            tt("dve", NL[:, 0], pA3[:, :, 0:128], msk[:, d, 0:2, 0:128], ALU.mult, [pAb, bf("msk")], [n("NL")])
            tt("dve", NL[:, 1], v2(pC[:, 0:256]), msk[:, d, 0:2, 256:384], ALU.mult, [pCb, bf("msk")], [n("NL")])
            tt("dve", ARB, pA3[:, :, 128:256], msk[:, d, 0:2, 128:256], ALU.mult, [pAb, bf("msk")], [n("ARB")])
            give(iA, iC)
            ((pK, pKb, iK),) = yield from take(1)
            for e in range(2):
                mm(pK[:, e * 256:(e + 1) * 256], KT16[:, tcols], AR16m[e][:, t, :], True, True, [bf("a_KT16"), bf("a_AR16m")], [pKb])
            yield
            tt("dve", MB.rearrange("p m x e c -> p m x (e c)"),
               NL.rearrange("p x e c -> p x (e c)").unsqueeze(1).to_broadcast([128, 4, 2, 256]),
               mk.rearrange("p m e c -> p m (e c)").unsqueeze(2).to_broadcast([128, 4, 2, 256]), ALU.mult, [n("NL"), bf("mk")], [n("MB")])
            tt("dve", RR[0], MB[:, 0], id4[:, 0:2].unsqueeze(1).to_broadcast([128, 2, 2, 128]), ALU.add, [n("MB"), bf("id4")], [n("RR0")])
            pK3 = pK.rearrange("p (a b) -> p a b", b=256)
            tt("dve", KA, pK3, msk[:, d, 0:2, 0:256], ALU.mult, [pKb, bf("msk")], [n("KA")])
            give(iK)
            yield
            Nc, Lc, NLb = MB[:, 0, 0], MB[:, 0, 1], n("MB")
            pend = None
            for lev in range(4):
                nbk = (1 if lev < 3 else 0) + (1 if lev == 0 else 0) + (1 if pend is not None else 0)
                bks = list((yield from take(nbk)))
                if lev < 3:
                    p1, p1b, i1 = bks.pop(0)
                    for e in range(2):
                        mm(p1[:, e * 128:(e + 1) * 128], Nc[:, e, :], Lc[:, e, :], True, True, [NLb], [p1b])
                        mm(p1[:, 256 + e * 128:256 + (e + 1) * 128], Lc[:, e, :], Nc[:, e, :], True, True, [NLb], [p1b])
                if lev == 0:
                    pZ, pZb, iZ = bks.pop(0)
                    for e in range(2):
                        mm(pZ[:, e * 64:(e + 1) * 64], KA[:, e, 0:128], Vtok[:, t, e * 64:(e + 1) * 64], True, True, [n("KA"), VtB], [pZb])
                if pend is not None:
                    p2, p2b, i2 = bks.pop(0)
                    LNp, LNpb, c_, n_ = pend
                    for e in range(2):
                        mm(p2[:, e * 128:(e + 1) * 128], LNp[:, 0, e, :], RR[c_][:, 0, e, :], True, False, [LNpb, n("RR%d" % c_)], [p2b])
                        mm(p2[:, e * 128:(e + 1) * 128], ident16, RR[c_][:, 0, e, :], False, True, [bf("ident16"), n("RR%d" % c_)], [p2b])
                        mm(p2[:, 256 + e * 128:256 + (e + 1) * 128], LNp[:, 1, e, :], RR[c_][:, 1, e, :], True, False, [LNpb, n("RR%d" % c_)], [p2b])
                        mm(p2[:, 256 + e * 128:256 + (e + 1) * 128], ident16, RR[c_][:, 1, e, :], False, True, [bf("ident16"), n("RR%d" % c_)], [p2b])
                yield
                if pend is not None:
                    cp("act", RR[n_], v22(p2), [p2b], [n("RR%d" % n_)])
                    give(i2)
                    pend = None
                if lev < 3:
                    LNn, LNb = LN[lev % 2], n("LN%d" % (lev % 2))
                    cp("dve", LNn, v22(p1), [p1b], [LNb])
                    give(i1)
                    pend = (LNn, LNb, lev % 2, (lev + 1) % 2)
                    Nc, Lc, NLb = LNn[:, 1], LNn[:, 0], LNb
                if lev == 0:
                    cp("act", AZ[:, t, :, 64:128], pZ[:, 0:128].rearrange("p (e k) -> p e k", k=64), [pZb], [bf("a_AZ%d" % t)])
                    give(iZ)
                yield
            cur = 1
            for li in range(3):
                nx = 1 - cur
                D_, Dt_, Db_ = RR[cur][:, 0], RR[cur][:, 1], n("RR%d" % cur)
                O_, Ot_, Ob_ = MB[:, 1 + li, 0], MB[:, 1 + li, 1], n("MB")
                ((p1, p1b, i1),) = yield from take(1)
                for e in range(2):
                    mm(p1[:, e * 128:(e + 1) * 128], Ot_[:, e, :], D_[:, e, :], True, True, [Ob_, Db_], [p1b])
                    if li < 2:
                        mm(p1[:, 256 + e * 128:256 + (e + 1) * 128], O_[:, e, :], Dt_[:, e, :], True, True, [Ob_, Db_], [p1b])
                yield
                if li < 2:
                    cp("act", XX, v22(p1), [p1b], [n("XX")])
                else:
                    cp("act", XX[:, 0], v2(p1[:, 0:256]), [p1b], [n("XX")])
                give(i1)
                yield
                ((p2, p2b, i2),) = yield from take(1)
                for e in range(2):
                    mm(p2[:, e * 128:(e + 1) * 128], Dt_[:, e, :], XX[:, 0, e, :], True, False, [Db_, n("XX")], [p2b])
                    mm(p2[:, e * 128:(e + 1) * 128], ident16, D_[:, e, :], False, True, [bf("ident16"), Db_], [p2b])
                    if li < 2:
                        mm(p2[:, 256 + e * 128:256 + (e + 1) * 128], D_[:, e, :], XX[:, 1, e, :], True, False, [Db_, n("XX")], [p2b])
                        mm(p2[:, 256 + e * 128:256 + (e + 1) * 128], ident16, Dt_[:, e, :], False, True, [bf("ident16"), Db_], [p2b])
                yield
                if li < 2:
                    cp("act", RR[nx], v22(p2), [p2b], [n("RR%d" % nx)])
                else:
                    cp("act", RR[nx][:, 0], v2(p2[:, 0:256]), [p2b], [n("RR%d" % nx)])
                give(i2)
                yield
                cur = nx
            TT, TTb = RR[0][:, 0], n("RR0")
            ((pW, pWb, iW),) = yield from take(1)
            for e in range(2):
                mm(pW[:, e * 128:(e + 1) * 128], TT[:, e, :], AZ[:, t, e, :], True, True, [TTb, bf("a_AZ%d" % t), bf("a_AZ")], [pWb])
            yield
            cp("act", WU, v2(pW[:, 0:256]), [pWb], [n("WU")])
            give(iW)
            yield
            (pP, pPb, iP), (pF, pFb, iF) = yield from take(2)
            for e in range(2):
                es = slice(e * 64, e * 64 + 64)
                tpo = (0, 64) if e else None
                bk = Btok[:, t, e * 64:(e + 1) * 64]
                kkk = Ktok[:, t, e * 64:(e + 1) * 64]
                vv = Vtok[:, t, e * 64:(e + 1) * 64]
                mm(pP[es, 0:64], WU[:, e, 0:64], bk, True, True, [n("WU"), bf("a_Btok")], [pPb], tp=tpo)
                mm(pF[es, 0:64], bk, WU[:, e, 64:128], True, False, [n("WU"), bf("a_Btok")], [pFb], tp=tpo)
                mm(pF[es, 0:64], kkk, vv, False, True, [bf("a_Ktok"), VtB], [pFb], tp=tpo)
                mm(pF[es, 64:192], WU[:, e, 0:64], ARB[:, e, :], True, True, [n("WU"), n("ARB")], [pFb], tp=tpo)
                mm(pF[:, 192 + e * 64:192 + (e + 1) * 64], ARB[:, e, :], WU[:, e, 64:128], True, False, [n("WU"), n("ARB")], [pFb])
                mm(pF[:, 192 + e * 64:192 + (e + 1) * 64], KA[:, e, 128:256], vv, False, True, [n("KA"), VtB], [pFb])
            yield
            cp("act", sto["PT"][:, ct, t, d * 64:(d + 1) * 64], pP[:, 0:64], [pPb], [bf("a_PT")])
            ts("dve", sto["G"][:, ct, t, d * 64:(d + 1) * 64], pF[:, 0:64], sto["Gam"][:, ct, t, d:d + 1], None, ALU.mult, None, [pFb, bf("a_Gam")], [bf("a_G")])
            tt("dve", sto["QT"][:, ct, t, d * 128:(d + 1) * 128], pF[:, 64:192], AR16[:, t, 128:256], ALU.add, [pFb, bf("a_AR16")], [bf("a_QT")])
            ydst = sto["Y0"][:, t, ct * 128:(ct + 1) * 128]
            if d == 0:
                cp("dve", ydst, pF[:, 192:320], [pFb], [bf("a_Y0_%d" % t)])
            else:
                tt("dve", ydst, pF[:, 192:320], ydst, ALU.add, [pFb, bf("a_Y0_%d" % t)], [bf("a_Y0_%d" % t)])
            give(iP, iF)
            yield

        def take_now(nb):
            assert len(free_banks) >= nb, "PSUM bank pool exhausted outside a generator"
            out = []
            for _ in range(nb):
                i = free_banks.pop(0)
                out.append((psF[i], psFb[i], i))
            return out

        def run_gens(gens):
            alive = list(gens)
            while alive:
                for g_ in list(alive):
                    try:
                        next(g_)
                    except StopIteration:
                        alive.remove(g_)

        def stage_ct(blk, ct):
            c0 = 0 if blk == "A" else 512
            raw, rb = rawA[0], bf("a_rawA0")
            Vt, Vtb = Vtok2[ct % 2], bf("a_Vtok%d" % (ct % 2))
            for wi, (ft, dst, dn) in enumerate(((ct, rT16, "a_rT"), (4 + ct, kT, "a_kT"), (8 + ct, vT16, "a_vT"))):
                slab, slb = wrkv[ft // 4]
                fo = (ft % 4) * 128
                ((p, pb, ip),) = yield from take(1)
                for kc in range(8):
                    mm(p, slab[:, kc, fo:fo + 128], xnT[:, kc, c0:c0 + 512], kc == 0, kc == 7, [slb, XN[c0 // 512]], [pb])
                if blk == "A":
                    r3 = raw.rearrange("p (s c) -> p s c", c=258)
                    cp("act", r3[:, :, 1:257], p.rearrange("p (s c) -> p s c", c=256), [pb], [rb])
                    give(ip)
                    prev, main, nxt = r3[:, :, 0:256], r3[:, :, 1:257], r3[:, :, 2:258]
                    v3 = lambda a_: a_.rearrange("p (s c) -> p s c", c=256)
                else:
                    ((p2, p2b, ip2),) = yield from take(1)
                    for kc in range(8):
                        mm(p2[:, 0:2], slab[:, kc, fo:fo + 128], xnTh[:, kc, :], kc == 0, kc == 7, [slb, bf("xnTh")], [p2b])
                    cp("act", raw[:, 1:513], p, [pb], [rb])
                    cp("act", raw[:, 0:1], p2[:, 0:1], [p2b], [rb])
                    cp("act", raw[:, 513:514], p2[:, 1:2], [p2b], [rb])
                    give(ip, ip2)
                    prev, main, nxt = raw[:, 0:512], raw[:, 1:513], raw[:, 2:514]
                    v3 = lambda a_: a_
                yield
                act(v3(shtmp), main, AF.Identity, [rb, bf("c0v")], [bf("a_Ex")], scale=c0v[:, ft:ft + 1])
                yield
                stt("dve", v3(shtmp), prev, pv("mu_p", ft), v3(shtmp), ALU.mult, ALU.add, [rb, bf("pvec"), bf("a_Ex")], [bf("a_Ex")])
                stt("dve", v3(dst), nxt, pv("mu_n", ft), v3(shtmp), ALU.mult, ALU.add, [rb, bf("pvec"), bf("a_Ex")], [bf(dn)])
                yield
            act(sq, kT, AF.Square, [bf("a_kT"), bf("pvec")], [bf("a_t1")], scale=pv("k_k", ct))
            ((p, pb, ip),) = yield from take(1)
            mm(p, blk1, sq, True, True, [bf("cst"), bf("a_t1")], [pb])
            yield
            ts("dve", rs, p, 1e-24, None, ALU.max, None, [pb], [bf("a_bT")])
            give(ip)
            act(rs, rs, AF.Sqrt, [bf("a_bT")], [bf("a_bT")])
            yield
            recip(rs, rs, [bf("a_bT")], [bf("a_bT")])
            stt("dve", kk, kT, pv("k_k", ct), rs, ALU.mult, ALU.mult, [bf("a_kT"), bf("pvec"), bf("a_bT")], [bf("a_kk")])
            act(rrk, rT16, AF.Copy, [bf("a_rT"), bf("pvec")], [bf("a_rrk")], scale=pv("r_k", ct))
            p, pb, ib = yield from takeB()
            for t in range(4):
                tr(p[:, t * 128:(t + 1) * 128], vT16[:, t * 128:(t + 1) * 128], ident16, [bf("a_vT"), bf("ident16")], [pb])
            yield
            cp("act", Vt, p[:, 0:512].rearrange("p (a b) -> p a b", b=128), [pb], [Vtb])
            freeB.append(ib)
            yield

        def prep_first(blk, ct, d):
            c0 = 0 if blk == "A" else 512
            sto = STO
            ds = slice(d * 64, d * 64 + 64)
            tpd = (64, 0) if d else None
            ((p, pb, ip),) = yield from take(1)
            for t in range(4):
                mm(p[:, t * 128:(t + 1) * 128], hTd[ds, c0 + t * 128:c0 + (t + 1) * 128], w2dec[ds, ct * 128:(ct + 1) * 128],
                   True, False, [bf("hTd"), bf("w2dec")], [pb], tp=tpd)
                mm(p[:, t * 128:(t + 1) * 128], ones32[0:1, :], w0row[0:1, d * 512 + ct * 128:d * 512 + (ct + 1) * 128],
                   False, True, [bf("ones32"), bf("w0row")], [pb])
            yield
            act(sg32, p.rearrange("p (a b) -> p a b", b=128), AF.Tanh, [pb], [bf("a_sg32")], scale=0.5)
            give(ip)
            yield
            ts("dve", sg32, sg32, 0.5, 0.5, ALU.mult, ALU.add, [bf("a_sg32")], [bf("a_sg32")])
            yield
            (pI, pIb, iI), (pX, pXb, iX), (pa_, pab_, ia_) = yield from take(3)
            for t in range(4):
                mm(pI[:, t * 128:(t + 1) * 128], sg32[:, t, :], cst[:, 1 + 2 * d, :], True, True, [bf("a_sg32"), bf("cst")], [pIb])
                mm(pX[:, t * 128:(t + 1) * 128], sg32[:, t, :], cst[:, 2 + 2 * d, :], True, True, [bf("a_sg32"), bf("cst")], [pXb])
            mm(pa_, a2cat[ds, ct * 128:(ct + 1) * 128], hTi[ds, c0:c0 + 512], True, True, [bf("a2cat"), bf("hTi")], [pab_], tp=tpd)
            yield
            act(Ei, pI, AF.Exp, [pIb], [bf("a_Ei")])
            act(En, pI, AF.Exp, [pIb], [bf("a_En")], scale=-1.0)
            act(Ex, pX, AF.Exp, [pXb], [bf("a_Ex")])
            act(aT, pa_, AF.Tanh, [pab_, bf("a0h")], [bf("a_aT")], bias=a0h[:, d * 4 + ct:d * 4 + ct + 1], scale=0.5)
            give(iI, iX, ia_)
            yield
            lastc = 127 if d == 0 else 0
            cp("dve", sto["Gam"][:, ct, :, d], Ei.rearrange("p (t c) -> p t c", c=128)[:, :, lastc], [bf("a_Ei")], [bf("a_Gam")])
            ts("dve", t1, aT, -1.0, kah[:, ct:ct + 1], ALU.add, ALU.mult, [bf("a_aT"), bf("kah")], [bf("a_t1")])
            yield
            stt("dve", kd[d], t1, 1.0, kT, ALU.add, ALU.mult, [bf("a_t1"), bf("a_kT")], [bf("a_kd%d" % d)])
            stt("dve", bT, aT, 1.0, kk, ALU.add, ALU.mult, [bf("a_kk"), bf("a_aT")], [bf("a_bT")])
            yield

        def prep_second(blk, ct, d):
            v4 = lambda a_: a_.rearrange("p (t c) -> p t c", c=128)
            stt("dve", AR16[:, :, 0:128], v4(kk), -1.0, v4(Ex), ALU.mult, ALU.mult, [bf("a_kk"), bf("a_Ex")], [bf("a_AR16")])
            tt("dve", AR16[:, :, 128:256], v4(rT16), v4(Ei), ALU.mult, [bf("a_rT"), bf("a_Ei")], [bf("a_AR16")])
            stt("dve", BT16, bT, 0.5, En, ALU.mult, ALU.mult, [bf("a_bT"), bf("a_En")], [bf("a_BT16")])
            for e_ in range(2):
                act(AR16m[e_], AR16, AF.Copy, [bf("a_AR16"), bf("hsel")], [bf("a_AR16m")], scale=hsel[:, e_:e_ + 1])
            tt("dve", KT16, kd[d], En, ALU.mult, [bf("a_kd%d" % d), bf("a_En")], [bf("a_KT16")])

        def prep_late(blk, ct, d):
            p, pb, ib = yield from takeB()
            for t in range(4):
                tr(p[:, t * 128:(t + 1) * 128], AR16[:, t, 0:128], ident16, [bf("a_AR16"), bf("ident16")], [pb])
            yield
            for e_ in range(2):
                cp("act", AZ[:, :, e_, 0:64], p[:, 0:512].rearrange("p (t c) -> p t c", c=128)[:, :, e_ * 64:(e_ + 1) * 64], [pb], [bf("a_AZ")])
            freeB.append(ib)
            yield
            p, pb, ib = yield from takeB()
            for t in range(4):
                tr(p[:, t * 128:(t + 1) * 128], BT16[:, t * 128:(t + 1) * 128], ident16, [bf("a_BT16"), bf("ident16")], [pb])
            yield
            cp("dve", Btok, p[:, 0:512].rearrange("p (a b) -> p a b", b=128), [pb], [bf("a_Btok")])
            freeB.append(ib)
            yield
            p, pb, ib = yield from takeB()
            for t in range(4):
                tr(p[:, t * 128:(t + 1) * 128], KT16[:, t * 128:(t + 1) * 128], ident16, [bf("a_KT16"), bf("ident16")], [pb])
            yield
            cp("act", Ktok, p[:, 0:512].rearrange("p (a b) -> p a b", b=128), [pb], [bf("a_Ktok")])
            freeB.append(ib)
            yield

        def bonus_ct(ct):
            sto = STO
            tt("dve", t1, kd[0], kd[1], ALU.add, [bf("a_kd0"), bf("a_kd1")], [bf("a_t1")])
            tt("dve", t1, t1, rrk, ALU.mult, [bf("a_t1"), bf("a_rrk")], [bf("a_t1")])
            ((p, pb, ip),) = take_now(1)
            mm(p, blk1, t1, True, True, [bf("cst"), bf("a_t1")], [pb])
            tt("dve", sto["bonus"][:, ct, :], p, vT16, ALU.mult, [pb, bf("a_vT")], [bf("a_bonus")])
            give(ip)

        def first_gen(blk, ct, d):
            if d == 0:
                yield from stage_ct(blk, ct)
            yield from prep_first(blk, ct, d)

        def rwkv_block(blk):
            seq = [(ct, d) for ct in range(4) for d in range(2)]
            run_gens([first_gen(blk, 0, 0)])
            for i_, (ct, d) in enumerate(seq):
                prep_second(blk, ct, d)
                if d == 1:
                    bonus_ct(ct)
                gens = [prep_late(blk, ct, d)] + [group_gen(blk, ct, d, t_, j_) for j_, t_ in enumerate(range(4))]
                if i_ + 1 < len(seq):
                    gens.append(first_gen(blk, *seq[i_ + 1]))
                run_gens(gens)

        def recur_gen(blk, tiles, d, init, useG=True, saveH=True, hs=0, after=None):
            sto = STO
            Hst, Hs16, Htmp = HS[hs]["Hst"], HS[hs]["Hs16"], HS[hs]["Htmp"]
            HB, H16B, HTB = bf("a_Hst%d" % hs), bf("a_Hs16%d" % hs), bf("a_Htmp%d" % hs)
            if init is None:
                memset("dve", Hst, 0.0, [HB])
            else:
                cp("dve", Hst, init[0], [init[1]], [HB])
            for t in tiles:
                cp("act", Hs16, Hst, [HB], [H16B])
                if saveH:
                    cp("act", sto["H0"][:, :, t, d * 64:(d + 1) * 64], Hst, [HB], [bf("a_H0_%d_%d" % (t, d))])
                (p0, p0b, i0), (p1, p1b, i1) = yield from take(2)
                pe2 = [(p0, p0b), (p1, p1b)]
                for ct in range(4):
                    for e in range(2):
                        es = slice(e * 64, e * 64 + 64)
                        mm(pe2[e][0][es, ct * 64:(ct + 1) * 64], sto["PT"][es, ct, t, d * 64:(d + 1) * 64], Hs16[es, ct, :], True, True,
                           [bf("a_PT"), H16B], [pe2[e][1]], tp=(64, 64) if e else None)
                yield
                for e in range(2):
                    es = slice(e * 64, e * 64 + 64)
                    tt("dve", Htmp[es], pe2[e][0][es, 0:256].rearrange("p (a b) -> p a b", b=64), Hst[es], ALU.add, [pe2[e][1], HB], [HTB])
                give(i0, i1)
                yield
                for ct in range(4):
                    if useG:
                        stt("dve", Hst[:, ct, :], Htmp[:, ct, :], sto["Gam"][:, ct, t, d:d + 1], sto["G"][:, ct, t, d * 64:(d + 1) * 64],
                            ALU.mult, ALU.add, [HTB, bf("a_Gam"), bf("a_G")], [HB])
                    else:
                        ts("dve", Hst[:, ct, :], Htmp[:, ct, :], sto["Gam"][:, ct, t, d:d + 1], None, ALU.mult, None,
                           [HTB, bf("a_Gam")], [HB])
                yield
            if after is not None:
                yield from after(Hst, HB)

        def rwkv_out_gen(blk, t, k):
            sto = STO
            c0 = 0 if blk == "A" else 512
            T_ = OT[k]
            ytok, ysq, yn16, gst, ynT = T_["ytok"], T_["ysq"], T_["yn16"], T_["gst"], T_["ynT"]
            n = lambda s_: bf("a_o%s_%d" % (s_, k))
            (p0, p0b, i0), (p1, p1b, i1) = yield from take(2)
            py2 = [(p0, p0b), (p1, p1b)]
            for h in range(8):
                ct, e = h // 2, h % 2
                es = slice(e * 64, e * 64 + 64)
                for d in range(2):
                    mm(py2[e][0][:, ct * 64:(ct + 1) * 64], sto["QT"][es, ct, t, d * 128:(d + 1) * 128], sto["H0"][es, ct, t, d * 64:(d + 1) * 64],
                       d == 0, d == 1, [bf("a_QT"), bf("a_H0_%d_%d" % (t, d))], [py2[e][1]], tp=(64, 0) if e else None)
            yield
            yt4 = ytok.rearrange("p (c e v) -> p c e v", e=2, v=64)
            y04 = sto["Y0"][:, t, :].rearrange("p (c e v) -> p c e v", e=2, v=64)
            for e in range(2):
                tt("dve", yt4[:, :, e, :], py2[e][0][:, 0:256].rearrange("p (c v) -> p c v", v=64), y04[:, :, e, :], ALU.add,
                   [py2[e][1], bf("a_Y0_%d" % t)], [n("ytok")])
            give(i0, i1)
            yield
            y3 = ytok.rearrange("p (h v) -> p h v", v=64)
            S.op("dve", lambda e_, y3=y3, gst=gst: e_.reduce_sum(gst[:, 0:8], y3, AX.X), reads=[n("ytok")], writes=[n("gst")])
            act(ysq, ytok, AF.Square, [n("ytok")], [n("ysq")])
            yield
            S.op("dve", lambda e_, ysq=ysq, gst=gst: e_.reduce_sum(gst[:, 8:16], ysq.rearrange("p (h v) -> p h v", v=64), AX.X), reads=[n("ysq")], writes=[n("gst")])
            ts("dve", gst[:, 16:24], gst[:, 0:8], 1.0 / 64, None, ALU.mult, None, [n("gst")], [n("gst")])
            tt("dve", gst[:, 24:32], gst[:, 16:24], gst[:, 16:24], ALU.mult, [n("gst")], [n("gst")])
            stt("dve", gst[:, 32:40], gst[:, 8:16], 1.0 / 64, gst[:, 24:32], ALU.mult, ALU.subtract, [n("gst")], [n("gst")])
            yield
            act(gst[:, 40:48], gst[:, 32:40], AF.Sqrt, [n("gst"), bf("epsr")], [n("gst")], bias=epsr[:, 1:2])
            yield
            recip(gst[:, 40:48], gst[:, 40:48], [n("gst")], [n("gst")])
            mb = gst[:, 16:24].unsqueeze(2).to_broadcast([128, 8, 64])
            rb_ = gst[:, 40:48].unsqueeze(2).to_broadcast([128, 8, 64])
            tt("dve", y3, y3, mb, ALU.subtract, [n("ytok"), n("gst")], [n("ytok")])
            tt("dve", yn16.rearrange("p (h v) -> p h v", v=64), y3, rb_, ALU.mult, [n("ytok"), n("gst")], [n("yn16")])
            yield
            pT, pTb, iT = yield from takeB()
            ((pg_, pgb_, ig),) = yield from take(1)
            for ct in range(4):
                tr(pT[:, ct * 128:(ct + 1) * 128], yn16[:, ct * 128:(ct + 1) * 128], ident16, [n("yn16"), bf("ident16")], [pTb])
            for ct in range(4):
                mm(pg_[:, ct * 128:(ct + 1) * 128], g2[:, ct * 128:(ct + 1) * 128], hTg[:, c0 + t * 128:c0 + (t + 1) * 128], True, True,
                   [bf("g2"), bf("hTg")], [pgb_])
            yield
            for ct in range(4):
                act(ynT[:, ct, :], pT[:, ct * 128:(ct + 1) * 128], AF.Identity, [pTb, bf("pvec")], [n("ysq")],
                    bias=pv("lnx_b", ct), scale=pv("lnx_g", ct))
            freeB.append(iT)
            yield
            tt("dve", ynT, ynT, sto["bonus"][:, :, t * 128:(t + 1) * 128], ALU.add, [n("ysq"), bf("a_bonus")], [n("ysq")])
            tt("dve", outT16[:, :, c0 + t * 128:c0 + (t + 1) * 128], ynT, pg_.rearrange("p (c t) -> p c t", t=128), ALU.mult,
               [n("ysq"), pgb_], [bf("outT16")])
            give(ig)
            yield

        def rwkv_out(blk, tiles):
            inherit(OT_NAMES, GB01_NAMES)
            run_gens([rwkv_out_gen(blk, t, k) for k, t in enumerate(tiles)])

        rwkv_block("A")
        chk("blockA")
        dump("PT", STO["PT"].rearrange("p a b c -> p (a b c)"), [bf("a_PT")]); dump("G", STO["G"].rearrange("p a b c -> p (a b c)"), [bf("a_G")])
        dump("QT", STO["QT"].rearrange("p a b c -> p (a b c)"), [bf("a_QT")]); dump("Y0", STO["Y0"].rearrange("p a b -> p (a b)"), [bf("a_Y0")]); dump("Gam", STO["Gam"].rearrange("p a b c -> p (a b c)"), [bf("a_Gam")])
        stv = st_d.rearrange("s d p c v -> s d p (c v)")
        inherit(HS_NAMES, GB3_NAMES)

        def out_state(seg, d):
            def f(Hst_, HB_):
                S.dma("sp", stv[seg, d], Hst_.rearrange("p c v -> p (c v)"), reads=[HB_], is_output=True)
                return
                yield
            return f
        run_gens([recur_gen("A", ([2 * seg, 2 * seg + 1] if d == 0 else [2 * seg + 1, 2 * seg]), d, None, hs=seg * 2 + d, after=out_state(seg, d))
                  for seg in range(2) for d in range(2)])
        chk("recurA")
        rwkv_out("A", range(4))
        chk("outA")
        inherit(GB3_NAMES, HS_NAMES)
        inherit(GB01_NAMES, OT_NAMES)
        rwkv_block("B")
        inherit(HS_NAMES, GB3_NAMES)
        wa, wab = loadw(w_in_v[:, :, 1536:2048], lambda w: w.rearrange("p (kc n) -> p kc n", kc=8), slot=1)
        wg, wgb = loadw(w_in_v[:, :, 2048:2560], lambda w: w.rearrange("p (kc n) -> p kc n", kc=8), slot=2)
        save_ptr = AR.ptr
        AR.ptr = alias_ptr
        XS = AR.alloc([128, 2, 2, 4, 64])
        G4 = AR.alloc([128, 4, 1024])
        ctmp = AR.alloc([128, 4, 64])
        assert AR.ptr <= alias_ptr + 5888
        AR.ptr = save_ptr
        retired = [bf(n) for n in ("a_rrk", "a_kk", "a_Vtok0", "a_Vtok1", "a_sg32", "a_Ei", "a_Ex", "a_En", "a_aT", "a_t1", "a_bT", "a_kd0", "a_kd1")]
        def after_N(d):
            def f(Hst_, HB_):
                cp("dve", XS[:, d, 1], Hst_, [HB_], [bf("a_XS")] + retired)
                return
                yield
            return f

        def after_M(d):
            def f(Hst_, HB_):
                (p0, p0b, i0), (p1, p1b, i1) = yield from take(2)
                pe2 = [(p0, p0b), (p1, p1b)]
                for ct in range(4):
                    for e in range(2):
                        es = slice(e * 64, e * 64 + 64)
                        mm(pe2[e][0][es, ct * 64:(ct + 1) * 64], Hst_[es, ct, :], cst[es, 0, e * 64:(e + 1) * 64], True, True,
                           [HB_, bf("cst")], [pe2[e][1]], tp=(64, 64) if e else None)
                yield
                for e in range(2):
                    es = slice(e * 64, e * 64 + 64)
                    cp("dve", XS[es, d, 0], pe2[e][0][es, 0:256].rearrange("p (a b) -> p a b", b=64), [pe2[e][1]], [bf("a_XS")] + retired)
                give(i0, i1)
            return f
        gl = []
        for d in range(2):
            tiles = [0, 1, 2, 3] if d == 0 else [3, 2, 1, 0]
            gl.append(recur_gen("B", tiles, d, None, useG=True, saveH=False, hs=2 * d, after=after_N(d)))
            gl.append(recur_gen("B", tiles, d, (idh, bf("idh")), useG=False, saveH=False, hs=2 * d + 1, after=after_M(d)))
        run_gens(gl)
        S.dma("pool", bounce_d, XS.rearrange("p a b c d -> p (a b c d)"), reads=[bf("a_XS")], writes=[bf("bounce")])
        S.coll(lambda en: en.collective_compute("AllGather", ALU.bypass, replica_groups=[[0, 1, 2, 3], [4, 5, 6, 7]],
                                                ins=[bounce_d.opt()], outs=[gath_d.opt()]),
               reads=[bf("bounce")], writes=[bf("gath")])
        S.dma("pool", G4, gath_d.rearrange("(r p) n -> p r n", p=128), reads=[bf("gath")], writes=[bf("a_G4")])
        G4v = G4.rearrange("p r (d m c v) -> p r d m c v", d=2, m=2, c=4)
        ctmps = [ctmp, HS[3]["Htmp"]]

        def compose_gen(d):
            HB = bf("Hin%d" % d)
            ct_, ctb_ = ctmps[d], bf("a_ctmp%d" % d)
            order = [0, 1, 2] if d == 0 else [3, 2, 1]
            for j in order:
                (p0, p0b, i0), (p1, p1b, i1) = yield from take(2)
                pe2 = [(p0, p0b), (p1, p1b)]
                for ct in range(4):
                    for e in range(2):
                        es = slice(e * 64, e * 64 + 64)
                        mm(pe2[e][0][es, ct * 64:(ct + 1) * 64], G4v[es, j, d, 0, ct, :], Hin[es, d, ct, :], True, True,
                           [bf("a_G4"), HB], [pe2[e][1]], tp=(64, 64) if e else None)
                yield
                for e in range(2):
                    es = slice(e * 64, e * 64 + 64)
                    tt("dve", ct_[es], pe2[e][0][es, 0:256].rearrange("p (a b) -> p a b", b=64), G4v[es, j, d, 1], ALU.add,
                       [pe2[e][1], bf("a_G4")], [ctb_])
                give(i0, i1)
                yield
                tt("dve", ct_, ct_, Hin[:, d], ALU.subtract, [ctb_, HB], [ctb_])
                stt("dve", Hin[:, d], ct_, selv[:, d * 4 + j:d * 4 + j + 1], Hin[:, d], ALU.mult, ALU.add, [ctb_, bf("selv"), HB], [HB])
                yield
        bf("a_ctmp1").r = list(bf("a_ctmp1").r) + list(bf("a_Htmp3").r) + ([bf("a_Htmp3").w] if bf("a_Htmp3").w is not None else [])
        run_gens([compose_gen(0), compose_gen(1)])
        bf("a_Htmp3").r = list(bf("a_Htmp3").r) + list(bf("a_ctmp1").r) + ([bf("a_ctmp1").w] if bf("a_ctmp1").w is not None else [])
        run_gens([recur_gen("B", ([0, 1, 2, 3] if d == 0 else [3, 2, 1, 0]), d, (Hin[:, d], bf("Hin%d" % d)), hs=d) for d in range(2)])
        rwkv_out("B", range(4))
        dump("outT", outT16.rearrange("p c n -> p (c n)"), [bf("outT16")])
        chk("rwkv")

        new_phase()
        x_sb = AR.alloc([128, 8, 1024])
        for t in range(8):
            S.dma("sp", x_sb[:, t, :], xm[t * 128:(t + 1) * 128, :], writes=[bf("xt%d" % t)])
        cv_ptr = AR.ptr
        cv = AR.alloc([128, 4, 1024])
        upA = [AR.alloc([128, 2, 286], BF16) for _ in range(2)]
        upB = [AR.alloc([128, 8, 94], BF16) for _ in range(2)]
        dgw_ptr = AR.ptr
        dgw = [AR.alloc([128, 31, 128], BF16) for _ in range(2)]
        ucT = AR.alloc([128, 4, 1024], BF16)
        mergedT = AR.alloc([128, 8, 1024], BF16)
        tmpa = [AR.alloc([128, 512]) for _ in range(2)]
        tmpb = [AR.alloc([128, 512]) for _ in range(2)]
        lnm = AR.alloc([128, 512]); lnr = AR.alloc([128, 512])
        g1rep = [AR.alloc([128, 1024]) for _ in range(2)]
        dg = AR.alloc([128, 128])
        for i in range(2):
            memset("dve", upA[i], 0.0, [bf("a_upA%d" % i)])
            memset("dve", upB[i], 0.0, [bf("a_upB%d" % i)])
        def glu_proj(ct, half):
            (pa, pab), (pg, pgb) = getF(), getF()
            for kc in range(8):
                mm(pa, wa[:, kc, ct * 128:(ct + 1) * 128], xnT[:, kc, half * 512:(half + 1) * 512], kc == 0, kc == 7, [wab, XN[half]], [pab])
            for kc in range(8):
                mm(pg, wg[:, kc, ct * 128:(ct + 1) * 128], xnT[:, kc, half * 512:(half + 1) * 512], kc == 0, kc == 7, [wgb, XN[half]], [pgb])
            sgt = tmpa[half]
            act(sgt, pg, AF.Sigmoid, [pgb], [bf("a_tmpa%d" % half)])
            if half == 0:
                up, upn, L = upA[ct % 2], "a_upA%d" % (ct % 2), 256
            else:
                up, upn, L = upB[ct % 2], "a_upB%d" % (ct % 2), 64
            v3 = lambda a_, L=L: a_.rearrange("p (r c) -> p r c", c=L)
            tt("dve", up[:, :, 15:15 + L], v3(pa), v3(sgt), ALU.mult, [pab, bf("a_tmpa%d" % half)], [bf(upn)])
            if half == 0:
                dw, dwb = dgw[ct % 2], bf("a_dgw%d" % (ct % 2))
                for j in range(31):
                    if j % 2:
                        act(dw[:, j, :], ident16, AF.Copy, [bf("ident16"), bf("pvec")], [dwb], scale=pv("conv_w", j * 4 + ct))
                    else:
                        ts("dve", dw[:, j, :], ident16, pv("conv_w", j * 4 + ct), None, ALU.mult, None, [bf("ident16"), bf("pvec")], [dwb])

        def conv_mm(ct, half):
            if half == 0:
                up, upn, L = upA[ct % 2], "a_upA%d" % (ct % 2), 256
            else:
                up, upn, L = upB[ct % 2], "a_upB%d" % (ct % 2), 64
            dw, dwb = dgw[ct % 2], bf("a_dgw%d" % (ct % 2))
            pcv, pcvb = getF()
            for j in range(31):
                mm(pcv, dw[:, j, :], up[:, :, j:j + L], j == 0, j == 30, [dwb, bf(upn)], [pcvb])
            act(cv[:, ct, half * 512:(half + 1) * 512], pcv, AF.Identity, [pcvb, bf("pvec")], [bf("a_cv%d_%d" % (ct, half))], bias=pv("conv_b", ct))

        seq_c = [(ct, half) for ct in range(4) for half in range(2)]
        glu_proj(*seq_c[0])
        for i_, ch_ in enumerate(seq_c):
            if i_ + 1 < len(seq_c):
                glu_proj(*seq_c[i_ + 1])
            conv_mm(*ch_)
        for half in range(2):
            hs = slice(half * 512, (half + 1) * 512)
            (pm, pmb), (pq, pqb) = getF(), getF()
            for ct in range(4):
                mm(pm, cst[:, 6, :], cv[:, ct, hs], ct == 0, ct == 3, [bf("cst"), bf("a_cv%d_%d" % (ct, half))], [pmb])
            for ct in range(4):
                sqt = tmpb[ct % 2]
                act(sqt, cv[:, ct, hs], AF.Square, [bf("a_cv%d_%d" % (ct, half))], [bf("a_tmpb%d" % (ct % 2))])
                mm(pq, cst[:, 6, :], sqt, ct == 0, ct == 3, [bf("cst"), bf("a_tmpb%d" % (ct % 2))], [pqb])
            cp("act", lnm, pm, [pmb], [bf("a_lnm")])
            tt("dve", lnr, lnm, lnm, ALU.mult, [bf("a_lnm")], [bf("a_lnr")])
            tt("dve", lnr, pq, lnr, ALU.subtract, [pqb, bf("a_lnr")], [bf("a_lnr")])
            act(lnr, lnr, AF.Sqrt, [bf("a_lnr"), bf("epsr")], [bf("a_lnr")], bias=epsr[:, 2:3])
            recip(lnr, lnr, [bf("a_lnr")], [bf("a_lnr")])
            for ct in range(4):
                tq = tmpb[ct % 2]; tqb = bf("a_tmpb%d" % (ct % 2))
                tt("dve", tq, cv[:, ct, hs], lnm, ALU.subtract, [bf("a_cv%d_%d" % (ct, half)), bf("a_lnm")], [tqb])
                tt("dve", tq, tq, lnr, ALU.mult, [tqb, bf("a_lnr")], [tqb])
                act(ucT[:, ct, hs], tq, AF.Silu, [tqb, bf("pvec")], [bf("a_ucT")], bias=pv("cln_b", ct), scale=pv("cln_g", ct))
        dump("ucT", ucT.rearrange("p c n -> p (c n)"), [bf("a_ucT")])
        wr, wrb = loadw(wor_d.rearrange("(kc p) n -> p kc n", p=128), lambda w: w.rearrange("p (kc n) -> p kc n", kc=4), slot=3)
        wc, wcb = loadw(woc_d.rearrange("(kc p) n -> p kc n", p=128), lambda w: w.rearrange("p (kc n) -> p kc n", kc=4), slot=0)
        for g in range(2):
            wgr, wgrb = loadw(w_in_v[:, :, 2560 + g * 512:2560 + (g + 1) * 512], lambda w: w.rearrange("p (kc n) -> p kc n", kc=8), slot=1)
            wgc, wgcb = loadw(w_in_v[:, :, 3584 + g * 512:3584 + (g + 1) * 512], lambda w: w.rearrange("p (kc n) -> p kc n", kc=8), slot=2)
            for f4 in range(4):
                fo = g * 4 + f4
                for half in range(2):
                    hs = slice(half * 512, (half + 1) * 512)
                    (pr, prb), (pc, pcb), (pgr, pgrb), (pgc, pgcb) = getF(), getF(), getF(), getF()
                    for kc in range(4):
                        mm(pr, wr[:, kc, fo * 128:(fo + 1) * 128], outT16[:, kc, hs], kc == 0, kc == 3, [wrb, bf("outT16")], [prb])
                    for kc in range(4):
                        mm(pc, wc[:, kc, fo * 128:(fo + 1) * 128], ucT[:, kc, hs], kc == 0, kc == 3, [wcb, bf("a_ucT")], [pcb])
                    for kc in range(8):
                        mm(pgr, wgr[:, kc, f4 * 128:(f4 + 1) * 128], xnT[:, kc, hs], kc == 0, kc == 7, [wgrb, XN[half]], [pgrb])
                    for kc in range(8):
                        mm(pgc, wgc[:, kc, f4 * 128:(f4 + 1) * 128], xnT[:, kc, hs], kc == 0, kc == 7, [wgcb, XN[half]], [pgcb])
                    ta, tab, tb_, tbb = tmpa[half], bf("a_tmpa%d" % half), tmpb[half], bf("a_tmpb%d" % half)
                    act(ta, pgr, AF.Sigmoid, [pgrb], [tab])
                    act(tb_, pgc, AF.Sigmoid, [pgcb], [tbb])
                    tt("dve", ta, pr, ta, ALU.mult, [prb, tab], [tab])
                    tt("dve", tb_, pc, tb_, ALU.mult, [pcb, tbb], [tbb])
                    tt("dve", mergedT[:, fo, hs], ta, tb_, ALU.add, [tab, tbb], [bf("a_mergedT")])

        def bcast_rows(dst_list, col0, tag):
            for j in range(2):
                for hh in range(2):
                    p, pb = getF()
                    for k4 in range(4):
                        kc = hh * 4 + k4
                        ts("dve", dg, ident32, mod[:, col0 + kc, j:j + 1], None, ALU.mult, None, [bf("cst"), bf("mod")], [bf("a_dg")])
                        mm(p[:, k4 * 128:(k4 + 1) * 128], cst[:, 7, :], dg, True, True, [bf("cst"), bf("a_dg")], [pb])
                    cp("act", dst_list[j][:, hh * 512:(hh + 1) * 512], p, [pb], [bf("a_%s%d" % (tag, j))])

        bcast_rows(g1rep, 16, "g1rep")
        wo_v = wo_d.rearrange("(kc p) n -> p kc n", p=128)
        sp3_ = AR.ptr
        AR.ptr = cv_ptr
        xs16c = AR.alloc([128, 8, 1024], BF16)
        AR.ptr = dgw_ptr
        junkc = AR.alloc([128, 1024])
        AR.ptr = sp3_
        inherit(["a_xs16c_%d" % t_ for t_ in range(8)] + ["a_junkc"],
                ["a_cv%d_%d" % (c_, h_) for c_ in range(4) for h_ in range(2)] + ["a_dgw0", "a_dgw1"])
        wos = [loadw(wo_v[:, :, nh * 512:(nh + 1) * 512], lambda w: w.rearrange("p (kc n) -> p kc n", kc=8), slot=3 * nh) for nh in range(2)]
        for t in range(8):
            j = 0 if t < 4 else 1
            for nh in range(2):
                ns = slice(nh * 512, (nh + 1) * 512)
                wo, wob = wos[nh]
                p, pb = getF()
                for kc in range(8):
                    mm(p, mergedT[:, kc, t * 128:(t + 1) * 128], wo[:, kc, :], kc == 0, kc == 7, [bf("a_mergedT"), wob], [pb])
                ta, tab = tmpa[nh], bf("a_tmpa%d" % nh)
                tt("dve", ta, p, g1rep[j][:, ns], ALU.mult, [pb, bf("a_g1rep%d" % j)], [tab])
                tt("dve", x_sb[:, t, ns], ta, x_sb[:, t, ns], ALU.add, [tab, bf("xt%d" % t)], [bf("xt%d" % t)])
            sb_ = bf("a_ssn%d" % t)
            memset("dve", ss[:, t:t + 1], 0.0, [sb_])
            act(junkc, x_sb[:, t, :], AF.Square, [bf("xt%d" % t)], [bf("a_junkc"), sb_], accum=ss[:, t:t + 1])
            act(rstd[:, t:t + 1], ss[:, t:t + 1], AF.Sqrt, [sb_, bf("epsr")], [sb_], bias=epsr[:, 0:1], scale=1.0 / 1024)
            recip(rstd[:, t:t + 1], rstd[:, t:t + 1], [sb_], [sb_])
            if t % 2 == 0:
                ts("dve", xs16c[:, t, :], x_sb[:, t, :], rstd[:, t:t + 1], None, ALU.mult, None, [bf("xt%d" % t), sb_], [bf("a_xs16c_%d" % t)])
            else:
                act(xs16c[:, t, :], x_sb[:, t, :], AF.Copy, [bf("xt%d" % t), sb_], [bf("a_xs16c_%d" % t)], scale=rstd[:, t:t + 1])
            if t % 4 == 3:
                half = t // 4
                for kc in range(8):
                    p, pb = getB()
                    for q in range(4):
                        t_ = half * 4 + q
                        tr(p[:, q * 128:(q + 1) * 128], xs16c[:, t_, kc * 128:(kc + 1) * 128], ident16, [bf("a_xs16c_%d" % t_), bf("ident16")], [pb])
                    act(xnT[:, kc, half * 512:(half + 1) * 512], p[:, 0:512], AF.Identity, [pb, bf("A2"), bf("mod")],
                        [bf("xnT%d" % half)], bias=mod[:, 24 + kc, half:half + 1], scale=A2[:, kc, half:half + 1])
        chk("phaseC")

        new_phase()
        x_sb = AR.alloc([128, 8, 1024])
        h16T = AR.alloc([128, 32, 1024], BF16)
        g2rep = [AR.alloc([128, 1024]) for _ in range(2)]
        fgrep = AR.alloc([128, 1024])
        rtmp = [AR.alloc([128, 512]) for _ in range(2)]
        dg = AR.alloc([128, 128])
        ytile = [AR.alloc([128, 1024]) for _ in range(2)]
        S.dma("sp", fgrep, fgrep_d, writes=[bf("a_fgrep")])
        bcast_rows(g2rep, 40, "g2rep")
        w1_v = w1_d.rearrange("(kc p) n -> p kc n", p=128)
        k_ = 0
        for s_ in range(8):
            w1s, w1b = loadw(w1_v[:, :, s_ * 512:(s_ + 1) * 512], lambda w: w.rearrange("p (kc n) -> p kc n", kc=8))
            for m in range(4):
                ff = s_ * 4 + m
                for half in range(2):
                    hs = slice(half * 512, (half + 1) * 512)
                    p, pb = getF()
                    for kc in range(8):
                        mm(p, w1s[:, kc, m * 128:(m + 1) * 128], xnT[:, kc, hs], kc == 0, kc == 7, [w1b, XN[0], XN[1]], [pb])
                    rt, rtb = rtmp[k_ % 2], bf("a_rtmp%d" % (k_ % 2))
                    act(rt, p, AF.Relu, [pb], [rtb])
                    tt("dve", h16T[:, ff, hs], rt, rt, ALU.mult, [rtb], [bf("a_h16T%d" % ff)])
                    k_ += 1
        w2_v = w2_d.rearrange("(fc p) n -> p fc n", p=128)
        for nh in range(2):
            ns = slice(nh * 512, (nh + 1) * 512)
            slabs = [loadw(w2_v[:, 8 * q_:8 * q_ + 8, ns], lambda w: w.rearrange("p (fc n) -> p fc n", fc=8), slot=q_) for q_ in range(4)]
            for t in range(8):
                j = 0 if t < 4 else 1
                p, pb = getF()
                for ff in range(32):
                    w2s, w2b = slabs[ff // 8]
                    mm(p, h16T[:, ff, t * 128:(t + 1) * 128], w2s[:, ff % 8, :], ff == 0, ff == 31, [bf("a_h16T%d" % ff), w2b], [pb])
                rt, rtb = rtmp[k_ % 2], bf("a_rtmp%d" % (k_ % 2))
                tt("dve", rt, p, g2rep[j][:, ns], ALU.mult, [pb, bf("a_g2rep%d" % j)], [rtb])
                tt("dve", x_sb[:, t, ns], rt, x_sb[:, t, ns], ALU.add, [rtb, bf("xt%d" % t)], [bf("xt%d" % t)])
                k_ += 1
                if nh == 1:
                    yt, ytb = ytile[t % 2], bf("a_ytile%d" % (t % 2))
                    sb_ = bf("a_ssf%d" % t)
                    memset("dve", ss[:, 8 + t:9 + t], 0.0, [sb_])
                    act(yt, x_sb[:, t, :], AF.Square, [bf("xt%d" % t)], [ytb, sb_], accum=ss[:, 8 + t:9 + t])
                    act(rstd[:, 8 + t:9 + t], ss[:, 8 + t:9 + t], AF.Sqrt, [sb_, bf("epsr")], [sb_], bias=epsr[:, 0:1], scale=1.0 / 1024)
                    recip(rstd[:, 8 + t:9 + t], rstd[:, 8 + t:9 + t], [sb_], [sb_])
                    stt("dve", yt, x_sb[:, t, :], rstd[:, 8 + t:9 + t], fgrep, ALU.mult, ALU.mult, [bf("xt%d" % t), sb_, bf("a_fgrep")], [ytb])
                    S.dma("sp", y_d[t * 128:(t + 1) * 128, :], yt, reads=[ytb], is_output=True)


    try:
        _rest()
    except _Stop:
        pass
    S.emit()
    st.close()
    return nc


def prep_inputs(inp):
    f = lambda k: np.asarray(inp[k], np.float32)
    xp, xs = f("x_prompt"), f("x_sample")
    pv = np.zeros((128, NPV), np.float32)

    def put(name, arr):
        a = _fm(arr)
        pv[:, PV_OFF[name]:PV_OFF[name] + a.shape[1]] = a
    put("ada_b", f("ada_b")[0]); put("n1g", f("norm1_g")[0]); put("n2g", f("norm2_g")[0])
    put("mu_p", f("mu_prev")[0]); put("mu_n", f("mu_next")[0])
    put("a0f", f("iclr_a0")[0, 0]); put("a0b", f("iclr_a0")[0, 1])
    put("k_k", f("k_k")[0]); put("k_a", f("k_a")[0]); put("r_k", f("r_k")[0].reshape(-1))
    put("lnx_g", f("lnx_g")[0]); put("lnx_b", f("lnx_b")[0]); put("conv_b", f("conv_b")[0])
    put("cln_g", f("conv_ln_g")[0]); put("cln_b", f("conv_ln_b")[0])
    cw = f("conv_w")[0]
    cwp = np.concatenate([_fm(cw[j]) for j in range(31)], axis=1)
    pv[:, PV_OFF["conv_w"]:PV_OFF["conv_w"] + 124] = cwp
    shared = dict(
        pvec=pv,
        w0row=np.ascontiguousarray(f("decay_w0")[0].reshape(1, 1024)),
        fgrep=np.ascontiguousarray(np.broadcast_to(f("final_g")[None, :], (128, 1024))),
        ident=np.eye(128, dtype=np.float32),
        w1cat=np.ascontiguousarray(np.concatenate([f("decay_w1")[0, 0], f("decay_w1")[0, 1], f("iclr_a1")[0, 0],
                                                   f("iclr_a1")[0, 1], f("gate_g1")[0]], axis=1)),
        w2dec=np.ascontiguousarray(np.concatenate([f("decay_w2")[0, 0], f("decay_w2")[0, 1]], axis=0)),
        a2cat=np.ascontiguousarray(np.concatenate([f("iclr_a2")[0, 0], f("iclr_a2")[0, 1]], axis=0)),
        g2=f("gate_g2")[0],
        w_in=f("w_in")[0],
        w_out_rwkv=f("w_out_rwkv")[0], w_out_conv=f("w_out_conv")[0], w_o=f("w_o")[0],
        mlp_w1=f("mlp_w1")[0], mlp_w2=f("mlp_w2")[0],
    )
    cst, msk, id4, mk = _consts()
    shared.update(cst=cst, msk=msk, id4=id4, mk=mk)
    shared.pop("ident")
    in_maps = []
    for c in range(NCORES):
        b, q = c // 4, c % 4
        xmc = np.concatenate([xp[2 * c], xp[2 * c + 1], xs[b, q * 512:(q + 1) * 512]], axis=0)
        xhc = np.zeros((2, 1024), np.float32)
        hmk = np.zeros((128, 8, 2), np.float32)
        if q > 0:
            xhc[0] = xs[b, q * 512 - 1]; hmk[:, :, 0] = 1.0
        if q < 3:
            xhc[1] = xs[b, (q + 1) * 512]; hmk[:, :, 1] = 1.0
        cond = np.stack([f("c_ctx"), f("c")[0], f("c")[1]], axis=1)
        cT = np.ascontiguousarray(cond.reshape(8, 128, 3).transpose(1, 0, 2).reshape(128, 24))
        m_ada = dict(ada_w=np.ascontiguousarray(f("ada_w")[0][:, q * 1536:(q + 1) * 1536]),
                     adab=_fm(f("ada_b")[0][q * 1536:(q + 1) * 1536]),
                     selb=np.ascontiguousarray(np.broadcast_to(np.array([1.0 - b, float(b)], np.float32)[None, :], (128, 2))))
        s0T = np.stack([np.ascontiguousarray(
            f(nm)[b, 0].transpose(0, 2, 1).reshape(4, 2, 64, 64).transpose(1, 2, 0, 3).reshape(128, 4, 64))
            for nm in ("state_fwd", "state_bwd")], axis=0)
        m = dict(shared)
        m.update(m_ada)
        m["s0T"] = s0T
        sel = np.zeros((128, 8), np.float32)
        for j in range(4):
            sel[:, j] = 1.0 if j < q else 0.0
            sel[:, 4 + j] = 1.0 if j > q else 0.0
        m["sel"] = sel
        m["idh"] = np.ascontiguousarray(np.tile(np.eye(64, dtype=np.float32)[:, None, :], (2, 4, 1)).reshape(128, 256))
        m.update(xm=np.ascontiguousarray(xmc), xh=xhc, hmask=hmk.reshape(128, 16), condT=cT)
        in_maps.append(m)
    return in_maps


def kernel(**inputs):
    in_maps = prep_inputs(inputs)
    nc = build()
    res = run_bass_kernel_spmd(nc, in_maps, core_ids=list(range(NCORES)))
    y_prompt = np.zeros((16, 256, 1024), np.float32)
    y_sample = np.zeros((2, 2048, 1024), np.float32)
    nsf = np.zeros((16, 1, 8, 64, 64), np.float32)
    nsb = np.zeros((16, 1, 8, 64, 64), np.float32)
    for c, r in enumerate(res.results):
        b, q = c // 4, c % 4
        y = np.asarray(r["y"], np.float32)
        y_prompt[2 * c] = y[0:256]
        y_prompt[2 * c + 1] = y[256:512]
        y_sample[b, q * 512:(q + 1) * 512] = y[512:1024]
        stt_ = np.asarray(r["st"], np.float32).reshape(2, 2, 2, 64, 4, 64).transpose(0, 1, 4, 2, 5, 3).reshape(2, 2, 8, 64, 64)
        nsf[2 * c:2 * c + 2, 0] = stt_[:, 0]
        nsb[2 * c:2 * c + 2, 0] = stt_[:, 1]
    return (y_prompt, y_sample, nsf, nsb)
```
